# Optimizing a Trainium2 kernel written in Bass

```python
import jax, jax.numpy as jnp
from jax import lax
import numpy as np

D_MODEL = 4096
BATCH = 4
SEQ = 4096
DEPTH = 4

HEAD_DIM = 128
N_HEADS_SB = 8
DIL_CONFIGS = ((128, 1), (512, 4), (2048, 16))
N_HEADS_PER_DIL = 4
N_HEADS_DIL = N_HEADS_PER_DIL * len(DIL_CONFIGS)
N_KEYS_DIL = DIL_CONFIGS[0][0] // DIL_CONFIGS[0][1] + 1
W_SB = N_HEADS_SB * HEAD_DIM
W_DIL = N_HEADS_DIL * HEAD_DIM
W_DIL_OUT = N_HEADS_PER_DIL * HEAD_DIM
W_IN = 3 * W_SB + 3 * W_DIL + 2 * D_MODEL
D_FF = -(-8 * D_MODEL // (3 * 256)) * 256
D_PLE = 256
Q_BLOCK = 128
EPS = 1e-6

kernel_name = "hybrid_stickbreak_dilated_gated_block"


def rmsnorm(x, g):
    xf = x.astype(jnp.float32)
    y = xf * lax.rsqrt(jnp.mean(xf * xf, axis=-1, keepdims=True) + EPS)
    return (y * g.astype(jnp.float32)).astype(x.dtype)


def alibi_slopes(n):
    return jnp.exp2(-8.0 * jnp.arange(1, n + 1, dtype=jnp.float32) / n)


def stick_breaking_attention(q, k, v):
    b, s, h, dh = q.shape
    scale = dh ** -0.5
    kpos = jnp.arange(s)

    def block(i):
        t0 = i * Q_BLOCK
        qb = lax.dynamic_slice_in_dim(q, t0, Q_BLOCK, axis=1)
        z = jnp.einsum('bqhd,bshd->bhqs', qb, k, preferred_element_type=jnp.float32) * scale
        qpos = t0 + jnp.arange(Q_BLOCK)
        causal = kpos[None, :] < qpos[:, None]
        log_stay = jnp.where(causal, jax.nn.log_sigmoid(-z), 0.0)
        after = lax.cumsum(log_stay, axis=3, reverse=True) - log_stay
        w = jnp.where(causal, jnp.exp(jax.nn.log_sigmoid(z) + after), 0.0)
        return jnp.einsum('bhqs,bshd->bqhd', w.astype(v.dtype), v)

    out = lax.map(block, jnp.arange(s // Q_BLOCK))
    return jnp.moveaxis(out, 0, 1).reshape(b, s, h, dh)


def dilated_attention(q, k, v, slopes):
    b, s, _, dh = q.shape
    n_g = len(DIL_CONFIGS)
    hg = N_HEADS_PER_DIL
    q = q.reshape(b, s, n_g, hg, dh)
    k = k.reshape(b, s, n_g, hg, dh)
    v = v.reshape(b, s, n_g, hg, dh)
    scale = dh ** -0.5
    offs = jnp.arange(N_KEYS_DIL)

    def block(i):
        t0 = i * Q_BLOCK
        qpos = t0 + jnp.arange(Q_BLOCK)
        outs, lses = [], []
        for gi, (window, dil) in enumerate(DIL_CONFIGS):
            dist = dil * offs
            idx = qpos[:, None] - dist[None, :]
            valid = idx >= 0
            idx = jnp.maximum(idx, 0)
            qb = lax.dynamic_slice_in_dim(q[:, :, gi], t0, Q_BLOCK, axis=1)
            kg = jnp.take(k[:, :, gi], idx, axis=1)
            vg = jnp.take(v[:, :, gi], idx, axis=1)
            sc = jnp.einsum('bqhd,bqjhd->bhqj', qb, kg, preferred_element_type=jnp.float32) * scale
            sc = sc - slopes[:, None, None] * dist.astype(jnp.float32)[None, None, :]
            sc = jnp.where(valid, sc, -jnp.inf)
            m = jnp.max(sc, axis=-1, keepdims=True)
            e = jnp.exp(sc - m)
            den = jnp.sum(e, axis=-1, keepdims=True)
            outs.append(jnp.einsum('bhqj,bqjhd->bqhd', e / den, vg.astype(jnp.float32)))
            lses.append(m[..., 0] + jnp.log(den[..., 0]))
        alpha = jax.nn.softmax(jnp.stack(lses, 0), axis=0)
        alpha = jnp.transpose(alpha, (0, 1, 3, 2))[..., None]
        return jnp.sum(alpha * jnp.stack(outs, 0), axis=0).astype(v.dtype)

    out = lax.map(block, jnp.arange(s // Q_BLOCK))
    return jnp.moveaxis(out, 0, 1).reshape(b, s, hg * dh)


def setup_inputs(seed: int = 0) -> dict:
    key = jax.random.key(seed)
    ks = jax.random.split(key, 20)
    f32 = jnp.float32

    def w(k, shape, fan_in):
        return jax.random.normal(k, shape, f32) * (fan_in ** -0.5)

    def gain(k):
        return 1.0 + 0.02 * jax.random.normal(k, (DEPTH, D_MODEL), f32)

    return {
        "x": jax.random.normal(ks[0], (BATCH, SEQ, D_MODEL), f32),
        "p": jax.random.normal(ks[1], (DEPTH, BATCH, SEQ, D_PLE), f32),
        "w_in": w(ks[2], (DEPTH, D_MODEL, W_IN), D_MODEL),
        "w_proj_sb": w(ks[3], (DEPTH, W_SB, D_MODEL), W_SB),
        "w_proj_dil": w(ks[4], (DEPTH, W_DIL_OUT, D_MODEL), W_DIL_OUT),
        "w_out": w(ks[5], (DEPTH, D_MODEL, D_MODEL), D_MODEL),
        "g_mix_pre": gain(ks[6]),
        "g_mix_post": gain(ks[7]),
        "w_ffn_gate": w(ks[8], (DEPTH, D_MODEL, D_FF), D_MODEL),
        "w_ffn_up": w(ks[9], (DEPTH, D_MODEL, D_FF), D_MODEL),
        "w_ffn_down": w(ks[10], (DEPTH, D_FF, D_MODEL), D_FF),
        "g_ffn_pre": gain(ks[11]),
        "g_ffn_post": gain(ks[12]),
        "w_ple_in": w(ks[13], (DEPTH, D_PLE, D_MODEL), D_PLE),
        "w_ple_gate_down": w(ks[14], (DEPTH, D_MODEL, D_PLE), D_MODEL),
        "w_ple_gate_up": w(ks[15], (DEPTH, D_PLE, D_MODEL), D_PLE),
        "g_ple_gate": gain(ks[16]),
        "g_ple_post": gain(ks[17]),
    }


def reference(x, p, w_in, w_proj_sb, w_proj_dil, w_out, g_mix_pre, g_mix_post,
              w_ffn_gate, w_ffn_up, w_ffn_down, g_ffn_pre, g_ffn_post,
              w_ple_in, w_ple_gate_down, w_ple_gate_up, g_ple_gate, g_ple_post):
    b, s, _ = x.shape
    slopes = alibi_slopes(N_HEADS_PER_DIL)
    splits = [W_SB, 2 * W_SB, 3 * W_SB, 3 * W_SB + W_DIL, 3 * W_SB + 2 * W_DIL,
              3 * W_SB + 3 * W_DIL, 3 * W_SB + 3 * W_DIL + D_MODEL]
    h = x
    for i in range(DEPTH):
        xn = rmsnorm(h, g_mix_pre[i])
        proj = xn @ w_in[i]
        q_sb, k_sb, v_sb, q_d, k_d, v_d, gate_sb, gate_d = jnp.split(proj, splits, axis=-1)
        hs = lambda t, n: t.reshape(b, s, n, HEAD_DIM)
        o_sb = stick_breaking_attention(hs(q_sb, N_HEADS_SB), hs(k_sb, N_HEADS_SB),
                                        hs(v_sb, N_HEADS_SB)).reshape(b, s, W_SB)
        o_d = dilated_attention(hs(q_d, N_HEADS_DIL), hs(k_d, N_HEADS_DIL),
                                hs(v_d, N_HEADS_DIL), slopes)
        merged = (jax.nn.sigmoid(gate_sb) * (o_sb @ w_proj_sb[i])
                  + jax.nn.sigmoid(gate_d) * (o_d @ w_proj_dil[i]))
        h = h + rmsnorm(merged @ w_out[i], g_mix_post[i])
        xn = rmsnorm(h, g_ffn_pre[i])
        f = (jax.nn.silu(xn @ w_ffn_gate[i]) * (xn @ w_ffn_up[i])) @ w_ffn_down[i]
        h = h + rmsnorm(f, g_ffn_post[i])
        gate = jax.nn.sigmoid((rmsnorm(h, g_ple_gate[i]) @ w_ple_gate_down[i]) @ w_ple_gate_up[i])
        e = (p[i] @ w_ple_in[i]) * gate
        h = h + rmsnorm(e, g_ple_post[i])
    return h
```

```python
import contextlib
import numpy as np
import concourse.bass as bass
import concourse.mybir as mybir
from concourse.bass_utils import run_bass_kernel_spmd

F32 = mybir.dt.float32
BF16 = mybir.dt.bfloat16
AF = mybir.ActivationFunctionType
ALU = mybir.AluOpType

HEAD = 128
NH_SB = 8
NG = 3
DILS = (1, 4, 16)
NH_G = 4
W_SB = NH_SB * HEAD
W_DIL = NG * NH_G * HEAD
D_PLE = 256
EPS = 1e-6
NEG = -30000.0
CH = 262144
FLATW = 2048


class Cfg:
    def __init__(self, D=4096, S=4096, DFF=11008, DEPTH=4, TT=512):
        self.D, self.S, self.DFF, self.DEPTH, self.TT = D, S, DFF, DEPTH, TT
        self.T = S // 2
        self.NT = self.T // TT
        self.KC = D // 128
        self.FC = DFF // 128
        self.NQK = (2 * W_SB + 2 * W_DIL) // 128
        segs = [
            ("in_qk", D, 2 * W_SB + 2 * W_DIL, 256),
            ("in_gate", D, 2 * D, 256),
            ("in_v", D, W_SB + W_DIL, 256),
            ("p_sb", W_SB, D, 256),
            ("p_dil", NH_G * HEAD, D, 256),
            ("w_out", D, D, 256),
            ("ffn_gu", D, 2 * DFF, 256),
            ("ffn_d0", DFF // 2, D, 128),
            ("ffn_d1", DFF // 2, D, 128),
            ("ple_gd", D, D_PLE, 256),
            ("ple_gu", D_PLE, D, 512),
            ("ple_in", D_PLE, D, 512),
        ]
        self.seg = {}
        groups = [["in_qk", "in_gate", "in_v", "p_sb", "p_dil", "w_out"], ["ffn_gu"],
                  ["ffn_d0", "ffn_d1", "ple_gd", "ple_gu", "ple_in"]]
        sd = {n: (K, N, NB) for n, K, N, NB in segs}
        per = 8 * CH
        self.gflat, self.gncoll, self.gsegs = [], [], groups
        for gi, names in enumerate(groups):
            off = 0
            for name in names:
                K, N, NB = sd[name]
                be = (K // 128) * NB
                self.seg[name] = dict(off=off, K=K, N=N, NB=NB, be=be, nblk=N // NB, kc=K // 128, grp=gi)
                off += K * N
            fl = ((off + per - 1) // per) * per
            self.gflat.append(fl)
            self.gncoll.append(fl // per)
        self.NGRP = len(groups)


def _pack(W, NB):
    K, N = W.shape
    return np.ascontiguousarray(W.reshape(K // 128, 128, N // NB, NB).transpose(2, 1, 0, 3)).reshape(-1)


def pack_layer(cfg, w_in, w_proj_sb, w_proj_dil, w_out, w_g, w_u, w_d, w_ple_in, w_gd, w_gu):
    D = cfg.D
    o = 0
    q_sb = w_in[:, o:o + W_SB]; o += W_SB
    k_sb = w_in[:, o:o + W_SB]; o += W_SB
    v_sb = w_in[:, o:o + W_SB]; o += W_SB
    q_d = w_in[:, o:o + W_DIL]; o += W_DIL
    k_d = w_in[:, o:o + W_DIL]; o += W_DIL
    v_d = w_in[:, o:o + W_DIL]; o += W_DIL
    g_sb = w_in[:, o:o + D]; o += D
    g_d = w_in[:, o:o + D]; o += D
    FC = cfg.FC
    gu = np.concatenate([w_g.reshape(D, FC, 1, 128), w_u.reshape(D, FC, 1, 128)], axis=2).reshape(D, 2 * cfg.DFF)
    hf = cfg.DFF // 2
    parts = [
        _pack(np.concatenate([q_sb, k_sb, q_d, k_d], axis=1), 256),
        _pack(np.concatenate([g_sb, g_d], axis=1), 256),
        _pack(np.concatenate([v_sb, v_d], axis=1), 256),
        _pack(w_proj_sb, 256),
        _pack(w_proj_dil, 256),
        _pack(w_out, 256),
        _pack(gu, 256),
        _pack(w_d[:hf], 128),
        _pack(w_d[hf:], 128),
        _pack(w_gd, 256),
        _pack(w_gu, 512),
        _pack(w_ple_in, 512),
    ]
    names = ["in_qk", "in_gate", "in_v", "p_sb", "p_dil", "w_out", "ffn_gu", "ffn_d0", "ffn_d1", "ple_gd", "ple_gu", "ple_in"]
    flats = [np.zeros(fl, np.float32) for fl in cfg.gflat]
    for name, p in zip(names, parts):
        sg = cfg.seg[name]
        assert p.size == sg["K"] * sg["N"]
        flats[sg["grp"]][sg["off"]:sg["off"] + p.size] = p
    return flats


CB_ONES, CB_TRI, CB_LI, CB_M01, CB_MNEG = 0, 128, 256, 384, 384 + 2048
CB_SEL = 384 + 4096
NCB = 384 + 4096 + 256
CF_DIL = 0
CF_ONE = NG * 2 * 2 * 128
CF_EPS = CF_ONE + 1
NCF = CF_EPS + 1


def make_consts(rank):
    p = np.arange(128)[:, None].astype(np.float64)
    tri = (p > np.arange(128)[None, :]).astype(np.float64)
    cb = [np.ones((128, 128)), tri, 1.0 - tri]
    c512 = np.arange(512)[None, :]
    for jj in range(4):
        cb.append((c512 > 128 * jj + p).astype(np.float64))
    for jj in range(4):
        cb.append(np.where(c512 > 128 * jj + p, 0.0, NEG))
    eye = np.eye(128)
    cb.append(eye if rank == 0 else np.zeros((128, 128)))
    cb.append(eye if rank == 1 else np.zeros((128, 128)))
    slopes = np.exp2(-8.0 * np.arange(1, NH_G + 1) / NH_G)
    pq = np.arange(128)[None, :]
    cf = []
    for g in range(NG):
        r = DILS[g]
        for lh in range(2):
            sl = slopes[2 * rank + lh]
            cf.append(np.where(pq >= p, -sl * r * (pq - p), NEG))
            cf.append(np.where(p >= pq, -sl * r * (128 + pq - p), NEG))
    cf.append(np.ones((128, 1)))
    cf.append(np.full((128, 1), EPS))
    return np.concatenate(cb + cf, axis=1).astype(np.float32)


class Buf:
    __slots__ = ("name", "w", "rs", "dsem", "dcnt")

    def __init__(self, name):
        self.name = name
        self.w = None
        self.rs = {}
        self.dsem = None
        self.dcnt = 0


class Prog:
    ENGS = ("pe", "act", "dve", "pool", "sp")

    def __init__(self, nc, stack):
        self.nc = nc
        self.stack = stack
        self.ops = {e: [] for e in self.ENGS}
        self.esem = {e: self.sem("prog_" + e) for e in self.ENGS}
        self.ecnt = {e: 0 for e in self.ENGS}
        self.seen = {e: {} for e in self.ENGS}
        self.csem = self.sem("coll")
        self.ccnt = 0
        self.dirty = {}
        self.semcnt = {}

    def sem(self, name):
        return self.stack.enter_context(self.nc.semaphore(name))

    def buf(self, name, dma=False, sem=None):
        b = Buf(name)
        if sem is not None:
            b.dsem = sem
        elif dma:
            b.dsem = self.sem("d_" + name)
        return b

    def _waits(self, eng, reads, writes):
        need = {}

        def add(tok):
            if tok is None:
                return
            s, v = tok
            k = id(s)
            if k not in need or need[k][1] < v:
                need[k] = (s, v)
        for b in reads:
            add(b.w)
        for b in writes:
            add(b.w)
            for r in b.rs.values():
                add(r)
        out = []
        seen = self.seen[eng]
        for k, (s, v) in need.items():
            if eng == "pe" and s is self.esem["pe"]:
                continue
            if k in self.semcnt:
                v = 16 * self.semcnt[k]
            if seen.get(k, 0) >= v:
                continue
            seen[k] = v
            out.append((s, v))
        return out

    def _commit(self, tok, reads, writes):
        k = id(tok[0])
        for b in writes:
            b.w = tok
            b.rs = {}
        for b in reads:
            b.rs[k] = tok

    def op(self, eng, fn, reads=(), writes=()):
        waits = self._waits(eng, reads, writes)
        self.ecnt[eng] += 1
        tok = (self.esem[eng], self.ecnt[eng])
        self.ops[eng].append((fn, waits, (self.esem[eng], 1)))
        self._commit(tok, reads, writes)

    def dma(self, eng, fn, dst, reads=(), extra_writes=()):
        writes = (dst,) + tuple(extra_writes)
        waits = self._waits(eng, reads, writes)
        c = self.semcnt.get(id(dst.dsem), 0) + 1
        self.semcnt[id(dst.dsem)] = c
        tok = (dst.dsem, 16 * c)
        self.ops[eng].append((fn, waits, (dst.dsem, 16)))
        self._commit(tok, reads, writes)
        self.dirty[id(dst)] = (dst, tok)

    def coll(self, fn, reads=(), writes=()):
        waits = self._waits("pool", reads, writes)
        self.ccnt += 1
        tok = (self.csem, self.ccnt)
        self.ops["pool"].append((fn, waits, (self.csem, None)))
        self._commit(tok, reads, writes)

    def barrier(self, engines=("pe", "act", "dve", "sp")):
        toks = [(self.esem[e], self.ecnt[e]) for e in engines if self.ecnt[e] > 0]
        toks += [t for (b, t) in self.dirty.values() if not b.name.startswith("wshb")]
        for e in engines:
            seen = self.seen[e]
            ws = []
            for s, v in toks:
                if seen.get(id(s), 0) >= v:
                    continue
                seen[id(s)] = v
                ws.append((s, v))
            if ws:
                self.ops[e].append((None, ws, None))
        self.dirty = {k: bt for k, bt in self.dirty.items() if bt[0].name.startswith("wshb")}

    def replay(self, eng, e):
        import os
        probe = os.environ.get("KDEBUG3") and eng == "sp"
        pstate = True
        for opi, (fn, waits, inc) in enumerate(self.ops[eng]):
            if probe and opi < 193 and opi % 8 == 0:
                try:
                    self.ops[eng][193][0](e); ok = True
                except Exception as ex:
                    ok = False
                if ok != pstate:
                    print("PROBE change at", opi, ok); pstate = ok
            for s, v in waits:
                e.wait_ge(s, v)
            if fn is None:
                continue
            try:
                ins = fn(e)
            except Exception as ex:
                import os
                if os.environ.get("KDEBUG"):
                    print("REPLAY FAIL", eng, len(self.ops[eng]), self.ops[eng].index((fn, waits, inc)), ex)
                    for t in range(2):
                        try:
                            ins = fn(e); print("retry ok"); break
                        except Exception as ex2:
                            print("retry fail", ex2)
                raise
            if inc is not None:
                if inc[1] is None:
                    ins.then_inc(inc[0])
                else:
                    ins.then_inc(inc[0], inc[1])


def build(cfg):
    nc = bass.Bass("TRN2", target_bir_lowering=False)
    D, S, T, TT, NT, KC, FC, DEPTH = cfg.D, cfg.S, cfg.T, cfg.TT, cfg.NT, cfg.KC, cfg.FC, cfg.DEPTH
    SCALE = float(HEAD) ** -0.5
    NSUB = TT // 128
    assert TT == 512
    stack = contextlib.ExitStack()
    with stack:
        P = Prog(nc, stack)

        xT = nc.dram_tensor("xT", [D, T], F32, kind="ExternalInput")
        pT = nc.dram_tensor("pT", [DEPTH * D_PLE, T], F32, kind="ExternalInput")
        gains = nc.dram_tensor("gains", [128, 6 * DEPTH * KC], F32, kind="ExternalInput")
        consts = nc.dram_tensor("consts", [128, NCB + NCF], F32, kind="ExternalInput")
        RPC = CH // FLATW
        NGRP = cfg.NGRP
        wsh = [[nc.dram_tensor(f"wsh{L}_{g}", [cfg.gncoll[g] * RPC, FLATW], F32, kind="ExternalInput") for g in range(NGRP)] for L in range(DEPTH)]
        outT = nc.dram_tensor("outT", [D, T], F32, kind="ExternalOutput")

        wshb = [[nc.dram_tensor(f"wshb{L}_{g}", [cfg.gncoll[g] * RPC, FLATW], BF16) for g in range(NGRP)] for L in range(DEPTH)]
        ws1 = [[nc.dram_tensor(f"ws1_{L}_{g}", [cfg.gncoll[g] * 4 * RPC, FLATW], BF16) for g in range(NGRP)] for L in range(DEPTH)]
        wfull = [[nc.dram_tensor(f"wfull{L}_{g}", [cfg.gflat[g] // FLATW, FLATW], BF16) for g in range(NGRP)] for L in range(DEPTH)]
        HT = nc.dram_tensor("HT", [D, T], F32)
        Y = nc.dram_tensor("Y", [D, T], F32)
        PTB = nc.dram_tensor("PTB", [DEPTH * D_PLE, T], BF16)
        NQC = cfg.NQK // 4
        XQL = nc.dram_tensor("XQL", [NQC * 512, T], BF16)
        XQG = nc.dram_tensor("XQG", [NQC * 2 * 512, T], BF16)
        HV = T // 2
        XVS = nc.dram_tensor("XVS", [T, W_SB], BF16)
        XVSG = nc.dram_tensor("XVSG", [4 * HV, W_SB], BF16)
        XVD = [nc.dram_tensor(f"XVD{g}", [T, NH_G * HEAD], BF16) for g in range(NG)]
        XVDG = [nc.dram_tensor(f"XVDG{g}", [2 * T, NH_G * HEAD], BF16) for g in range(NG)]
        GATE = nc.dram_tensor("GATE", [2 * D, T], BF16)
        OXL = nc.dram_tensor("OXL", [768, S], BF16)
        OXG = nc.dram_tensor("OXG", [6 * 2 * 128, S], BF16)

        b_X = P.buf("xT")
        b_HT = [P.buf(f"HT{i}", dma=True) for i in range(NT)]
        b_Y = [P.buf(f"Y{i}", dma=True) for i in range(NT)]
        b_XQL = P.buf("XQL", dma=True)
        b_XV = P.buf("XV", dma=True)
        b_GATE = [P.buf(f"GATE{i}", dma=True) for i in range(NT)]
        b_OXL = P.buf("OXL", dma=True)
        b_XQGc = [P.buf(f"XQG{c}") for c in range(cfg.NQK // 4)]
        b_XVGs = P.buf("XVGs")
        b_XVGd = [P.buf(f"XVGd{g}") for g in range(NG)]
        b_OXG = P.buf("OXG")
        b_PTB = P.buf("PTB", dma=True)
        b_OUT = P.buf("OUT", dma=True)
        wcast_sems = [P.sem(f"wcast{k}") for k in range(4)]
        b_wshb = [[[P.buf(f"wshb{L}_{g}_{i}", sem=wcast_sems[i % 4]) for i in range(cfg.gncoll[g])] for g in range(NGRP)] for L in range(DEPTH)]
        b_wfull = [[[P.buf(f"wf{L}_{g}_{i}") for i in range(cfg.gncoll[g])] for g in range(NGRP)] for L in range(DEPTH)]

        def sbt(name, shape, dt):
            return stack.enter_context(nc.sbuf_tensor(name, shape, dt))
        CB = sbt("CB", [128, NCB], BF16)
        CF = sbt("CF", [128, NCF], F32)
        GN = sbt("GN", [128, 6 * DEPTH * KC], F32)
        XNE = max(KC * TT, 16384)
        XN = sbt("XN", [128, XNE], BF16)
        ATE = max(FC * TT, 7 * S, (KC + 24) * TT)
        AT = sbt("AT", [128, ATE], BF16)
        WB = sbt("WB", [128, 2 * 8192 + 2 * 2048], BF16)
        FW = sbt("FW", [128, 8 * 512], F32)
        ST = sbt("ST", [128, 4 * 512], BF16)
        OST = sbt("OST", [128, 2 * 512], BF16)
        SQ = sbt("SQ", [128, 2 * TT], BF16)
        RS = sbt("RS", [128, 2 * TT], F32)
        PS = stack.enter_context(nc.psum_tensor("PS", [128, 8 * 512], F32))
        psb = [P.buf(f"ps{i}") for i in range(8)]

        def ps(i, n=512, o=0):
            return PS[:, i * 512 + o:i * 512 + o + n]

        b_CB, b_CF, b_GN = P.buf("CB", dma=True), P.buf("CF", dma=True), P.buf("GN", dma=True)
        NCHK = ATE // TT
        xn_sems = [P.sem(f"d_xn{k}") for k in range(4)]
        at_sems = [P.sem(f"d_at{k}") for k in range(4)]
        b_XNc = [P.buf(f"XN{k}", sem=xn_sems[k % 4]) for k in range(XNE // TT)]
        b_ATc = [P.buf(f"AT{k}", sem=at_sems[k % 4]) for k in range(NCHK)]
        b_RS = [P.buf("RS0"), P.buf("RS1")]
        b_SQ = [P.buf("sq0"), P.buf("sq1")]
        b_OST = [P.buf("ost0"), P.buf("ost1")]
        fwb = [P.buf(f"fw{k}", dma=True) for k in range(8)]
        b_ST = [P.buf(f"st{k}", dma=True) for k in range(4)]

        def fw(k, n=TT):
            return FW[:, k * 512:k * 512 + n]

        def st(k, n=TT):
            return ST[:, k * 512:k * 512 + n]

        def ost(k, n=TT):
            return OST[:, k * 512:k * 512 + n]

        def rs(i):
            return RS[:, i * TT:(i + 1) * TT]

        def xn(kc):
            return XN[:, kc * TT:(kc + 1) * TT]

        def at(kc):
            return AT[:, kc * TT:(kc + 1) * TT]

        ones_bf = CB[:, CB_ONES:CB_ONES + 128]
        tri_bf = CB[:, CB_TRI:CB_TRI + 128]
        li_bf = CB[:, CB_LI:CB_LI + 128]
        sel_bf = [CB[:, CB_SEL:CB_SEL + 128], CB[:, CB_SEL + 128:CB_SEL + 256]]
        one_col = CF[:, CF_ONE:CF_ONE + 1]
        eps_col = CF[:, CF_EPS:CF_EPS + 1]
        rank_holder = {}

        def gain(kind, L, kc):
            o = (kind * DEPTH + L) * KC + kc
            return GN[:, o:o + 1]

        P.dma("pool", lambda e: e.dma_start(out=CB[:, :], in_=consts[:, 0:NCB]), b_CB)
        P.dma("sp", lambda e: e.dma_start(out=CF[:, :], in_=consts[:, NCB:NCB + NCF]), b_CF)
        P.dma("sp", lambda e: e.dma_start(out=GN[:, :], in_=gains[:, :]), b_GN)
        for L_ in range(DEPTH):
            P.dma("pool", lambda e, L_=L_: e.dma_start(out=PTB[L_ * D_PLE:(L_ + 1) * D_PLE, :], in_=pT[L_ * D_PLE:(L_ + 1) * D_PLE, :]), b_PTB)

        QUADS = [[0, 1, 2, 3], [4, 5, 6, 7]]
        XPAIRS = [[0, 4], [1, 5], [2, 6], [3, 7]]
        PAIRS = [[0, 1], [2, 3], [4, 5], [6, 7]]

        def wcast(L, g, i):
            r0 = i * RPC
            src_ = wsh[L][g][r0:r0 + RPC, :]
            dstb = wshb[L][g][r0:r0 + RPC, :]
            P.dma("pool", lambda e: e.dma_start(out=dstb, in_=src_), b_wshb[L][g][i])

        def wgather(L, g, i):
            r0 = i * RPC
            dstb = wshb[L][g][r0:r0 + RPC, :]
            bsh = b_wshb[L][g][i]
            s1 = ws1[L][g][i * 4 * RPC:(i + 1) * 4 * RPC, :]
            b1 = P.buf("s1")
            P.coll(lambda e: e.collective_compute("AllGather", ALU.bypass, replica_groups=QUADS,
                                                  ins=[dstb], outs=[s1]), reads=[bsh], writes=[b1])
            wf = wfull[L][g][i * 8 * RPC:(i + 1) * 8 * RPC, :]
            P.coll(lambda e: e.collective_compute("AllGather", ALU.bypass, replica_groups=XPAIRS,
                                                  ins=[s1], outs=[wf]), reads=[b1], writes=[b_wfull[L][g][i]])

        wq = [(L, g, i) for L in range(DEPTH) for g in range(NGRP) for i in range(cfg.gncoll[g])]
        ncoll_layer = sum(cfg.gncoll)
        wq_pos = [0]
        wc_pos = [0]
        CAST_AHEAD = 3

        def _step():
            while wc_pos[0] < min(len(wq), wq_pos[0] + 1 + CAST_AHEAD):
                wcast(*wq[wc_pos[0]])
                wc_pos[0] += 1
            wgather(*wq[wq_pos[0]])
            wq_pos[0] += 1

        def pump_weights(n):
            for _ in range(n):
                if wq_pos[0] < len(wq):
                    _step()

        def pump_layer(L):
            while wq_pos[0] < len(wq) and wq[wq_pos[0]][0] <= L:
                _step()

        def wblock(L, name, b):
            sg = cfg.seg[name]
            be = sg["be"]
            off = sg["off"] + b * 128 * be
            g = sg["grp"]
            v = wfull[L][g].ap().rearrange("r (a c) -> (r a) c", c=128)
            r0 = off // 128
            ap = v[r0:r0 + be, :].rearrange("(p a) c -> p (a c)", p=128)
            i0 = off // (8 * CH)
            i1 = (off + 128 * be - 1) // (8 * CH)
            return ap, [b_wfull[L][g][i] for i in range(i0, i1 + 1)], be

        class Slots:
            def __init__(self, n, size, base, name):
                self.n, self.size, self.base = n, size, base
                self.bufs = [P.buf(f"{name}{k}", dma=True) for k in range(n)]
                self.k = 0

            def load(self, L, seg, b):
                ap, deps, be = wblock(L, seg, b)
                assert be <= self.size
                k = self.k
                self.k = (k + 1) % self.n
                o = self.base + k * self.size
                dst = WB[:, o:o + be]
                P.dma("sp", lambda e: e.dma_start(out=dst, in_=ap), self.bufs[k], reads=deps)
                return dst, self.bufs[k]

        slots_b = Slots(2, 8192, 0, "wbig")
        slots_s = Slots(2, 2048, 16384, "wsml")
        bank_toggle = [0]

        def next_bank():
            b = bank_toggle[0] % 2
            bank_toggle[0] += 1
            return b

        def sumsq_accum(src_ap, src_bufs, sqk, first, last):
            sq = SQ[:, sqk * TT:(sqk + 1) * TT]
            P.op("act", lambda e: e.activation(out=sq, in_=src_ap, func=AF.Square), reads=src_bufs, writes=[b_SQ[sqk]])
            P.op("pe", lambda e: e.matmul(ps(7), lhsT=ones_bf, rhs=sq, start=first, stop=last),
                 reads=[b_SQ[sqk], b_CB], writes=[psb[7]])

        def rstd_from_sumsq(ri):
            P.op("act", lambda e: e.activation(out=rs(ri), in_=ps(7), func=AF.Sqrt, bias=eps_col, scale=1.0 / D),
                 reads=[psb[7], b_CF], writes=[b_RS[ri]])
            P.op("dve", lambda e: e.reciprocal(out=rs(ri), in_=rs(ri)), reads=[b_RS[ri]], writes=[b_RS[ri]])

        def hview(t, i):
            return t.ap().rearrange("(kc p) t -> p kc t", p=128)[:, :, i * TT:(i + 1) * TT]

        def norm_pass(L, i, h_src, hbuf_src, resid, g_post, pre, g_pre, h_dst=None, b_dst=None):
            hs = hview(h_src, i)
            yv = hview(Y, i)
            hd = hview(h_dst if h_dst is not None else HT, i)
            bd = b_dst if b_dst is not None else b_HT[i]
            HB = AT[:, 0:KC * 2 * TT].bitcast(F32)

            def hbk(kc):
                return HB[:, kc * TT:(kc + 1) * TT]

            def hbb(kc):
                return [b_ATc[2 * kc], b_ATc[2 * kc + 1]]
            G = min(4, KC)
            for g0 in range(0, KC, G):
                bl = [bb for kc in range(g0, g0 + G) for bb in hbb(kc)]
                P.dma("sp", lambda e, g0=g0: e.dma_start(out=HB[:, g0 * TT:(g0 + G) * TT].rearrange("p (k t) -> p k t", k=G), in_=hs[:, g0:g0 + G, :]),
                      bl[0], reads=[hbuf_src], extra_writes=bl[1:])
            if resid:
                YG = min(4, KC)
                for yg in range(KC // YG):
                    s = yg % 2
                    P.dma("sp", lambda e, yg=yg, s=s: e.dma_start(
                        out=FW[:, 4 * s * 512:(4 * s + YG) * 512].rearrange("p (k t) -> p k t", k=YG), in_=yv[:, yg * YG:(yg + 1) * YG, :]),
                        fwb[4 * s], reads=[b_Y[i]], extra_writes=fwb[4 * s + 1:4 * s + YG])
                    for j in range(YG):
                        kc = yg * YG + j
                        yt, yb = fw(4 * s + j), fwb[4 * s + j]
                        gp = gain(g_post, L, kc)
                        P.op("dve", lambda e, yt=yt, gp=gp: e.scalar_tensor_tensor(
                            out=yt, in0=yt, scalar=gp, in1=rs(0), op0=ALU.mult, op1=ALU.mult),
                            reads=[yb, b_RS[0], b_GN], writes=[yb])
                        P.op("dve", lambda e, yt=yt, kc=kc: e.tensor_tensor(out=hbk(kc), in0=yt, in1=hbk(kc), op=ALU.add),
                             reads=[yb] + hbb(kc), writes=hbb(kc))
                        if pre:
                            sumsq_accum(hbk(kc), hbb(kc), kc % 2, kc == 0, kc == KC - 1)
                        if (kc + 1) % G == 0:
                            g0 = kc + 1 - G
                            bl = [bb for k2 in range(g0, g0 + G) for bb in hbb(k2)]
                            P.dma("act", lambda e, g0=g0: e.dma_start(out=hd[:, g0:g0 + G, :], in_=HB[:, g0 * TT:(g0 + G) * TT].rearrange("p (k t) -> p k t", k=G)),
                                  bd, reads=bl)
            elif pre:
                for kc in range(KC):
                    sumsq_accum(hbk(kc), hbb(kc), kc % 2, kc == 0, kc == KC - 1)
            if pre:
                rstd_from_sumsq(1)
                for kc in range(KC):
                    g2 = gain(g_pre, L, kc)
                    P.op("dve", lambda e, kc=kc, g2=g2: e.scalar_tensor_tensor(
                        out=xn(kc), in0=hbk(kc), scalar=g2, in1=rs(1), op0=ALU.mult, op1=ALU.mult),
                        reads=hbb(kc) + [b_RS[1], b_GN], writes=[b_XNc[kc]])

        def dense_fm(L, segs, rhs_of_kc, rhs_buf_of_kc, epilogue, sub_cols=128):
            sg0 = cfg.seg[segs[0]]
            NB = sg0["NB"]
            for b in range(sg0["nblk"]):
                wts = [slots_b.load(L, s, b) for s in segs[:1]]
                for sub in range(NB // sub_cols):
                    bank = next_bank()
                    kbase = 0
                    for si, s in enumerate(segs):
                        if si > 0 and sub == 0:
                            wts.append(slots_b.load(L, s, b))
                        wt, wbuf = wts[si]
                        kcn = cfg.seg[s]["kc"]
                        for kc in range(kcn):
                            lhsT = wt[:, kc * NB + sub * sub_cols: kc * NB + (sub + 1) * sub_cols]
                            first = (si == 0 and kc == 0)
                            last = (si == len(segs) - 1 and kc == kcn - 1)
                            P.op("pe", lambda e, lhsT=lhsT, kk=kbase + kc, bank=bank, first=first, last=last: e.matmul(
                                ps(bank), lhsT=lhsT, rhs=rhs_of_kc(kk), start=first, stop=last),
                                reads=[wbuf, rhs_buf_of_kc(kbase + kc)], writes=[psb[bank]])
                        kbase += kcn
                    epilogue(b * (NB // sub_cols) + sub, bank)

        st_k = [0]

        def next_st():
            k = st_k[0] % 4
            st_k[0] += 1
            return k


        def select_pieces(npieces, a_piece, b_piece, a_bufs, b_bufs, dst_piece, dst_bufs, banks, width=512):
            for pc in range(npieces):
                bank = banks[pc % len(banks)]
                P.op("pe", lambda e, pc=pc, bank=bank: e.matmul(ps(bank, width), lhsT=sel_bf[0], rhs=a_piece(pc), start=True, stop=False),
                     reads=[b_CB] + a_bufs(pc), writes=[psb[bank]])
                P.op("pe", lambda e, pc=pc, bank=bank: e.matmul(ps(bank, width), lhsT=sel_bf[1], rhs=b_piece(pc), start=False, stop=True),
                     reads=[b_CB] + b_bufs(pc), writes=[psb[bank]])
                if pc % 2 == 0:
                    P.op("act", lambda e, pc=pc, bank=bank: e.copy(out=dst_piece(pc), in_=ps(bank, width)), reads=[psb[bank]], writes=dst_bufs(pc))
                else:
                    P.op("dve", lambda e, pc=pc, bank=bank: e.tensor_copy(out=dst_piece(pc), in_=ps(bank, width)), reads=[psb[bank]], writes=dst_bufs(pc))

        if getattr(cfg, "debug", False):
            dbgT = nc.dram_tensor("dbg_T", [128, 5 * 512], F32, kind="ExternalOutput")
            dbgB = nc.dram_tensor("dbg_B", [128, 5 * 512], BF16, kind="ExternalOutput")
            dbgH1 = nc.dram_tensor("dbg_H1", [D, T], F32, kind="ExternalOutput")
            dbgH2 = nc.dram_tensor("dbg_H2", [D, T], F32, kind="ExternalOutput")
            dbgY1 = nc.dram_tensor("dbg_Y1", [D, T], F32, kind="ExternalOutput")
            dbgY2 = nc.dram_tensor("dbg_Y2", [D, T], F32, kind="ExternalOutput")
            b_dbgT = P.buf("dbgT", dma=True)

        def o_coll(c):
            P.coll(lambda e, c=c: e.collective_compute("AllGather", ALU.bypass, replica_groups=PAIRS,
                                                       ins=[OXL[c * 128:(c + 1) * 128, :]], outs=[OXG[c * 256:(c + 1) * 256, :]]),
                   reads=[b_OXL], writes=[b_OXG])

        def attention():
            nhb = T // 128
            QT, KT, VS = AT[:, 0:S], AT[:, S:2 * S], AT[:, 2 * S:3 * S]
            VSv = VS.rearrange("p (b d) -> p b d", d=128)
            bq = b_ATc[0:S // TT]
            bk = b_ATc[S // TT:2 * S // TT]
            bv = b_ATc[2 * S // TT:3 * S // TT]
            xqg3 = XQG.ap().rearrange("(a b) t -> a b t", b=1024)
            xqg4 = XQG.ap().rearrange("(a b c) t -> a b c t", b=4, c=256)
            xvs_v = XVSG.ap().rearrange("(q b p) (r c) -> q p b r c", p=128, b=HV // 128, c=512)
            def sb_tile(lh, tq):
                J = 4 * tq + 4
                qcols = QT[:, tq * 512:(tq + 1) * 512]

                def stage_a(n):
                    j = J - 1 - n
                    jj = j - 4 * tq
                    z = n % 2
                    P.op("pe", lambda e: e.matmul(ps(z), lhsT=KT[:, j * 128:(j + 1) * 128], rhs=qcols, start=True, stop=True),
                         reads=[bk[0], bq[0]], writes=[psb[z]])
                    P.op("act", lambda e: e.activation(out=fw(z), in_=ps(z), func=AF.Exp, scale=SCALE), reads=[psb[z]], writes=[fwb[z]])
                    P.op("act", lambda e: e.activation(out=fw(2 + z), in_=fw(z), func=AF.Ln, bias=one_col, scale=1.0),
                         reads=[fwb[z], b_CF], writes=[fwb[2 + z]])

                def stage_a2(n):
                    j = J - 1 - n
                    jj = j - 4 * tq
                    z = n % 2
                    if jj >= 0:
                        m01 = CB[:, CB_M01 + jj * 512:CB_M01 + (jj + 1) * 512]
                        P.op("dve", lambda e: e.scalar_tensor_tensor(out=st(z), in0=fw(2 + z), scalar=-1.0, in1=m01, op0=ALU.mult, op1=ALU.mult),
                             reads=[fwb[2 + z], b_CB], writes=[b_ST[z]])
                    else:
                        P.op("dve", lambda e: e.tensor_scalar(out=st(z), in0=fw(2 + z), scalar1=-1.0, scalar2=None, op0=ALU.mult),
                             reads=[fwb[2 + z]], writes=[b_ST[z]])
                    P.op("dve", lambda e: e.scalar_tensor_tensor(out=fw(4 + z), in0=ps(z), scalar=SCALE, in1=fw(2 + z), op0=ALU.mult, op1=ALU.subtract),
                         reads=[psb[z], fwb[2 + z]], writes=[fwb[4 + z]])

                def o_mm(n):
                    j = J - 1 - n
                    z = n % 2
                    P.op("pe", lambda e: e.matmul(ps(3), lhsT=VSv[:, j, :], rhs=st(2 + z), start=(n == 0), stop=(n == J - 1)),
                         reads=[bv[0], b_ST[2 + z]], writes=[psb[3]])

                def stage_b(n):
                    j = J - 1 - n
                    jj = j - 4 * tq
                    z = n % 2
                    P.op("pe", lambda e: e.matmul(ps(2), lhsT=tri_bf, rhs=st(z), start=(n == 0), stop=(n == J - 1)),
                         reads=[b_ST[z], b_CB], writes=[psb[2]])
                    if n > 0:
                        o_mm(n - 1)
                    P.op("dve", lambda e: e.tensor_tensor(out=fw(6 + z), in0=fw(4 + z), in1=ps(2), op=ALU.add),
                         reads=[fwb[4 + z], psb[2]], writes=[fwb[6 + z]])
                    if jj >= 0:
                        mneg = CB[:, CB_MNEG + jj * 512:CB_MNEG + (jj + 1) * 512]
                        P.op("dve", lambda e: e.tensor_tensor(out=fw(6 + z), in0=fw(6 + z), in1=mneg, op=ALU.add),
                             reads=[fwb[6 + z], b_CB], writes=[fwb[6 + z]])

                def stage_b2(n):
                    z = n % 2
                    P.op("act", lambda e: e.activation(out=st(2 + z), in_=fw(6 + z), func=AF.Exp), reads=[fwb[6 + z]], writes=[b_ST[2 + z]])
                    if n < J - 1:
                        P.op("pe", lambda e: e.matmul(ps(2), lhsT=li_bf, rhs=st(z), start=False, stop=False),
                             reads=[b_ST[z], b_CB], writes=[psb[2]])

                stage_a(0)
                stage_a2(0)
                for n in range(J):
                    if n + 1 < J:
                        stage_a(n + 1)
                    stage_b(n)
                    if n + 1 < J:
                        stage_a2(n + 1)
                    stage_b2(n)
                    if getattr(cfg, "debug", False) and lh == 0 and tq == 0 and n == 0:
                        for kk, src_k in enumerate((0, 2, 4, 6)):
                            P.dma("sp", lambda e, kk=kk, src_k=src_k: e.dma_start(out=dbgT[:, kk * 512:(kk + 1) * 512], in_=fw(src_k)), b_dbgT, reads=[fwb[src_k]])
                        for kk, src_k in enumerate((0, 2)):
                            P.dma("sp", lambda e, kk=kk, src_k=src_k: e.dma_start(out=dbgB[:, kk * 512:(kk + 1) * 512], in_=st(src_k)), b_dbgT, reads=[b_ST[src_k]])
                        P.dma("sp", lambda e: e.dma_start(out=dbgB[:, 1024:1536], in_=QT[:, 0:512]), b_dbgT, reads=[bq[0]])
                        P.dma("sp", lambda e: e.dma_start(out=dbgB[:, 1536:2048], in_=KT[:, 0:512]), b_dbgT, reads=[bk[0]])
                        P.dma("sp", lambda e: e.dma_start(out=dbgB[:, 2048:2560], in_=VS[:, 0:512]), b_dbgT, reads=[bv[0]])
                o_mm(J - 1)
                ok = tq % 2
                P.op("act", lambda e, ok=ok: e.copy(out=ost(ok), in_=ps(3)), reads=[psb[3]], writes=[b_OST[ok]])
                P.dma("act", lambda e, ok=ok, lh=lh, tq=tq: e.dma_start(out=OXL[lh * 128:(lh + 1) * 128, tq * 512:(tq + 1) * 512], in_=ost(ok)),
                      b_OXL, reads=[b_OST[ok]])


            STA, STB = AT[:, 5 * S:6 * S], AT[:, 6 * S:7 * S]
            nbs = S // TT
            bsa, bsb = b_ATc[5 * nbs:6 * nbs], b_ATc[6 * nbs:7 * nbs]
            STAv = STA.rearrange("p (b d) -> p b d", d=128)
            STBv = STB.rearrange("p (b d) -> p b d", d=128)
            xvs3 = XVSG.ap().rearrange("(q b p) c -> q p b c", p=128, b=HV // 128)

            def load_fm(rows_a, rows_b, dst, dbufs, banks):
                for rp in range(2):
                    ra, rb = rows_a(rp), rows_b(rp)
                    P.dma("sp", lambda e, rp=rp, ra=ra: e.dma_start(out=STA[:, rp * T:(rp + 1) * T], in_=XQG[ra:ra + 128, :]),
                          bsa[0], reads=[b_XQGc[ra // 1024]], extra_writes=bsa[1:])
                    P.dma("sp", lambda e, rp=rp, rb=rb: e.dma_start(out=STB[:, rp * T:(rp + 1) * T], in_=XQG[rb:rb + 128, :]),
                          bsb[0], reads=[b_XQGc[rb // 1024]], extra_writes=bsb[1:])
                select_pieces(S // 512, lambda pc: STA[:, pc * 512:(pc + 1) * 512], lambda pc: STB[:, pc * 512:(pc + 1) * 512],
                              lambda pc: [bsa[0]], lambda pc: [bsb[0]], lambda pc: dst[:, pc * 512:(pc + 1) * 512],
                              lambda pc: [dbufs[0]], banks)

            for lh in range(4):
                load_fm(lambda rp: rp * 512 + lh * 128, lambda rp: 1024 + rp * 512 + lh * 128, QT, bq, (4, 5))
                load_fm(lambda rp: 2048 + rp * 512 + lh * 128, lambda rp: 3072 + rp * 512 + lh * 128, KT, bk, (4, 5))
                for rp in range(2):
                    for c in range(2):
                        blk0 = rp * nhb + c * (HV // 128)
                        P.dma("sp", lambda e, lh=lh, rp=rp, c=c, blk0=blk0: e.dma_start(
                            out=STAv[:, blk0:blk0 + HV // 128, :], in_=xvs3[c * 2 + rp, :, :, lh * 128:(lh + 1) * 128]),
                            bsa[0], reads=[b_XVGs], extra_writes=bsa[1:])
                        P.dma("sp", lambda e, lh=lh, rp=rp, c=c, blk0=blk0: e.dma_start(
                            out=STBv[:, blk0:blk0 + HV // 128, :], in_=xvs3[c * 2 + rp, :, :, (4 + lh) * 128:(5 + lh) * 128]),
                            bsb[0], reads=[b_XVGs], extra_writes=bsb[1:])
                select_pieces(S // 512, lambda pc: STA[:, pc * 512:(pc + 1) * 512], lambda pc: STB[:, pc * 512:(pc + 1) * 512],
                              lambda pc: [bsa[0]], lambda pc: [bsb[0]], lambda pc: VS[:, pc * 512:(pc + 1) * 512],
                              lambda pc: [bv[0]], (4, 5))
                for tq in range(S // 512):
                    sb_tile(lh, tq)
            for c in range(4):
                o_coll(c)
            P.barrier()
            QN, KN, QP, KP, VD = (AT[:, k * S:(k + 1) * S] for k in range(5))
            nb_ = S // TT
            bqn, bkn, bqp, bkp, bvd = (b_ATc[k * nb_:(k + 1) * nb_] for k in range(5))
            ACC = XN[:, 0:16384].bitcast(F32)
            ACCN, ACCD = ACC[:, 0:S], ACC[:, S:2 * S]
            bacc = b_XNc[0]
            for lh2 in range(2):
                P.op("dve", lambda e: e.memset(ACC[:, :], 0.0), writes=[bacc])
                for g in range(NG):
                    r = DILS[g]
                    UB = S // (128 * r)
                    UBh = UB // 2
                    load_fm(lambda rp: (4 + g) * 1024 + rp * 512 + lh2 * 128, lambda rp: (4 + g) * 1024 + rp * 512 + (2 + lh2) * 128, QN, bqn, (0, 1))
                    load_fm(lambda rp: (7 + g) * 1024 + rp * 512 + lh2 * 128, lambda rp: (7 + g) * 1024 + rp * 512 + (2 + lh2) * 128, KN, bkn, (0, 1))
                    vsrc = XVDG[g].ap().rearrange("(q rho b p) c -> q p rho b c", p=128, rho=r, b=UBh)
                    sta_v = STA.rearrange("p (rho ub d) -> p rho ub d", rho=r, ub=UB)
                    stb_v = STB.rearrange("p (rho ub d) -> p rho ub d", rho=r, ub=UB)
                    for rp in range(2):
                        for rho in range(r):
                            P.dma("sp", lambda e, rp=rp, lh2=lh2, vsrc=vsrc, sta_v=sta_v, rho=rho, UBh=UBh: e.dma_start(
                                out=sta_v[:, rho, rp * UBh:(rp + 1) * UBh, :], in_=vsrc[rp, :, rho, :, lh2 * 128:(lh2 + 1) * 128]),
                                bsa[0], reads=[b_XVGd[g]], extra_writes=bsa[1:])
                            P.dma("sp", lambda e, rp=rp, lh2=lh2, vsrc=vsrc, stb_v=stb_v, rho=rho, UBh=UBh: e.dma_start(
                                out=stb_v[:, rho, rp * UBh:(rp + 1) * UBh, :], in_=vsrc[rp, :, rho, :, (2 + lh2) * 128:(3 + lh2) * 128]),
                                bsb[0], reads=[b_XVGd[g]], extra_writes=bsb[1:])
                    select_pieces(S // 512, lambda pc: STA[:, pc * 512:(pc + 1) * 512], lambda pc: STB[:, pc * 512:(pc + 1) * 512],
                                  lambda pc: [bsa[0]], lambda pc: [bsb[0]], lambda pc: VD[:, pc * 512:(pc + 1) * 512],
                                  lambda pc: [bvd[0]], (0, 1))
                    if r == 1:
                        Qs, Ks, bqs, bks = QN, KN, bqn, bkn
                    else:
                        P.op("act", lambda e, r=r: e.copy(out=QP.rearrange("p (r u) -> p r u", r=r), in_=QN.rearrange("p (u r) -> p r u", r=r)),
                             reads=[bqn[0]], writes=bqp)
                        P.op("dve", lambda e, r=r: e.tensor_copy(out=KP.rearrange("p (r u) -> p r u", r=r), in_=KN.rearrange("p (u r) -> p r u", r=r)),
                             reads=[bkn[0]], writes=bkp)
                        Qs, Ks, bqs, bks = QP, KP, bqp, bkp
                    VDv = VD.rearrange("p (b d) -> p b d", d=128)
                    bias_o = CF_DIL + ((g * 2 + lh2) * 2) * 128
                    accn_v = ACCN.rearrange("p (u r) -> p r u", r=r)
                    accd_v = ACCD.rearrange("p (u r) -> p r u", r=r)
                    cnt = 0
                    for rho in range(r):
                        for ub in range(UB):
                            blk = rho * UB + ub
                            q4 = cnt % 4
                            cnt += 1
                            kbs = [(blk, 0)] + ([(blk - 1, 1)] if ub > 0 else [])
                            for ki, (kb, kind) in enumerate(kbs):
                                sq4 = (2 * q4 + ki) % 4
                                fk = (2 * cnt + ki) % 4
                                sps = ps(sq4, 128)
                                P.op("pe", lambda e, kb=kb, blk=blk, sps=sps, Ks=Ks, Qs=Qs: e.matmul(
                                    sps, lhsT=Ks[:, kb * 128:(kb + 1) * 128], rhs=Qs[:, blk * 128:(blk + 1) * 128], start=True, stop=True),
                                    reads=[bks[0], bqs[0]], writes=[psb[sq4]])
                                bias = CF[:, bias_o + kind * 128: bias_o + (kind + 1) * 128]
                                P.op("dve", lambda e, sps=sps, bias=bias, fk=fk: e.scalar_tensor_tensor(
                                    out=fw(fk, 128), in0=sps, scalar=SCALE, in1=bias, op0=ALU.mult, op1=ALU.add),
                                    reads=[psb[sq4], b_CF], writes=[fwb[fk]])
                                P.op("act", lambda e, fk=fk: e.activation(out=st(fk, 128), in_=fw(fk, 128), func=AF.Exp),
                                     reads=[fwb[fk]], writes=[b_ST[fk]])
                                P.op("pe", lambda e, kb=kb, fk=fk, q4=q4, ki=ki, nk=len(kbs): e.matmul(
                                    ps(4 + q4 % 2, 128), lhsT=VDv[:, kb, :], rhs=st(fk, 128), start=(ki == 0), stop=(ki == nk - 1)),
                                    reads=[bvd[0], b_ST[fk]], writes=[psb[4 + q4 % 2]])
                                P.op("pe", lambda e, fk=fk, q4=q4, ki=ki, nk=len(kbs): e.matmul(
                                    ps(6 + q4 % 2, 128), lhsT=ones_bf, rhs=st(fk, 128), start=(ki == 0), stop=(ki == nk - 1)),
                                    reads=[b_CB, b_ST[fk]], writes=[psb[6 + q4 % 2]])
                            an = accn_v[:, rho, ub * 128:(ub + 1) * 128]
                            ad = accd_v[:, rho, ub * 128:(ub + 1) * 128]
                            P.op("dve", lambda e, an=an, q4=q4: e.tensor_tensor(out=an, in0=an, in1=ps(4 + q4 % 2, 128), op=ALU.add),
                                 reads=[psb[4 + q4 % 2], bacc], writes=[bacc])
                            P.op("dve", lambda e, ad=ad, q4=q4: e.tensor_tensor(out=ad, in0=ad, in1=ps(6 + q4 % 2, 128), op=ALU.add),
                                 reads=[psb[6 + q4 % 2], bacc], writes=[bacc])
                for pc in range(S // 512):
                    k = 4 + pc % 2
                    ok = pc % 2
                    P.op("dve", lambda e, pc=pc, k=k: e.reciprocal(out=fw(k), in_=ACCD[:, pc * 512:(pc + 1) * 512]), reads=[bacc], writes=[fwb[k]])
                    P.op("dve", lambda e, pc=pc, k=k, ok=ok: e.tensor_tensor(out=ost(ok), in0=ACCN[:, pc * 512:(pc + 1) * 512], in1=fw(k), op=ALU.mult),
                         reads=[bacc, fwb[k]], writes=[b_OST[ok]])
                    P.dma("act", lambda e, pc=pc, ok=ok, lh2=lh2: e.dma_start(
                        out=OXL[512 + lh2 * 128:512 + (lh2 + 1) * 128, pc * 512:(pc + 1) * 512], in_=ost(ok)),
                        b_OXL, reads=[b_OST[ok]])

        pump_layer(0)
        for L in range(DEPTH):
            for i in range(NT):
                if L == 0:
                    norm_pass(L, i, xT, b_X, False, 0, True, 0)
                else:
                    norm_pass(L, i, HT, b_HT[i], False, 0, True, 0)

                def ep_qk(c, bank, i=i):
                    k = next_st()
                    if c % 2 == 0:
                        P.op("act", lambda e: e.copy(out=st(k), in_=ps(bank)), reads=[psb[bank]], writes=[b_ST[k]])
                    else:
                        P.op("dve", lambda e: e.tensor_copy(out=st(k), in_=ps(bank)), reads=[psb[bank]], writes=[b_ST[k]])
                    P.dma("act", lambda e: e.dma_start(out=XQL[c * 128:(c + 1) * 128, i * TT:(i + 1) * TT], in_=st(k)),
                          b_XQL, reads=[b_ST[k]])
                dense_fm(L, ["in_qk"], xn, lambda kc: b_XNc[kc], ep_qk)

                def ep_gate(c, bank, i=i):
                    k = next_st()
                    P.op("act", lambda e: e.activation(out=st(k), in_=ps(bank), func=AF.Sigmoid), reads=[psb[bank]], writes=[b_ST[k]])
                    P.dma("act", lambda e: e.dma_start(out=GATE[c * 128:(c + 1) * 128, i * TT:(i + 1) * TT], in_=st(k)),
                          b_GATE[i], reads=[b_ST[k]])
                dense_fm(L, ["in_gate"], xn, lambda kc: b_XNc[kc], ep_gate)

                sg = cfg.seg["in_v"]
                VW = W_SB + W_DIL
                VST = AT[:, 0:NSUB * VW].rearrange("p (s c) -> p s c", s=NSUB)
                b_VST = b_ATc[0:(NSUB * VW + TT - 1) // TT]
                for b in range(sg["nblk"]):
                    wt, wbuf = slots_b.load(L, "in_v", b)
                    r = 1 if b < 4 else DILS[(b - 4) // 2]
                    for sbk in range(NSUB):
                        bank = next_bank()
                        if r == 1:
                            cols = lambda kc, sbk=sbk: XN[:, kc * TT + sbk * 128: kc * TT + (sbk + 1) * 128]
                        elif r == 4:
                            cols = lambda kc, sbk=sbk: xn(kc).rearrange("p (u r) -> p r u", r=4)[:, sbk, :]
                        else:
                            cols = lambda kc, sbk=sbk: xn(kc).rearrange("p (u r) -> p r u", r=4)[:, sbk, :]
                        for kc in range(KC):
                            P.op("pe", lambda e, kc=kc, cols=cols, bank=bank, wt=wt: e.matmul(
                                ps(bank, 256), lhsT=cols(kc), rhs=wt[:, kc * 256:(kc + 1) * 256], start=(kc == 0), stop=(kc == KC - 1)),
                                reads=[wbuf, b_XNc[kc]], writes=[psb[bank]])
                        dstv = VST[:, sbk, b * 256:(b + 1) * 256]
                        if (b + sbk) % 2 == 0:
                            P.op("act", lambda e, dstv=dstv, bank=bank: e.copy(out=dstv, in_=ps(bank, 256)), reads=[psb[bank]], writes=[b_VST[0]])
                        else:
                            P.op("dve", lambda e, dstv=dstv, bank=bank: e.tensor_copy(out=dstv, in_=ps(bank, 256)), reads=[psb[bank]], writes=[b_VST[0]])
                for sbk in range(NSUB):
                    r0 = i * TT + sbk * 128
                    P.dma("act", lambda e, sbk=sbk, r0=r0: e.dma_start(out=XVS[r0:r0 + 128, :], in_=VST[:, sbk, 0:W_SB]), b_XV, reads=[b_VST[0]])
                    P.dma("act", lambda e, sbk=sbk, r0=r0: e.dma_start(out=XVD[0][r0:r0 + 128, :], in_=VST[:, sbk, W_SB:W_SB + 512]), b_XV, reads=[b_VST[0]])
                    rr = sbk * (T // 4) + i * (TT // 4)
                    P.dma("act", lambda e, sbk=sbk, rr=rr: e.dma_start(out=XVD[1][rr:rr + 128, :], in_=VST[:, sbk, W_SB + 512:W_SB + 1024]), b_XV, reads=[b_VST[0]])
                    for m in range(4):
                        rr2 = (4 * m + sbk) * (T // 16) + i * (TT // 16)
                        P.dma("act", lambda e, sbk=sbk, m=m, rr2=rr2: e.dma_start(
                            out=XVD[2][rr2:rr2 + TT // 16, :], in_=AT[m:128:4, sbk * VW + W_SB + 1024: sbk * VW + W_SB + 1536]),
                            b_XV, reads=[b_VST[0]])
                for bb in b_VST[1:]:
                    bb.w = b_VST[0].w
                    bb.rs = dict(b_VST[0].rs)

            def xq_coll(c):
                P.coll(lambda e, c=c: e.collective_compute("AllGather", ALU.bypass, replica_groups=PAIRS,
                                                           ins=[XQL[c * 512:(c + 1) * 512, :]], outs=[XQG[c * 1024:(c + 1) * 1024, :]]),
                       reads=[b_XQL], writes=[b_XQGc[c]])
            for c in range(4):
                xq_coll(c)
            for c in range(2):
                P.coll(lambda e, c=c: e.collective_compute("AllGather", ALU.bypass, replica_groups=PAIRS,
                                                           ins=[XVS[c * HV:(c + 1) * HV, :]], outs=[XVSG[c * 2 * HV:(c + 1) * 2 * HV, :]]),
                       reads=[b_XV], writes=[b_XVGs])
            for c in range(4, NQC):
                xq_coll(c)
            for g in range(NG):
                P.coll(lambda e, g=g: e.collective_compute("AllGather", ALU.bypass, replica_groups=PAIRS,
                                                           ins=[XVD[g][:, :]], outs=[XVDG[g][:, :]]),
                       reads=[b_XV], writes=[b_XVGd[g]])
            if L + 1 < DEPTH:
                pump_weights(ncoll_layer // 3)
            P.barrier()

            attention()
            P.barrier()
            for c in range(4, 6):
                o_coll(c)
            if L + 1 < DEPTH:
                pump_layer(L + 1)

            for i in range(NT):
                for kc in range(12):
                    rblk = (kc % 4) if kc < 8 else (4 + (kc - 8) % 2)
                    rp = (kc // 4) if kc < 8 else ((kc - 8) // 2)
                    row0 = (rblk * 2 + rp) * 128
                    P.dma("sp", lambda e, kc=kc, row0=row0, i=i: e.dma_start(out=at(KC + kc), in_=OXG[row0:row0 + 128, i * TT:(i + 1) * TT]),
                          b_ATc[KC + kc], reads=[b_OXG])
                    P.dma("sp", lambda e, kc=kc, row0=row0, i=i: e.dma_start(out=at(KC + 12 + kc), in_=OXG[row0:row0 + 128, T + i * TT:T + (i + 1) * TT]),
                          b_ATc[KC + 12 + kc], reads=[b_OXG])
                select_pieces(12, lambda pc: at(KC + pc), lambda pc: at(KC + 12 + pc), lambda pc: [b_ATc[KC + pc]], lambda pc: [b_ATc[KC + 12 + pc]],
                              lambda pc: xn(pc), lambda pc: [b_XNc[pc]], (0, 1))
                gv = GATE.ap().rearrange("(c p) t -> p c t", p=128)
                for b in range(cfg.seg["p_sb"]["nblk"]):
                    wsb_t, wsb_b = slots_s.load(L, "p_sb", b)
                    wd_t, wd_b = slots_s.load(L, "p_dil", b)
                    for sub in range(2):
                        oc = b * 2 + sub
                        kg = oc % 2
                        P.dma("sp", lambda e, oc=oc, kg=kg, i=i: e.dma_start(out=st(kg), in_=gv[:, oc, i * TT:(i + 1) * TT]),
                              b_ST[kg], reads=[b_GATE[i]])
                        P.dma("sp", lambda e, oc=oc, kg=kg, i=i: e.dma_start(out=st(2 + kg), in_=gv[:, KC + oc, i * TT:(i + 1) * TT]),
                              b_ST[2 + kg], reads=[b_GATE[i]])
                        for kc in range(8):
                            lhsT = wsb_t[:, kc * 256 + sub * 128: kc * 256 + (sub + 1) * 128]
                            P.op("pe", lambda e, lhsT=lhsT, kc=kc: e.matmul(ps(2), lhsT=lhsT, rhs=xn(kc), start=(kc == 0), stop=(kc == 7)),
                                 reads=[wsb_b, b_XNc[kc]], writes=[psb[2]])
                        for kc in range(4):
                            lhsT = wd_t[:, kc * 256 + sub * 128: kc * 256 + (sub + 1) * 128]
                            P.op("pe", lambda e, lhsT=lhsT, kc=kc: e.matmul(ps(3), lhsT=lhsT, rhs=xn(8 + kc), start=(kc == 0), stop=(kc == 3)),
                                 reads=[wd_b, b_XNc[8 + kc]], writes=[psb[3]])
                        P.op("dve", lambda e, kg=kg: e.tensor_tensor(out=fw(kg), in0=ps(2), in1=st(kg), op=ALU.mult),
                             reads=[psb[2], b_ST[kg]], writes=[fwb[kg]])
                        P.op("dve", lambda e, kg=kg: e.tensor_tensor(out=fw(2 + kg), in0=ps(3), in1=st(2 + kg), op=ALU.mult),
                             reads=[psb[3], b_ST[2 + kg]], writes=[fwb[2 + kg]])
                        P.op("dve", lambda e, kg=kg, oc=oc: e.tensor_tensor(out=at(oc), in0=fw(kg), in1=fw(2 + kg), op=ALU.add),
                             reads=[fwb[kg], fwb[2 + kg]], writes=[b_ATc[oc]])

                def ep_y(c, bank, i=i, total=KC):
                    k = 4 + c % 2
                    P.op("act", lambda e: e.copy(out=fw(k), in_=ps(bank)), reads=[psb[bank]], writes=[fwb[k]])
                    sumsq_accum(ps(bank), [psb[bank]], c % 2, c == 0, c == total - 1)
                    P.dma("act", lambda e: e.dma_start(out=Y[c * 128:(c + 1) * 128, i * TT:(i + 1) * TT], in_=fw(k)),
                          b_Y[i], reads=[fwb[k]])
                dense_fm(L, ["w_out"], at, lambda kc: b_ATc[kc], ep_y)
                rstd_from_sumsq(0)
                if L == 0:
                    norm_pass(L, i, xT, b_X, True, 1, True, 2)
                else:
                    norm_pass(L, i, HT, b_HT[i], True, 1, True, 2)

                if getattr(cfg, "debug", False) and i == 0:
                    P.dma("sp", lambda e, i=i: e.dma_start(out=dbgH1[:, i * TT:(i + 1) * TT], in_=HT[:, i * TT:(i + 1) * TT]), b_dbgT, reads=[b_HT[i]])
                    P.dma("sp", lambda e, i=i: e.dma_start(out=dbgY1[:, i * TT:(i + 1) * TT], in_=Y[:, i * TT:(i + 1) * TT]), b_dbgT, reads=[b_Y[i]])
                gu_state = {}

                def ep_gu(c, bank, i=i):
                    blk, which = c // 2, c % 2
                    if which == 0:
                        k = 6 + blk % 2
                        P.op("act", lambda e: e.activation(out=fw(k), in_=ps(bank), func=AF.Silu), reads=[psb[bank]], writes=[fwb[k]])
                        gu_state["k"] = k
                    else:
                        k = gu_state["k"]
                        P.op("dve", lambda e: e.tensor_tensor(out=at(blk), in0=fw(k), in1=ps(bank), op=ALU.mult),
                             reads=[fwb[k], psb[bank]], writes=[b_ATc[blk]])
                dense_fm(L, ["ffn_gu"], xn, lambda kc: b_XNc[kc], ep_gu)
                dense_fm(L, ["ffn_d0", "ffn_d1"], at, lambda kc: b_ATc[kc], ep_y)
                rstd_from_sumsq(0)
                norm_pass(L, i, HT, b_HT[i], True, 3, True, 4)

                if getattr(cfg, "debug", False) and i == 0:
                    P.dma("sp", lambda e, i=i: e.dma_start(out=dbgH2[:, i * TT:(i + 1) * TT], in_=HT[:, i * TT:(i + 1) * TT]), b_dbgT, reads=[b_HT[i]])
                    P.dma("sp", lambda e, i=i: e.dma_start(out=dbgY2[:, i * TT:(i + 1) * TT], in_=Y[:, i * TT:(i + 1) * TT]), b_dbgT, reads=[b_Y[i]])
                def ep_d1(c, bank):
                    P.op("act", lambda e: e.copy(out=at(c), in_=ps(bank)), reads=[psb[bank]], writes=[b_ATc[c]])
                dense_fm(L, ["ple_gd"], xn, lambda kc: b_XNc[kc], ep_d1)
                ptv = PTB.ap().rearrange("(l c p) t -> l p c t", c=2, p=128)
                for c2 in range(2):
                    P.dma("sp", lambda e, i=i, L=L, c2=c2: e.dma_start(out=at(2 + c2), in_=ptv[L, :, c2, i * TT:(i + 1) * TT]),
                          b_ATc[2 + c2], reads=[b_PTB])
                for b in range(cfg.seg["ple_gu"]["nblk"]):
                    wgu_t, wgu_b = slots_s.load(L, "ple_gu", b)
                    win_t, win_b = slots_s.load(L, "ple_in", b)
                    for sub in range(4):
                        oc = b * 4 + sub
                        for kc in range(2):
                            P.op("pe", lambda e, kc=kc, sub=sub, wgu_t=wgu_t: e.matmul(
                                ps(2), lhsT=wgu_t[:, kc * 512 + sub * 128: kc * 512 + (sub + 1) * 128], rhs=at(kc),
                                start=(kc == 0), stop=(kc == 1)), reads=[wgu_b, b_ATc[kc]], writes=[psb[2]])
                        for kc in range(2):
                            P.op("pe", lambda e, kc=kc, sub=sub, win_t=win_t: e.matmul(
                                ps(3), lhsT=win_t[:, kc * 512 + sub * 128: kc * 512 + (sub + 1) * 128], rhs=at(2 + kc),
                                start=(kc == 0), stop=(kc == 1)), reads=[win_b, b_ATc[2 + kc]], writes=[psb[3]])
                        k = oc % 2
                        k2 = 4 + oc % 2
                        P.op("act", lambda e, k=k: e.activation(out=fw(k), in_=ps(2), func=AF.Sigmoid), reads=[psb[2]], writes=[fwb[k]])
                        P.op("dve", lambda e, k=k, k2=k2: e.tensor_tensor(out=fw(k2), in0=fw(k), in1=ps(3), op=ALU.mult),
                             reads=[fwb[k], psb[3]], writes=[fwb[k2]])
                        sumsq_accum(fw(k2), [fwb[k2]], oc % 2, oc == 0, oc == KC - 1)
                        P.dma("act", lambda e, oc=oc, k2=k2, i=i: e.dma_start(out=Y[oc * 128:(oc + 1) * 128, i * TT:(i + 1) * TT], in_=fw(k2)),
                              b_Y[i], reads=[fwb[k2]])
                rstd_from_sumsq(0)
                last = (L == DEPTH - 1)
                norm_pass(L, i, HT, b_HT[i], True, 5, False, 0, h_dst=(outT if last else None), b_dst=(b_OUT if last else None))
            P.barrier()

        if getattr(cfg, "debug", False):
            b_dbg = P.buf("dbg", dma=True)
            for name, t in (("XQG", XQG), ("XVSG", XVSG), ("OXG", OXG), ("XQL", XQL), ("XVS", XVS), ("XVD0", XVD[0]), ("XVD1", XVD[1]), ("XVD2", XVD[2]), ("GATE", GATE), ("OXL", OXL), ("Ydbg", Y)):
                shp = list(t.shape)
                dt_ = BF16 if name != "Ydbg" else F32
                o = nc.dram_tensor("dbg_" + name, shp, dt_, kind="ExternalOutput")
                step = max(1, 1024 * 1024 // (shp[1] * 2))
                for r0 in range(0, shp[0], step):
                    r1 = min(shp[0], r0 + step)
                    P.dma("sp", lambda e, o=o, t=t, r0=r0, r1=r1: e.dma_start(out=o[r0:r1, :], in_=t[r0:r1, :]), b_dbg)
        P.barrier(engines=("pe", "act", "dve", "sp", "pool"))

        with nc.Block() as block:
            @block.sync
            def _(e):
                P.replay("sp", e)

            @block.tensor
            def _(e):
                P.replay("pe", e)

            @block.vector
            def _(e):
                P.replay("dve", e)

            @block.scalar
            def _(e):
                P.replay("act", e)

            @block.gpsimd
            def _(e):
                P.replay("pool", e)
    return nc


def run(cfg, x, p, w_in, w_proj_sb, w_proj_dil, w_out, g_mix_pre, g_mix_post, w_ffn_gate, w_ffn_up, w_ffn_down,
        g_ffn_pre, g_ffn_post, w_ple_in, w_ple_gate_down, w_ple_gate_up, g_ple_gate, g_ple_post):
    D, S, T, DEPTH, KC = cfg.D, cfg.S, cfg.T, cfg.DEPTH, cfg.KC
    B = x.shape[0]
    assert B * 2 == 8
    f = lambda a: np.asarray(a, dtype=np.float32)
    x, p = f(x), f(p)
    nc = build(cfg)
    shards = [[None] * DEPTH for _ in range(8)]
    for L in range(DEPTH):
        flats = pack_layer(cfg, f(w_in[L]), f(w_proj_sb[L]), f(w_proj_dil[L]), f(w_out[L]), f(w_ffn_gate[L]), f(w_ffn_up[L]),
                           f(w_ffn_down[L]), f(w_ple_in[L]), f(w_ple_gate_down[L]), f(w_ple_gate_up[L]))
        for c in range(8):
            shards[c][L] = []
        for g, flat in enumerate(flats):
            v = flat.reshape(cfg.gncoll[g], 8, CH)
            for c in range(8):
                shards[c][L].append(np.ascontiguousarray(v[:, c, :]).reshape(cfg.gncoll[g] * CH // FLATW, FLATW))
        del flats, v
    gl = [g_mix_pre, g_mix_post, g_ffn_pre, g_ffn_post, g_ple_gate, g_ple_post]
    gains = np.stack([f(g) for g in gl], 0)
    gains = np.ascontiguousarray(gains.reshape(6, DEPTH, KC, 128).transpose(3, 0, 1, 2)).reshape(128, 6 * DEPTH * KC)
    in_maps = []
    for c in range(8):
        b, s = c // 2, c % 2
        m = {
            "xT": np.ascontiguousarray(x[b, s * T:(s + 1) * T, :].T),
            "pT": np.ascontiguousarray(p[:, b, s * T:(s + 1) * T, :].transpose(0, 2, 1)).reshape(DEPTH * D_PLE, T),
            "gains": gains,
            "consts": make_consts(s),
        }
        for L in range(DEPTH):
            for g in range(cfg.NGRP):
                m[f"wsh{L}_{g}"] = shards[c][L][g]
        in_maps.append(m)
    res = run_bass_kernel_spmd(nc, in_maps, core_ids=list(range(8)))
    if getattr(cfg, "debug", False):
        cfg.dbg = res.results
    out = np.empty((B, S, D), np.float32)
    for c in range(8):
        b, s = c // 2, c % 2
        out[b, s * T:(s + 1) * T, :] = res.results[c]["outT"].T
    return out


def kernel(**inputs):
    cfg = Cfg()
    return run(cfg, **inputs)
```

```python
import contextlib
import numpy as np
import concourse.bass as bass
import concourse.mybir as mybir
from concourse.bass_utils import run_bass_kernel_spmd

F32 = mybir.dt.float32
BF16 = mybir.dt.bfloat16
AF = mybir.ActivationFunctionType
ALU = mybir.AluOpType

HEAD = 128
NH_SB = 8
NG = 3
DILS = (1, 4, 16)
NH_G = 4
W_SB = NH_SB * HEAD
W_DIL = NG * NH_G * HEAD
D_PLE = 256
EPS = 1e-6
NEG = -30000.0
CH = 262144
FLATW = 2048


class Cfg:
    def __init__(self, D=4096, S=4096, DFF=11008, DEPTH=4, TT=512):
        self.D, self.S, self.DFF, self.DEPTH, self.TT = D, S, DFF, DEPTH, TT
        self.T = S // 2
        self.NT = self.T // TT
        self.KC = D // 128
        self.FC = DFF // 128
        self.NQK = (2 * W_SB + 2 * W_DIL) // 128
        segs = [
            ("in_qk", D, 2 * W_SB + 2 * W_DIL, 256),
            ("in_gate", D, 2 * D, 256),
            ("in_v", D, W_SB + W_DIL, 256),
            ("p_sb", W_SB, D, 256),
            ("p_dil", NH_G * HEAD, D, 256),
            ("w_out", D, D, 256),
            ("ffn_gu", D, 2 * DFF, 256),
            ("ffn_d0", DFF // 2, D, 128),
            ("ffn_d1", DFF // 2, D, 128),
            ("ple_gd", D, D_PLE, 256),
            ("ple_gu", D_PLE, D, 512),
            ("ple_in", D_PLE, D, 512),
        ]
        self.seg = {}
        groups = [["in_qk", "in_gate", "in_v", "p_sb", "p_dil", "w_out"], ["ffn_gu"],
                  ["ffn_d0", "ffn_d1", "ple_gd", "ple_gu", "ple_in"]]
        sd = {n: (K, N, NB) for n, K, N, NB in segs}
        per = 8 * CH
        self.gflat, self.gncoll, self.gsegs = [], [], groups
        for gi, names in enumerate(groups):
            off = 0
            for name in names:
                K, N, NB = sd[name]
                be = (K // 128) * NB
                self.seg[name] = dict(off=off, K=K, N=N, NB=NB, be=be, nblk=N // NB, kc=K // 128, grp=gi)
                off += K * N
            fl = ((off + per - 1) // per) * per
            self.gflat.append(fl)
            self.gncoll.append(fl // per)
        self.NGRP = len(groups)


def _pack(W, NB):
    K, N = W.shape
    return np.ascontiguousarray(W.reshape(K // 128, 128, N // NB, NB).transpose(2, 1, 0, 3)).reshape(-1)


def pack_layer(cfg, w_in, w_proj_sb, w_proj_dil, w_out, w_g, w_u, w_d, w_ple_in, w_gd, w_gu):
    D = cfg.D
    o = 0
    q_sb = w_in[:, o:o + W_SB]; o += W_SB
    k_sb = w_in[:, o:o + W_SB]; o += W_SB
    v_sb = w_in[:, o:o + W_SB]; o += W_SB
    q_d = w_in[:, o:o + W_DIL]; o += W_DIL
    k_d = w_in[:, o:o + W_DIL]; o += W_DIL
    v_d = w_in[:, o:o + W_DIL]; o += W_DIL
    g_sb = w_in[:, o:o + D]; o += D
    g_d = w_in[:, o:o + D]; o += D
    FC = cfg.FC
    gu = np.concatenate([w_g.reshape(D, FC, 1, 128), w_u.reshape(D, FC, 1, 128)], axis=2).reshape(D, 2 * cfg.DFF)
    hf = cfg.DFF // 2
    parts = [
        _pack(np.concatenate([q_sb, k_sb, q_d, k_d], axis=1), 256),
        _pack(np.concatenate([g_sb, g_d], axis=1), 256),
        _pack(np.concatenate([v_sb, v_d], axis=1), 256),
        _pack(w_proj_sb, 256),
        _pack(w_proj_dil, 256),
        _pack(w_out, 256),
        _pack(gu, 256),
        _pack(w_d[:hf], 128),
        _pack(w_d[hf:], 128),
        _pack(w_gd, 256),
        _pack(w_gu, 512),
        _pack(w_ple_in, 512),
    ]
    names = ["in_qk", "in_gate", "in_v", "p_sb", "p_dil", "w_out", "ffn_gu", "ffn_d0", "ffn_d1", "ple_gd", "ple_gu", "ple_in"]
    flats = [np.zeros(fl, np.float32) for fl in cfg.gflat]
    for name, p in zip(names, parts):
        sg = cfg.seg[name]
        assert p.size == sg["K"] * sg["N"]
        flats[sg["grp"]][sg["off"]:sg["off"] + p.size] = p
    return flats


CB_ONES, CB_TRI, CB_LI, CB_M01, CB_MNEG = 0, 128, 256, 384, 384 + 2048
CB_SEL = 384 + 4096
NCB = 384 + 4096 + 256
CF_DIL = 0
CF_ONE = NG * 2 * 2 * 128
CF_EPS = CF_ONE + 1
NCF = CF_EPS + 1


def make_consts(rank):
    p = np.arange(128)[:, None].astype(np.float64)
    tri = (p > np.arange(128)[None, :]).astype(np.float64)
    cb = [np.ones((128, 128)), tri, 1.0 - tri]
    c512 = np.arange(512)[None, :]
    for jj in range(4):
        cb.append((c512 > 128 * jj + p).astype(np.float64))
    for jj in range(4):
        cb.append(np.where(c512 > 128 * jj + p, 0.0, NEG))
    eye = np.eye(128)
    cb.append(eye if rank == 0 else np.zeros((128, 128)))
    cb.append(eye if rank == 1 else np.zeros((128, 128)))
    slopes = np.exp2(-8.0 * np.arange(1, NH_G + 1) / NH_G)
    pq = np.arange(128)[None, :]
    cf = []
    for g in range(NG):
        r = DILS[g]
        for lh in range(2):
            sl = slopes[2 * rank + lh]
            cf.append(np.where(pq >= p, -sl * r * (pq - p), NEG))
            cf.append(np.where(p >= pq, -sl * r * (128 + pq - p), NEG))
    cf.append(np.ones((128, 1)))
    cf.append(np.full((128, 1), EPS))
    return np.concatenate(cb + cf, axis=1).astype(np.float32)


class Buf:
    __slots__ = ("name", "w", "rs", "dsem", "dcnt")

    def __init__(self, name):
        self.name = name
        self.w = None
        self.rs = {}
        self.dsem = None
        self.dcnt = 0


class Prog:
    ENGS = ("pe", "act", "dve", "pool", "sp")

    def __init__(self, nc, stack):
        self.nc = nc
        self.stack = stack
        self.ops = {e: [] for e in self.ENGS}
        self.esem = {e: self.sem("prog_" + e) for e in self.ENGS}
        self.ecnt = {e: 0 for e in self.ENGS}
        self.seen = {e: {} for e in self.ENGS}
        self.csem = self.sem("coll")
        self.ccnt = 0
        self.dirty = {}
        self.semcnt = {}

    def sem(self, name):
        return self.stack.enter_context(self.nc.semaphore(name))

    def buf(self, name, dma=False, sem=None):
        b = Buf(name)
        if sem is not None:
            b.dsem = sem
        elif dma:
            b.dsem = self.sem("d_" + name)
        return b

    def _waits(self, eng, reads, writes):
        need = {}

        def add(tok):
            if tok is None:
                return
            s, v = tok
            k = id(s)
            if k not in need or need[k][1] < v:
                need[k] = (s, v)
        for b in reads:
            add(b.w)
        for b in writes:
            add(b.w)
            for r in b.rs.values():
                add(r)
        out = []
        seen = self.seen[eng]
        for k, (s, v) in need.items():
            if eng == "pe" and s is self.esem["pe"]:
                continue
            if k in self.semcnt:
                v = 16 * self.semcnt[k]
            if seen.get(k, 0) >= v:
                continue
            seen[k] = v
            out.append((s, v))
        return out

    def _commit(self, tok, reads, writes):
        k = id(tok[0])
        for b in writes:
            b.w = tok
            b.rs = {}
        for b in reads:
            b.rs[k] = tok

    def op(self, eng, fn, reads=(), writes=()):
        waits = self._waits(eng, reads, writes)
        self.ecnt[eng] += 1
        tok = (self.esem[eng], self.ecnt[eng])
        self.ops[eng].append((fn, waits, (self.esem[eng], 1)))
        self._commit(tok, reads, writes)

    def dma(self, eng, fn, dst, reads=(), extra_writes=()):
        writes = (dst,) + tuple(extra_writes)
        waits = self._waits(eng, reads, writes)
        c = self.semcnt.get(id(dst.dsem), 0) + 1
        self.semcnt[id(dst.dsem)] = c
        tok = (dst.dsem, 16 * c)
        self.ops[eng].append((fn, waits, (dst.dsem, 16)))
        self._commit(tok, reads, writes)
        self.dirty[id(dst)] = (dst, tok)

    def coll(self, fn, reads=(), writes=()):
        waits = self._waits("pool", reads, writes)
        self.ccnt += 1
        tok = (self.csem, self.ccnt)
        self.ops["pool"].append((fn, waits, (self.csem, None)))
        self._commit(tok, reads, writes)

    def barrier(self, engines=("pe", "act", "dve", "sp")):
        toks = [(self.esem[e], self.ecnt[e]) for e in engines if self.ecnt[e] > 0]
        toks += [t for (b, t) in self.dirty.values() if not b.name.startswith("wshb")]
        for e in engines:
            seen = self.seen[e]
            ws = []
            for s, v in toks:
                if seen.get(id(s), 0) >= v:
                    continue
                seen[id(s)] = v
                ws.append((s, v))
            if ws:
                self.ops[e].append((None, ws, None))
        self.dirty = {k: bt for k, bt in self.dirty.items() if bt[0].name.startswith("wshb")}

    def replay(self, eng, e):
        import os
        probe = os.environ.get("KDEBUG3") and eng == "sp"
        pstate = True
        for opi, (fn, waits, inc) in enumerate(self.ops[eng]):
            if probe and opi < 193 and opi % 8 == 0:
                try:
                    self.ops[eng][193][0](e); ok = True
                except Exception as ex:
                    ok = False
                if ok != pstate:
                    print("PROBE change at", opi, ok); pstate = ok
            for s, v in waits:
                e.wait_ge(s, v)
            if fn is None:
                continue
            try:
                ins = fn(e)
            except Exception as ex:
                import os
                if os.environ.get("KDEBUG"):
                    print("REPLAY FAIL", eng, len(self.ops[eng]), self.ops[eng].index((fn, waits, inc)), ex)
                    for t in range(2):
                        try:
                            ins = fn(e); print("retry ok"); break
                        except Exception as ex2:
                            print("retry fail", ex2)
                raise
            if inc is not None:
                if inc[1] is None:
                    ins.then_inc(inc[0])
                else:
                    ins.then_inc(inc[0], inc[1])


def build(cfg):
    nc = bass.Bass("TRN2", target_bir_lowering=False)
    D, S, T, TT, NT, KC, FC, DEPTH = cfg.D, cfg.S, cfg.T, cfg.TT, cfg.NT, cfg.KC, cfg.FC, cfg.DEPTH
    SCALE = float(HEAD) ** -0.5
    NSUB = TT // 128
    assert TT == 512
    stack = contextlib.ExitStack()
    with stack:
        P = Prog(nc, stack)

        xT = nc.dram_tensor("xT", [D, T], F32, kind="ExternalInput")
        pT = nc.dram_tensor("pT", [DEPTH * D_PLE, T], F32, kind="ExternalInput")
        gains = nc.dram_tensor("gains", [128, 6 * DEPTH * KC], F32, kind="ExternalInput")
        consts = nc.dram_tensor("consts", [128, NCB + NCF], F32, kind="ExternalInput")
        RPC = CH // FLATW
        NGRP = cfg.NGRP
        wsh = [[nc.dram_tensor(f"wsh{L}_{g}", [cfg.gncoll[g] * RPC, FLATW], F32, kind="ExternalInput") for g in range(NGRP)] for L in range(DEPTH)]
        outT = nc.dram_tensor("outT", [D, T], F32, kind="ExternalOutput")

        wshb = [[nc.dram_tensor(f"wshb{L}_{g}", [cfg.gncoll[g] * RPC, FLATW], BF16) for g in range(NGRP)] for L in range(DEPTH)]
        ws1 = [[nc.dram_tensor(f"ws1_{L}_{g}", [cfg.gncoll[g] * 4 * RPC, FLATW], BF16) for g in range(NGRP)] for L in range(DEPTH)]
        wfull = [[nc.dram_tensor(f"wfull{L}_{g}", [cfg.gflat[g] // FLATW, FLATW], BF16) for g in range(NGRP)] for L in range(DEPTH)]
        HT = nc.dram_tensor("HT", [D, T], F32)
        Y = nc.dram_tensor("Y", [D, T], F32)
        PTB = nc.dram_tensor("PTB", [DEPTH * D_PLE, T], BF16)
        NQC = cfg.NQK // 4
        XQL = nc.dram_tensor("XQL", [NQC * 512, T], BF16)
        XQG = nc.dram_tensor("XQG", [NQC * 2 * 512, T], BF16)
        HV = T // 2
        XVS = nc.dram_tensor("XVS", [T, W_SB], BF16)
        XVSG = nc.dram_tensor("XVSG", [4 * HV, W_SB], BF16)
        XVD = [nc.dram_tensor(f"XVD{g}", [T, NH_G * HEAD], BF16) for g in range(NG)]
        XVDG = [nc.dram_tensor(f"XVDG{g}", [2 * T, NH_G * HEAD], BF16) for g in range(NG)]
        GATE = nc.dram_tensor("GATE", [2 * D, T], BF16)
        OXL = nc.dram_tensor("OXL", [768, S], BF16)
        OXG = nc.dram_tensor("OXG", [6 * 2 * 128, S], BF16)

        b_X = P.buf("xT")
        b_HT = [P.buf(f"HT{i}", dma=True) for i in range(NT)]
        b_Y = [P.buf(f"Y{i}", dma=True) for i in range(NT)]
        b_XQL = P.buf("XQL", dma=True)
        b_XV = P.buf("XV", dma=True)
        b_GATE = [P.buf(f"GATE{i}", dma=True) for i in range(NT)]
        b_OXL = P.buf("OXL", dma=True)
        b_XQGc = [P.buf(f"XQG{c}") for c in range(cfg.NQK // 4)]
        b_XVGs = P.buf("XVGs")
        b_XVGd = [P.buf(f"XVGd{g}") for g in range(NG)]
        b_OXG = P.buf("OXG")
        b_PTB = P.buf("PTB", dma=True)
        b_OUT = P.buf("OUT", dma=True)
        wcast_sems = [P.sem(f"wcast{k}") for k in range(4)]
        b_wshb = [[[P.buf(f"wshb{L}_{g}_{i}", sem=wcast_sems[i % 4]) for i in range(cfg.gncoll[g])] for g in range(NGRP)] for L in range(DEPTH)]
        b_wfull = [[[P.buf(f"wf{L}_{g}_{i}") for i in range(cfg.gncoll[g])] for g in range(NGRP)] for L in range(DEPTH)]

        def sbt(name, shape, dt):
            return stack.enter_context(nc.sbuf_tensor(name, shape, dt))
        CB = sbt("CB", [128, NCB], BF16)
        CF = sbt("CF", [128, NCF], F32)
        GN = sbt("GN", [128, 6 * DEPTH * KC], F32)
        XNE = max(KC * TT, 16384)
        XN = sbt("XN", [128, XNE], BF16)
        ATE = max(FC * TT, 7 * S, (KC + 24) * TT)
        AT = sbt("AT", [128, ATE], BF16)
        WB = sbt("WB", [128, 2 * 8192 + 2 * 2048], BF16)
        FW = sbt("FW", [128, 8 * 512], F32)
        ST = sbt("ST", [128, 4 * 512], BF16)
        OST = sbt("OST", [128, 2 * 512], BF16)
        SQ = sbt("SQ", [128, 2 * TT], BF16)
        RS = sbt("RS", [128, 2 * TT], F32)
        PS = stack.enter_context(nc.psum_tensor("PS", [128, 8 * 512], F32))
        psb = [P.buf(f"ps{i}") for i in range(8)]

        def ps(i, n=512, o=0):
            return PS[:, i * 512 + o:i * 512 + o + n]

        b_CB, b_CF, b_GN = P.buf("CB", dma=True), P.buf("CF", dma=True), P.buf("GN", dma=True)
        NCHK = ATE // TT
        xn_sems = [P.sem(f"d_xn{k}") for k in range(4)]
        at_sems = [P.sem(f"d_at{k}") for k in range(4)]
        b_XNc = [P.buf(f"XN{k}", sem=xn_sems[k % 4]) for k in range(XNE // TT)]
        b_ATc = [P.buf(f"AT{k}", sem=at_sems[k % 4]) for k in range(NCHK)]
        b_RS = [P.buf("RS0"), P.buf("RS1")]
        b_SQ = [P.buf("sq0"), P.buf("sq1")]
        b_OST = [P.buf("ost0"), P.buf("ost1")]
        fwb = [P.buf(f"fw{k}", dma=True) for k in range(8)]
        b_ST = [P.buf(f"st{k}", dma=True) for k in range(4)]

        def fw(k, n=TT):
            return FW[:, k * 512:k * 512 + n]

        def st(k, n=TT):
            return ST[:, k * 512:k * 512 + n]

        def ost(k, n=TT):
            return OST[:, k * 512:k * 512 + n]

        def rs(i):
            return RS[:, i * TT:(i + 1) * TT]

        def xn(kc):
            return XN[:, kc * TT:(kc + 1) * TT]

        def at(kc):
            return AT[:, kc * TT:(kc + 1) * TT]

        ones_bf = CB[:, CB_ONES:CB_ONES + 128]
        tri_bf = CB[:, CB_TRI:CB_TRI + 128]
        li_bf = CB[:, CB_LI:CB_LI + 128]
        sel_bf = [CB[:, CB_SEL:CB_SEL + 128], CB[:, CB_SEL + 128:CB_SEL + 256]]
        one_col = CF[:, CF_ONE:CF_ONE + 1]
        eps_col = CF[:, CF_EPS:CF_EPS + 1]
        rank_holder = {}

        def gain(kind, L, kc):
            o = (kind * DEPTH + L) * KC + kc
            return GN[:, o:o + 1]

        P.dma("pool", lambda e: e.dma_start(out=CB[:, :], in_=consts[:, 0:NCB]), b_CB)
        P.dma("sp", lambda e: e.dma_start(out=CF[:, :], in_=consts[:, NCB:NCB + NCF]), b_CF)
        P.dma("sp", lambda e: e.dma_start(out=GN[:, :], in_=gains[:, :]), b_GN)
        for L_ in range(DEPTH):
            P.dma("pool", lambda e, L_=L_: e.dma_start(out=PTB[L_ * D_PLE:(L_ + 1) * D_PLE, :], in_=pT[L_ * D_PLE:(L_ + 1) * D_PLE, :]), b_PTB)

        QUADS = [[0, 1, 2, 3], [4, 5, 6, 7]]
        XPAIRS = [[0, 4], [1, 5], [2, 6], [3, 7]]
        PAIRS = [[0, 1], [2, 3], [4, 5], [6, 7]]

        def wcast(L, g, i):
            r0 = i * RPC
            src_ = wsh[L][g][r0:r0 + RPC, :]
            dstb = wshb[L][g][r0:r0 + RPC, :]
            P.dma("pool", lambda e: e.dma_start(out=dstb, in_=src_), b_wshb[L][g][i])

        def wgather(L, g, i):
            r0 = i * RPC
            dstb = wshb[L][g][r0:r0 + RPC, :]
            bsh = b_wshb[L][g][i]
            s1 = ws1[L][g][i * 4 * RPC:(i + 1) * 4 * RPC, :]
            b1 = P.buf("s1")
            P.coll(lambda e: e.collective_compute("AllGather", ALU.bypass, replica_groups=QUADS,
                                                  ins=[dstb], outs=[s1]), reads=[bsh], writes=[b1])
            wf = wfull[L][g][i * 8 * RPC:(i + 1) * 8 * RPC, :]
            P.coll(lambda e: e.collective_compute("AllGather", ALU.bypass, replica_groups=XPAIRS,
                                                  ins=[s1], outs=[wf]), reads=[b1], writes=[b_wfull[L][g][i]])

        wq = [(L, g, i) for L in range(DEPTH) for g in range(NGRP) for i in range(cfg.gncoll[g])]
        ncoll_layer = sum(cfg.gncoll)
        wq_pos = [0]
        wc_pos = [0]
        CAST_AHEAD = 3

        def _step():
            while wc_pos[0] < min(len(wq), wq_pos[0] + 1 + CAST_AHEAD):
                wcast(*wq[wc_pos[0]])
                wc_pos[0] += 1
            wgather(*wq[wq_pos[0]])
            wq_pos[0] += 1

        def pump_weights(n):
            for _ in range(n):
                if wq_pos[0] < len(wq):
                    _step()

        def pump_layer(L):
            while wq_pos[0] < len(wq) and wq[wq_pos[0]][0] <= L:
                _step()

        def wblock(L, name, b):
            sg = cfg.seg[name]
            be = sg["be"]
            off = sg["off"] + b * 128 * be
            g = sg["grp"]
            v = wfull[L][g].ap().rearrange("r (a c) -> (r a) c", c=128)
            r0 = off // 128
            ap = v[r0:r0 + be, :].rearrange("(p a) c -> p (a c)", p=128)
            i0 = off // (8 * CH)
            i1 = (off + 128 * be - 1) // (8 * CH)
            return ap, [b_wfull[L][g][i] for i in range(i0, i1 + 1)], be

        class Slots:
            def __init__(self, n, size, base, name):
                self.n, self.size, self.base = n, size, base
                self.bufs = [P.buf(f"{name}{k}", dma=True) for k in range(n)]
                self.k = 0

            def load(self, L, seg, b):
                ap, deps, be = wblock(L, seg, b)
                assert be <= self.size
                k = self.k
                self.k = (k + 1) % self.n
                o = self.base + k * self.size
                dst = WB[:, o:o + be]
                P.dma("sp", lambda e: e.dma_start(out=dst, in_=ap), self.bufs[k], reads=deps)
                return dst, self.bufs[k]

        slots_b = Slots(2, 8192, 0, "wbig")
        slots_s = Slots(2, 2048, 16384, "wsml")
        bank_toggle = [0]

        def next_bank():
            b = bank_toggle[0] % 2
            bank_toggle[0] += 1
            return b

        def sumsq_accum(src_ap, src_bufs, sqk, first, last):
            sq = SQ[:, sqk * TT:(sqk + 1) * TT]
            P.op("act", lambda e: e.activation(out=sq, in_=src_ap, func=AF.Square), reads=src_bufs, writes=[b_SQ[sqk]])
            P.op("pe", lambda e: e.matmul(ps(7), lhsT=ones_bf, rhs=sq, start=first, stop=last),
                 reads=[b_SQ[sqk], b_CB], writes=[psb[7]])

        def rstd_from_sumsq(ri):
            P.op("act", lambda e: e.activation(out=rs(ri), in_=ps(7), func=AF.Sqrt, bias=eps_col, scale=1.0 / D),
                 reads=[psb[7], b_CF], writes=[b_RS[ri]])
            P.op("dve", lambda e: e.reciprocal(out=rs(ri), in_=rs(ri)), reads=[b_RS[ri]], writes=[b_RS[ri]])

        def hview(t, i):
            return t.ap().rearrange("(kc p) t -> p kc t", p=128)[:, :, i * TT:(i + 1) * TT]

        def norm_pass(L, i, h_src, hbuf_src, resid, g_post, pre, g_pre, h_dst=None, b_dst=None):
            hs = hview(h_src, i)
            yv = hview(Y, i)
            hd = hview(h_dst if h_dst is not None else HT, i)
            bd = b_dst if b_dst is not None else b_HT[i]
            HB = AT[:, 0:KC * 2 * TT].bitcast(F32)

            def hbk(kc):
                return HB[:, kc * TT:(kc + 1) * TT]

            def hbb(kc):
                return [b_ATc[2 * kc], b_ATc[2 * kc + 1]]
            G = min(4, KC)
            for g0 in range(0, KC, G):
                bl = [bb for kc in range(g0, g0 + G) for bb in hbb(kc)]
                P.dma("sp", lambda e, g0=g0: e.dma_start(out=HB[:, g0 * TT:(g0 + G) * TT].rearrange("p (k t) -> p k t", k=G), in_=hs[:, g0:g0 + G, :]),
                      bl[0], reads=[hbuf_src], extra_writes=bl[1:])
            if resid:
                YG = min(4, KC)
                for yg in range(KC // YG):
                    s = yg % 2
                    P.dma("sp", lambda e, yg=yg, s=s: e.dma_start(
                        out=FW[:, 4 * s * 512:(4 * s + YG) * 512].rearrange("p (k t) -> p k t", k=YG), in_=yv[:, yg * YG:(yg + 1) * YG, :]),
                        fwb[4 * s], reads=[b_Y[i]], extra_writes=fwb[4 * s + 1:4 * s + YG])
                    for j in range(YG):
                        kc = yg * YG + j
                        yt, yb = fw(4 * s + j), fwb[4 * s + j]
                        gp = gain(g_post, L, kc)
                        P.op("dve", lambda e, yt=yt, gp=gp: e.scalar_tensor_tensor(
                            out=yt, in0=yt, scalar=gp, in1=rs(0), op0=ALU.mult, op1=ALU.mult),
                            reads=[yb, b_RS[0], b_GN], writes=[yb])
                        P.op("dve", lambda e, yt=yt, kc=kc: e.tensor_tensor(out=hbk(kc), in0=yt, in1=hbk(kc), op=ALU.add),
                             reads=[yb] + hbb(kc), writes=hbb(kc))
                        if pre:
                            sumsq_accum(hbk(kc), hbb(kc), kc % 2, kc == 0, kc == KC - 1)
                        if (kc + 1) % G == 0:
                            g0 = kc + 1 - G
                            bl = [bb for k2 in range(g0, g0 + G) for bb in hbb(k2)]
                            P.dma("act", lambda e, g0=g0: e.dma_start(out=hd[:, g0:g0 + G, :], in_=HB[:, g0 * TT:(g0 + G) * TT].rearrange("p (k t) -> p k t", k=G)),
                                  bd, reads=bl)
            elif pre:
                for kc in range(KC):
                    sumsq_accum(hbk(kc), hbb(kc), kc % 2, kc == 0, kc == KC - 1)
            if pre:
                rstd_from_sumsq(1)
                for kc in range(KC):
                    g2 = gain(g_pre, L, kc)
                    P.op("dve", lambda e, kc=kc, g2=g2: e.scalar_tensor_tensor(
                        out=xn(kc), in0=hbk(kc), scalar=g2, in1=rs(1), op0=ALU.mult, op1=ALU.mult),
                        reads=hbb(kc) + [b_RS[1], b_GN], writes=[b_XNc[kc]])

        def dense_fm(L, segs, rhs_of_kc, rhs_buf_of_kc, epilogue, sub_cols=128):
            sg0 = cfg.seg[segs[0]]
            NB = sg0["NB"]
            for b in range(sg0["nblk"]):
                wts = [slots_b.load(L, s, b) for s in segs[:1]]
                for sub in range(NB // sub_cols):
                    bank = next_bank()
                    kbase = 0
                    for si, s in enumerate(segs):
                        if si > 0 and sub == 0:
                            wts.append(slots_b.load(L, s, b))
                        wt, wbuf = wts[si]
                        kcn = cfg.seg[s]["kc"]
                        for kc in range(kcn):
                            lhsT = wt[:, kc * NB + sub * sub_cols: kc * NB + (sub + 1) * sub_cols]
                            first = (si == 0 and kc == 0)
                            last = (si == len(segs) - 1 and kc == kcn - 1)
                            P.op("pe", lambda e, lhsT=lhsT, kk=kbase + kc, bank=bank, first=first, last=last: e.matmul(
                                ps(bank), lhsT=lhsT, rhs=rhs_of_kc(kk), start=first, stop=last),
                                reads=[wbuf, rhs_buf_of_kc(kbase + kc)], writes=[psb[bank]])
                        kbase += kcn
                    epilogue(b * (NB // sub_cols) + sub, bank)

        st_k = [0]

        def next_st():
            k = st_k[0] % 4
            st_k[0] += 1
            return k


        def select_pieces(npieces, a_piece, b_piece, a_bufs, b_bufs, dst_piece, dst_bufs, banks, width=512):
            for pc in range(npieces):
                bank = banks[pc % len(banks)]
                P.op("pe", lambda e, pc=pc, bank=bank: e.matmul(ps(bank, width), lhsT=sel_bf[0], rhs=a_piece(pc), start=True, stop=False),
                     reads=[b_CB] + a_bufs(pc), writes=[psb[bank]])
                P.op("pe", lambda e, pc=pc, bank=bank: e.matmul(ps(bank, width), lhsT=sel_bf[1], rhs=b_piece(pc), start=False, stop=True),
                     reads=[b_CB] + b_bufs(pc), writes=[psb[bank]])
                if pc % 2 == 0:
                    P.op("act", lambda e, pc=pc, bank=bank: e.copy(out=dst_piece(pc), in_=ps(bank, width)), reads=[psb[bank]], writes=dst_bufs(pc))
                else:
                    P.op("dve", lambda e, pc=pc, bank=bank: e.tensor_copy(out=dst_piece(pc), in_=ps(bank, width)), reads=[psb[bank]], writes=dst_bufs(pc))

        if getattr(cfg, "debug", False):
            dbgT = nc.dram_tensor("dbg_T", [128, 5 * 512], F32, kind="ExternalOutput")
            dbgB = nc.dram_tensor("dbg_B", [128, 5 * 512], BF16, kind="ExternalOutput")
            dbgH1 = nc.dram_tensor("dbg_H1", [D, T], F32, kind="ExternalOutput")
            dbgH2 = nc.dram_tensor("dbg_H2", [D, T], F32, kind="ExternalOutput")
            dbgY1 = nc.dram_tensor("dbg_Y1", [D, T], F32, kind="ExternalOutput")
            dbgY2 = nc.dram_tensor("dbg_Y2", [D, T], F32, kind="ExternalOutput")
            b_dbgT = P.buf("dbgT", dma=True)

        def o_coll(c):
            P.coll(lambda e, c=c: e.collective_compute("AllGather", ALU.bypass, replica_groups=PAIRS,
                                                       ins=[OXL[c * 128:(c + 1) * 128, :]], outs=[OXG[c * 256:(c + 1) * 256, :]]),
                   reads=[b_OXL], writes=[b_OXG])

        def attention():
            nhb = T // 128
            QT, KT, VS = AT[:, 0:S], AT[:, S:2 * S], AT[:, 2 * S:3 * S]
            VSv = VS.rearrange("p (b d) -> p b d", d=128)
            bq = b_ATc[0:S // TT]
            bk = b_ATc[S // TT:2 * S // TT]
            bv = b_ATc[2 * S // TT:3 * S // TT]
            xqg3 = XQG.ap().rearrange("(a b) t -> a b t", b=1024)
            xqg4 = XQG.ap().rearrange("(a b c) t -> a b c t", b=4, c=256)
            xvs_v = XVSG.ap().rearrange("(q b p) (r c) -> q p b r c", p=128, b=HV // 128, c=512)
            def sb_tile(lh, tq):
                J = 4 * tq + 4
                qcols = QT[:, tq * 512:(tq + 1) * 512]

                def stage_a(n):
                    j = J - 1 - n
                    jj = j - 4 * tq
                    z = n % 2
                    P.op("pe", lambda e: e.matmul(ps(z), lhsT=KT[:, j * 128:(j + 1) * 128], rhs=qcols, start=True, stop=True),
                         reads=[bk[0], bq[0]], writes=[psb[z]])
                    P.op("act", lambda e: e.activation(out=fw(z), in_=ps(z), func=AF.Exp, scale=SCALE), reads=[psb[z]], writes=[fwb[z]])
                    P.op("act", lambda e: e.activation(out=fw(2 + z), in_=fw(z), func=AF.Ln, bias=one_col, scale=1.0),
                         reads=[fwb[z], b_CF], writes=[fwb[2 + z]])

                def stage_a2(n):
                    j = J - 1 - n
                    jj = j - 4 * tq
                    z = n % 2
                    if jj >= 0:
                        m01 = CB[:, CB_M01 + jj * 512:CB_M01 + (jj + 1) * 512]
                        P.op("dve", lambda e: e.scalar_tensor_tensor(out=st(z), in0=fw(2 + z), scalar=-1.0, in1=m01, op0=ALU.mult, op1=ALU.mult),
                             reads=[fwb[2 + z], b_CB], writes=[b_ST[z]])
                    else:
                        P.op("dve", lambda e: e.tensor_scalar(out=st(z), in0=fw(2 + z), scalar1=-1.0, scalar2=None, op0=ALU.mult),
                             reads=[fwb[2 + z]], writes=[b_ST[z]])
                    P.op("dve", lambda e: e.scalar_tensor_tensor(out=fw(4 + z), in0=ps(z), scalar=SCALE, in1=fw(2 + z), op0=ALU.mult, op1=ALU.subtract),
                         reads=[psb[z], fwb[2 + z]], writes=[fwb[4 + z]])

                def o_mm(n):
                    j = J - 1 - n
                    z = n % 2
                    P.op("pe", lambda e: e.matmul(ps(3), lhsT=VSv[:, j, :], rhs=st(2 + z), start=(n == 0), stop=(n == J - 1)),
                         reads=[bv[0], b_ST[2 + z]], writes=[psb[3]])

                def stage_b(n):
                    j = J - 1 - n
                    jj = j - 4 * tq
                    z = n % 2
                    P.op("pe", lambda e: e.matmul(ps(2), lhsT=tri_bf, rhs=st(z), start=(n == 0), stop=(n == J - 1)),
                         reads=[b_ST[z], b_CB], writes=[psb[2]])
                    if n > 0:
                        o_mm(n - 1)
                    P.op("dve", lambda e: e.tensor_tensor(out=fw(6 + z), in0=fw(4 + z), in1=ps(2), op=ALU.add),
                         reads=[fwb[4 + z], psb[2]], writes=[fwb[6 + z]])
                    if jj >= 0:
                        mneg = CB[:, CB_MNEG + jj * 512:CB_MNEG + (jj + 1) * 512]
                        P.op("dve", lambda e: e.tensor_tensor(out=fw(6 + z), in0=fw(6 + z), in1=mneg, op=ALU.add),
                             reads=[fwb[6 + z], b_CB], writes=[fwb[6 + z]])

                def stage_b2(n):
                    z = n % 2
                    P.op("act", lambda e: e.activation(out=st(2 + z), in_=fw(6 + z), func=AF.Exp), reads=[fwb[6 + z]], writes=[b_ST[2 + z]])
                    if n < J - 1:
                        P.op("pe", lambda e: e.matmul(ps(2), lhsT=li_bf, rhs=st(z), start=False, stop=False),
                             reads=[b_ST[z], b_CB], writes=[psb[2]])

                stage_a(0)
                stage_a2(0)
                for n in range(J):
                    if n + 1 < J:
                        stage_a(n + 1)
                    stage_b(n)
                    if n + 1 < J:
                        stage_a2(n + 1)
                    stage_b2(n)
                    if getattr(cfg, "debug", False) and lh == 0 and tq == 0 and n == 0:
                        for kk, src_k in enumerate((0, 2, 4, 6)):
                            P.dma("sp", lambda e, kk=kk, src_k=src_k: e.dma_start(out=dbgT[:, kk * 512:(kk + 1) * 512], in_=fw(src_k)), b_dbgT, reads=[fwb[src_k]])
                        for kk, src_k in enumerate((0, 2)):
                            P.dma("sp", lambda e, kk=kk, src_k=src_k: e.dma_start(out=dbgB[:, kk * 512:(kk + 1) * 512], in_=st(src_k)), b_dbgT, reads=[b_ST[src_k]])
                        P.dma("sp", lambda e: e.dma_start(out=dbgB[:, 1024:1536], in_=QT[:, 0:512]), b_dbgT, reads=[bq[0]])
                        P.dma("sp", lambda e: e.dma_start(out=dbgB[:, 1536:2048], in_=KT[:, 0:512]), b_dbgT, reads=[bk[0]])
                        P.dma("sp", lambda e: e.dma_start(out=dbgB[:, 2048:2560], in_=VS[:, 0:512]), b_dbgT, reads=[bv[0]])
                o_mm(J - 1)
                ok = tq % 2
                P.op("act", lambda e, ok=ok: e.copy(out=ost(ok), in_=ps(3)), reads=[psb[3]], writes=[b_OST[ok]])
                P.dma("act", lambda e, ok=ok, lh=lh, tq=tq: e.dma_start(out=OXL[lh * 128:(lh + 1) * 128, tq * 512:(tq + 1) * 512], in_=ost(ok)),
                      b_OXL, reads=[b_OST[ok]])


            STA, STB = AT[:, 5 * S:6 * S], AT[:, 6 * S:7 * S]
            nbs = S // TT
            bsa, bsb = b_ATc[5 * nbs:6 * nbs], b_ATc[6 * nbs:7 * nbs]
            STAv = STA.rearrange("p (b d) -> p b d", d=128)
            STBv = STB.rearrange("p (b d) -> p b d", d=128)
            xvs3 = XVSG.ap().rearrange("(q b p) c -> q p b c", p=128, b=HV // 128)

            def load_fm(rows_a, rows_b, dst, dbufs, banks):
                for rp in range(2):
                    ra, rb = rows_a(rp), rows_b(rp)
                    P.dma("sp", lambda e, rp=rp, ra=ra: e.dma_start(out=STA[:, rp * T:(rp + 1) * T], in_=XQG[ra:ra + 128, :]),
                          bsa[0], reads=[b_XQGc[ra // 1024]], extra_writes=bsa[1:])
                    P.dma("sp", lambda e, rp=rp, rb=rb: e.dma_start(out=STB[:, rp * T:(rp + 1) * T], in_=XQG[rb:rb + 128, :]),
                          bsb[0], reads=[b_XQGc[rb // 1024]], extra_writes=bsb[1:])
                select_pieces(S // 512, lambda pc: STA[:, pc * 512:(pc + 1) * 512], lambda pc: STB[:, pc * 512:(pc + 1) * 512],
                              lambda pc: [bsa[0]], lambda pc: [bsb[0]], lambda pc: dst[:, pc * 512:(pc + 1) * 512],
                              lambda pc: [dbufs[0]], banks)

            for lh in range(4):
                load_fm(lambda rp: rp * 512 + lh * 128, lambda rp: 1024 + rp * 512 + lh * 128, QT, bq, (4, 5))
                load_fm(lambda rp: 2048 + rp * 512 + lh * 128, lambda rp: 3072 + rp * 512 + lh * 128, KT, bk, (4, 5))
                for rp in range(2):
                    for c in range(2):
                        blk0 = rp * nhb + c * (HV // 128)
                        P.dma("sp", lambda e, lh=lh, rp=rp, c=c, blk0=blk0: e.dma_start(
                            out=STAv[:, blk0:blk0 + HV // 128, :], in_=xvs3[c * 2 + rp, :, :, lh * 128:(lh + 1) * 128]),
                            bsa[0], reads=[b_XVGs], extra_writes=bsa[1:])
                        P.dma("sp", lambda e, lh=lh, rp=rp, c=c, blk0=blk0: e.dma_start(
                            out=STBv[:, blk0:blk0 + HV // 128, :], in_=xvs3[c * 2 + rp, :, :, (4 + lh) * 128:(5 + lh) * 128]),
                            bsb[0], reads=[b_XVGs], extra_writes=bsb[1:])
                select_pieces(S // 512, lambda pc: STA[:, pc * 512:(pc + 1) * 512], lambda pc: STB[:, pc * 512:(pc + 1) * 512],
                              lambda pc: [bsa[0]], lambda pc: [bsb[0]], lambda pc: VS[:, pc * 512:(pc + 1) * 512],
                              lambda pc: [bv[0]], (4, 5))
                for tq in range(S // 512):
                    sb_tile(lh, tq)
            for c in range(4):
                o_coll(c)
            P.barrier()
            QN, KN, QP, KP, VD = (AT[:, k * S:(k + 1) * S] for k in range(5))
            nb_ = S // TT
            bqn, bkn, bqp, bkp, bvd = (b_ATc[k * nb_:(k + 1) * nb_] for k in range(5))
            ACC = XN[:, 0:16384].bitcast(F32)
            ACCN, ACCD = ACC[:, 0:S], ACC[:, S:2 * S]
            bacc = b_XNc[0]
            for lh2 in range(2):
                P.op("dve", lambda e: e.memset(ACC[:, :], 0.0), writes=[bacc])
                for g in range(NG):
                    r = DILS[g]
                    UB = S // (128 * r)
                    UBh = UB // 2
                    load_fm(lambda rp: (4 + g) * 1024 + rp * 512 + lh2 * 128, lambda rp: (4 + g) * 1024 + rp * 512 + (2 + lh2) * 128, QN, bqn, (0, 1))
                    load_fm(lambda rp: (7 + g) * 1024 + rp * 512 + lh2 * 128, lambda rp: (7 + g) * 1024 + rp * 512 + (2 + lh2) * 128, KN, bkn, (0, 1))
                    vsrc = XVDG[g].ap().rearrange("(q rho b p) c -> q p rho b c", p=128, rho=r, b=UBh)
                    sta_v = STA.rearrange("p (rho ub d) -> p rho ub d", rho=r, ub=UB)
                    stb_v = STB.rearrange("p (rho ub d) -> p rho ub d", rho=r, ub=UB)
                    for rp in range(2):
                        for rho in range(r):
                            P.dma("sp", lambda e, rp=rp, lh2=lh2, vsrc=vsrc, sta_v=sta_v, rho=rho, UBh=UBh: e.dma_start(
                                out=sta_v[:, rho, rp * UBh:(rp + 1) * UBh, :], in_=vsrc[rp, :, rho, :, lh2 * 128:(lh2 + 1) * 128]),
                                bsa[0], reads=[b_XVGd[g]], extra_writes=bsa[1:])
                            P.dma("sp", lambda e, rp=rp, lh2=lh2, vsrc=vsrc, stb_v=stb_v, rho=rho, UBh=UBh: e.dma_start(
                                out=stb_v[:, rho, rp * UBh:(rp + 1) * UBh, :], in_=vsrc[rp, :, rho, :, (2 + lh2) * 128:(3 + lh2) * 128]),
                                bsb[0], reads=[b_XVGd[g]], extra_writes=bsb[1:])
                    select_pieces(S // 512, lambda pc: STA[:, pc * 512:(pc + 1) * 512], lambda pc: STB[:, pc * 512:(pc + 1) * 512],
                                  lambda pc: [bsa[0]], lambda pc: [bsb[0]], lambda pc: VD[:, pc * 512:(pc + 1) * 512],
                                  lambda pc: [bvd[0]], (0, 1))
                    if r == 1:
                        Qs, Ks, bqs, bks = QN, KN, bqn, bkn
                    else:
                        P.op("act", lambda e, r=r: e.copy(out=QP.rearrange("p (r u) -> p r u", r=r), in_=QN.rearrange("p (u r) -> p r u", r=r)),
                             reads=[bqn[0]], writes=bqp)
                        P.op("dve", lambda e, r=r: e.tensor_copy(out=KP.rearrange("p (r u) -> p r u", r=r), in_=KN.rearrange("p (u r) -> p r u", r=r)),
                             reads=[bkn[0]], writes=bkp)
                        Qs, Ks, bqs, bks = QP, KP, bqp, bkp
                    VDv = VD.rearrange("p (b d) -> p b d", d=128)
                    bias_o = CF_DIL + ((g * 2 + lh2) * 2) * 128
                    accn_v = ACCN.rearrange("p (u r) -> p r u", r=r)
                    accd_v = ACCD.rearrange("p (u r) -> p r u", r=r)
                    qbs = [(rho, ub) for rho in range(r) for ub in range(UB)]

                    def d_stage1(qi, Ks=Ks, Qs=Qs, bks=bks, bqs=bqs, UB=UB, bias_o=bias_o):
                        rho, ub = qbs[qi]
                        blk = rho * UB + ub
                        kbs = [(blk, 0)] + ([(blk - 1, 1)] if ub > 0 else [])
                        for ki, (kb, kind) in enumerate(kbs):
                            fk = (2 * qi + ki) % 4
                            sps = ps(fk, 128)
                            P.op("pe", lambda e, kb=kb, blk=blk, sps=sps: e.matmul(
                                sps, lhsT=Ks[:, kb * 128:(kb + 1) * 128], rhs=Qs[:, blk * 128:(blk + 1) * 128], start=True, stop=True),
                                reads=[bks[0], bqs[0]], writes=[psb[fk]])
                            bias = CF[:, bias_o + kind * 128: bias_o + (kind + 1) * 128]
                            P.op("dve", lambda e, sps=sps, bias=bias, fk=fk: e.scalar_tensor_tensor(
                                out=fw(fk, 128), in0=sps, scalar=SCALE, in1=bias, op0=ALU.mult, op1=ALU.add),
                                reads=[psb[fk], b_CF], writes=[fwb[fk]])
                            P.op("act", lambda e, fk=fk: e.activation(out=st(fk, 128), in_=fw(fk, 128), func=AF.Exp),
                                 reads=[fwb[fk]], writes=[b_ST[fk]])

                    def d_stage2(qi, UB=UB, accn_v=accn_v, accd_v=accd_v):
                        rho, ub = qbs[qi]
                        blk = rho * UB + ub
                        kbs = [(blk, 0)] + ([(blk - 1, 1)] if ub > 0 else [])
                        nk = len(kbs)
                        bn, bd_ = 4 + qi % 2, 6 + qi % 2
                        for ki, (kb, kind) in enumerate(kbs):
                            fk = (2 * qi + ki) % 4
                            P.op("pe", lambda e, kb=kb, fk=fk, ki=ki: e.matmul(
                                ps(bn, 128), lhsT=VDv[:, kb, :], rhs=st(fk, 128), start=(ki == 0), stop=(ki == nk - 1)),
                                reads=[bvd[0], b_ST[fk]], writes=[psb[bn]])
                            P.op("pe", lambda e, fk=fk, ki=ki: e.matmul(
                                ps(bd_, 128), lhsT=ones_bf, rhs=st(fk, 128), start=(ki == 0), stop=(ki == nk - 1)),
                                reads=[b_CB, b_ST[fk]], writes=[psb[bd_]])
                        an = accn_v[:, rho, ub * 128:(ub + 1) * 128]
                        ad = accd_v[:, rho, ub * 128:(ub + 1) * 128]
                        P.op("dve", lambda e: e.tensor_tensor(out=an, in0=an, in1=ps(bn, 128), op=ALU.add),
                             reads=[psb[bn], bacc], writes=[bacc])
                        P.op("dve", lambda e: e.tensor_tensor(out=ad, in0=ad, in1=ps(bd_, 128), op=ALU.add),
                             reads=[psb[bd_], bacc], writes=[bacc])

                    d_stage1(0)
                    for qi in range(len(qbs)):
                        if qi + 1 < len(qbs):
                            d_stage1(qi + 1)
                        d_stage2(qi)
                for pc in range(S // 512):
                    k = 4 + pc % 2
                    ok = pc % 2
                    P.op("dve", lambda e, pc=pc, k=k: e.reciprocal(out=fw(k), in_=ACCD[:, pc * 512:(pc + 1) * 512]), reads=[bacc], writes=[fwb[k]])
                    P.op("dve", lambda e, pc=pc, k=k, ok=ok: e.tensor_tensor(out=ost(ok), in0=ACCN[:, pc * 512:(pc + 1) * 512], in1=fw(k), op=ALU.mult),
                         reads=[bacc, fwb[k]], writes=[b_OST[ok]])
                    P.dma("act", lambda e, pc=pc, ok=ok, lh2=lh2: e.dma_start(
                        out=OXL[512 + lh2 * 128:512 + (lh2 + 1) * 128, pc * 512:(pc + 1) * 512], in_=ost(ok)),
                        b_OXL, reads=[b_OST[ok]])

        pump_layer(0)
        for L in range(DEPTH):
            for i in range(NT):
                if L == 0:
                    norm_pass(L, i, xT, b_X, False, 0, True, 0)
                else:
                    norm_pass(L, i, HT, b_HT[i], False, 0, True, 0)

                def ep_qk(c, bank, i=i):
                    k = next_st()
                    if c % 2 == 0:
                        P.op("act", lambda e: e.copy(out=st(k), in_=ps(bank)), reads=[psb[bank]], writes=[b_ST[k]])
                    else:
                        P.op("dve", lambda e: e.tensor_copy(out=st(k), in_=ps(bank)), reads=[psb[bank]], writes=[b_ST[k]])
                    P.dma("act", lambda e: e.dma_start(out=XQL[c * 128:(c + 1) * 128, i * TT:(i + 1) * TT], in_=st(k)),
                          b_XQL, reads=[b_ST[k]])
                dense_fm(L, ["in_qk"], xn, lambda kc: b_XNc[kc], ep_qk)

                def ep_gate(c, bank, i=i):
                    k = next_st()
                    P.op("act", lambda e: e.activation(out=st(k), in_=ps(bank), func=AF.Sigmoid), reads=[psb[bank]], writes=[b_ST[k]])
                    P.dma("act", lambda e: e.dma_start(out=GATE[c * 128:(c + 1) * 128, i * TT:(i + 1) * TT], in_=st(k)),
                          b_GATE[i], reads=[b_ST[k]])
                dense_fm(L, ["in_gate"], xn, lambda kc: b_XNc[kc], ep_gate)

                sg = cfg.seg["in_v"]
                VW = W_SB + W_DIL
                VST = AT[:, 0:NSUB * VW].rearrange("p (s c) -> p s c", s=NSUB)
                b_VST = b_ATc[0:(NSUB * VW + TT - 1) // TT]
                for b in range(sg["nblk"]):
                    wt, wbuf = slots_b.load(L, "in_v", b)
                    r = 1 if b < 4 else DILS[(b - 4) // 2]
                    for sbk in range(NSUB):
                        bank = next_bank()
                        if r == 1:
                            cols = lambda kc, sbk=sbk: XN[:, kc * TT + sbk * 128: kc * TT + (sbk + 1) * 128]
                        elif r == 4:
                            cols = lambda kc, sbk=sbk: xn(kc).rearrange("p (u r) -> p r u", r=4)[:, sbk, :]
                        else:
                            cols = lambda kc, sbk=sbk: xn(kc).rearrange("p (u r) -> p r u", r=4)[:, sbk, :]
                        for kc in range(KC):
                            P.op("pe", lambda e, kc=kc, cols=cols, bank=bank, wt=wt: e.matmul(
                                ps(bank, 256), lhsT=cols(kc), rhs=wt[:, kc * 256:(kc + 1) * 256], start=(kc == 0), stop=(kc == KC - 1)),
                                reads=[wbuf, b_XNc[kc]], writes=[psb[bank]])
                        dstv = VST[:, sbk, b * 256:(b + 1) * 256]
                        if (b + sbk) % 2 == 0:
                            P.op("act", lambda e, dstv=dstv, bank=bank: e.copy(out=dstv, in_=ps(bank, 256)), reads=[psb[bank]], writes=[b_VST[0]])
                        else:
                            P.op("dve", lambda e, dstv=dstv, bank=bank: e.tensor_copy(out=dstv, in_=ps(bank, 256)), reads=[psb[bank]], writes=[b_VST[0]])
                for sbk in range(NSUB):
                    r0 = i * TT + sbk * 128
                    P.dma("act", lambda e, sbk=sbk, r0=r0: e.dma_start(out=XVS[r0:r0 + 128, :], in_=VST[:, sbk, 0:W_SB]), b_XV, reads=[b_VST[0]])
                    P.dma("act", lambda e, sbk=sbk, r0=r0: e.dma_start(out=XVD[0][r0:r0 + 128, :], in_=VST[:, sbk, W_SB:W_SB + 512]), b_XV, reads=[b_VST[0]])
                    rr = sbk * (T // 4) + i * (TT // 4)
                    P.dma("act", lambda e, sbk=sbk, rr=rr: e.dma_start(out=XVD[1][rr:rr + 128, :], in_=VST[:, sbk, W_SB + 512:W_SB + 1024]), b_XV, reads=[b_VST[0]])
                    for m in range(4):
                        rr2 = (4 * m + sbk) * (T // 16) + i * (TT // 16)
                        P.dma("act", lambda e, sbk=sbk, m=m, rr2=rr2: e.dma_start(
                            out=XVD[2][rr2:rr2 + TT // 16, :], in_=AT[m:128:4, sbk * VW + W_SB + 1024: sbk * VW + W_SB + 1536]),
                            b_XV, reads=[b_VST[0]])
                for bb in b_VST[1:]:
                    bb.w = b_VST[0].w
                    bb.rs = dict(b_VST[0].rs)

            def xq_coll(c):
                P.coll(lambda e, c=c: e.collective_compute("AllGather", ALU.bypass, replica_groups=PAIRS,
                                                           ins=[XQL[c * 512:(c + 1) * 512, :]], outs=[XQG[c * 1024:(c + 1) * 1024, :]]),
                       reads=[b_XQL], writes=[b_XQGc[c]])
            for c in range(4):
                xq_coll(c)
            for c in range(2):
                P.coll(lambda e, c=c: e.collective_compute("AllGather", ALU.bypass, replica_groups=PAIRS,
                                                           ins=[XVS[c * HV:(c + 1) * HV, :]], outs=[XVSG[c * 2 * HV:(c + 1) * 2 * HV, :]]),
                       reads=[b_XV], writes=[b_XVGs])
            for c in range(4, NQC):
                xq_coll(c)
            for g in range(NG):
                P.coll(lambda e, g=g: e.collective_compute("AllGather", ALU.bypass, replica_groups=PAIRS,
                                                           ins=[XVD[g][:, :]], outs=[XVDG[g][:, :]]),
                       reads=[b_XV], writes=[b_XVGd[g]])
            if L + 1 < DEPTH:
                pump_weights(ncoll_layer // 3)
            P.barrier()

            attention()
            P.barrier()
            for c in range(4, 6):
                o_coll(c)
            if L + 1 < DEPTH:
                pump_layer(L + 1)

            for i in range(NT):
                for kc in range(12):
                    rblk = (kc % 4) if kc < 8 else (4 + (kc - 8) % 2)
                    rp = (kc // 4) if kc < 8 else ((kc - 8) // 2)
                    row0 = (rblk * 2 + rp) * 128
                    P.dma("sp", lambda e, kc=kc, row0=row0, i=i: e.dma_start(out=at(KC + kc), in_=OXG[row0:row0 + 128, i * TT:(i + 1) * TT]),
                          b_ATc[KC + kc], reads=[b_OXG])
                    P.dma("sp", lambda e, kc=kc, row0=row0, i=i: e.dma_start(out=at(KC + 12 + kc), in_=OXG[row0:row0 + 128, T + i * TT:T + (i + 1) * TT]),
                          b_ATc[KC + 12 + kc], reads=[b_OXG])
                select_pieces(12, lambda pc: at(KC + pc), lambda pc: at(KC + 12 + pc), lambda pc: [b_ATc[KC + pc]], lambda pc: [b_ATc[KC + 12 + pc]],
                              lambda pc: xn(pc), lambda pc: [b_XNc[pc]], (0, 1))
                gv = GATE.ap().rearrange("(c p) t -> p c t", p=128)
                for b in range(cfg.seg["p_sb"]["nblk"]):
                    wsb_t, wsb_b = slots_s.load(L, "p_sb", b)
                    wd_t, wd_b = slots_s.load(L, "p_dil", b)
                    for sub in range(2):
                        oc = b * 2 + sub
                        kg = oc % 2
                        P.dma("sp", lambda e, oc=oc, kg=kg, i=i: e.dma_start(out=st(kg), in_=gv[:, oc, i * TT:(i + 1) * TT]),
                              b_ST[kg], reads=[b_GATE[i]])
                        P.dma("sp", lambda e, oc=oc, kg=kg, i=i: e.dma_start(out=st(2 + kg), in_=gv[:, KC + oc, i * TT:(i + 1) * TT]),
                              b_ST[2 + kg], reads=[b_GATE[i]])
                        for kc in range(8):
                            lhsT = wsb_t[:, kc * 256 + sub * 128: kc * 256 + (sub + 1) * 128]
                            P.op("pe", lambda e, lhsT=lhsT, kc=kc: e.matmul(ps(2), lhsT=lhsT, rhs=xn(kc), start=(kc == 0), stop=(kc == 7)),
                                 reads=[wsb_b, b_XNc[kc]], writes=[psb[2]])
                        for kc in range(4):
                            lhsT = wd_t[:, kc * 256 + sub * 128: kc * 256 + (sub + 1) * 128]
                            P.op("pe", lambda e, lhsT=lhsT, kc=kc: e.matmul(ps(3), lhsT=lhsT, rhs=xn(8 + kc), start=(kc == 0), stop=(kc == 3)),
                                 reads=[wd_b, b_XNc[8 + kc]], writes=[psb[3]])
                        P.op("dve", lambda e, kg=kg: e.tensor_tensor(out=fw(kg), in0=ps(2), in1=st(kg), op=ALU.mult),
                             reads=[psb[2], b_ST[kg]], writes=[fwb[kg]])
                        P.op("dve", lambda e, kg=kg: e.tensor_tensor(out=fw(2 + kg), in0=ps(3), in1=st(2 + kg), op=ALU.mult),
                             reads=[psb[3], b_ST[2 + kg]], writes=[fwb[2 + kg]])
                        P.op("dve", lambda e, kg=kg, oc=oc: e.tensor_tensor(out=at(oc), in0=fw(kg), in1=fw(2 + kg), op=ALU.add),
                             reads=[fwb[kg], fwb[2 + kg]], writes=[b_ATc[oc]])

                def ep_y(c, bank, i=i, total=KC):
                    k = 4 + c % 2
                    P.op("act", lambda e: e.copy(out=fw(k), in_=ps(bank)), reads=[psb[bank]], writes=[fwb[k]])
                    sumsq_accum(ps(bank), [psb[bank]], c % 2, c == 0, c == total - 1)
                    P.dma("act", lambda e: e.dma_start(out=Y[c * 128:(c + 1) * 128, i * TT:(i + 1) * TT], in_=fw(k)),
                          b_Y[i], reads=[fwb[k]])
                dense_fm(L, ["w_out"], at, lambda kc: b_ATc[kc], ep_y)
                rstd_from_sumsq(0)
                if L == 0:
                    norm_pass(L, i, xT, b_X, True, 1, True, 2)
                else:
                    norm_pass(L, i, HT, b_HT[i], True, 1, True, 2)

                if getattr(cfg, "debug", False) and i == 0:
                    P.dma("sp", lambda e, i=i: e.dma_start(out=dbgH1[:, i * TT:(i + 1) * TT], in_=HT[:, i * TT:(i + 1) * TT]), b_dbgT, reads=[b_HT[i]])
                    P.dma("sp", lambda e, i=i: e.dma_start(out=dbgY1[:, i * TT:(i + 1) * TT], in_=Y[:, i * TT:(i + 1) * TT]), b_dbgT, reads=[b_Y[i]])
                gu_state = {}

                def ep_gu(c, bank, i=i):
                    blk, which = c // 2, c % 2
                    if which == 0:
                        k = 6 + blk % 2
                        P.op("act", lambda e: e.activation(out=fw(k), in_=ps(bank), func=AF.Silu), reads=[psb[bank]], writes=[fwb[k]])
                        gu_state["k"] = k
                    else:
                        k = gu_state["k"]
                        P.op("dve", lambda e: e.tensor_tensor(out=at(blk), in0=fw(k), in1=ps(bank), op=ALU.mult),
                             reads=[fwb[k], psb[bank]], writes=[b_ATc[blk]])
                dense_fm(L, ["ffn_gu"], xn, lambda kc: b_XNc[kc], ep_gu)
                dense_fm(L, ["ffn_d0", "ffn_d1"], at, lambda kc: b_ATc[kc], ep_y)
                rstd_from_sumsq(0)
                norm_pass(L, i, HT, b_HT[i], True, 3, True, 4)

                if getattr(cfg, "debug", False) and i == 0:
                    P.dma("sp", lambda e, i=i: e.dma_start(out=dbgH2[:, i * TT:(i + 1) * TT], in_=HT[:, i * TT:(i + 1) * TT]), b_dbgT, reads=[b_HT[i]])
                    P.dma("sp", lambda e, i=i: e.dma_start(out=dbgY2[:, i * TT:(i + 1) * TT], in_=Y[:, i * TT:(i + 1) * TT]), b_dbgT, reads=[b_Y[i]])
                def ep_d1(c, bank):
                    P.op("act", lambda e: e.copy(out=at(c), in_=ps(bank)), reads=[psb[bank]], writes=[b_ATc[c]])
                dense_fm(L, ["ple_gd"], xn, lambda kc: b_XNc[kc], ep_d1)
                ptv = PTB.ap().rearrange("(l c p) t -> l p c t", c=2, p=128)
                for c2 in range(2):
                    P.dma("sp", lambda e, i=i, L=L, c2=c2: e.dma_start(out=at(2 + c2), in_=ptv[L, :, c2, i * TT:(i + 1) * TT]),
                          b_ATc[2 + c2], reads=[b_PTB])
                for b in range(cfg.seg["ple_gu"]["nblk"]):
                    wgu_t, wgu_b = slots_s.load(L, "ple_gu", b)
                    win_t, win_b = slots_s.load(L, "ple_in", b)
                    for sub in range(4):
                        oc = b * 4 + sub
                        for kc in range(2):
                            P.op("pe", lambda e, kc=kc, sub=sub, wgu_t=wgu_t: e.matmul(
                                ps(2), lhsT=wgu_t[:, kc * 512 + sub * 128: kc * 512 + (sub + 1) * 128], rhs=at(kc),
                                start=(kc == 0), stop=(kc == 1)), reads=[wgu_b, b_ATc[kc]], writes=[psb[2]])
                        for kc in range(2):
                            P.op("pe", lambda e, kc=kc, sub=sub, win_t=win_t: e.matmul(
                                ps(3), lhsT=win_t[:, kc * 512 + sub * 128: kc * 512 + (sub + 1) * 128], rhs=at(2 + kc),
                                start=(kc == 0), stop=(kc == 1)), reads=[win_b, b_ATc[2 + kc]], writes=[psb[3]])
                        k = oc % 2
                        k2 = 4 + oc % 2
                        P.op("act", lambda e, k=k: e.activation(out=fw(k), in_=ps(2), func=AF.Sigmoid), reads=[psb[2]], writes=[fwb[k]])
                        P.op("dve", lambda e, k=k, k2=k2: e.tensor_tensor(out=fw(k2), in0=fw(k), in1=ps(3), op=ALU.mult),
                             reads=[fwb[k], psb[3]], writes=[fwb[k2]])
                        sumsq_accum(fw(k2), [fwb[k2]], oc % 2, oc == 0, oc == KC - 1)
                        P.dma("act", lambda e, oc=oc, k2=k2, i=i: e.dma_start(out=Y[oc * 128:(oc + 1) * 128, i * TT:(i + 1) * TT], in_=fw(k2)),
                              b_Y[i], reads=[fwb[k2]])
                rstd_from_sumsq(0)
                last = (L == DEPTH - 1)
                norm_pass(L, i, HT, b_HT[i], True, 5, False, 0, h_dst=(outT if last else None), b_dst=(b_OUT if last else None))
            P.barrier()

        if getattr(cfg, "debug", False):
            b_dbg = P.buf("dbg", dma=True)
            for name, t in (("XQG", XQG), ("XVSG", XVSG), ("OXG", OXG), ("XQL", XQL), ("XVS", XVS), ("XVD0", XVD[0]), ("XVD1", XVD[1]), ("XVD2", XVD[2]), ("GATE", GATE), ("OXL", OXL), ("Ydbg", Y)):
                shp = list(t.shape)
                dt_ = BF16 if name != "Ydbg" else F32
                o = nc.dram_tensor("dbg_" + name, shp, dt_, kind="ExternalOutput")
                step = max(1, 1024 * 1024 // (shp[1] * 2))
                for r0 in range(0, shp[0], step):
                    r1 = min(shp[0], r0 + step)
                    P.dma("sp", lambda e, o=o, t=t, r0=r0, r1=r1: e.dma_start(out=o[r0:r1, :], in_=t[r0:r1, :]), b_dbg)
        P.barrier(engines=("pe", "act", "dve", "sp", "pool"))

        with nc.Block() as block:
            @block.sync
            def _(e):
                P.replay("sp", e)

            @block.tensor
            def _(e):
                P.replay("pe", e)

            @block.vector
            def _(e):
                P.replay("dve", e)

            @block.scalar
            def _(e):
                P.replay("act", e)

            @block.gpsimd
            def _(e):
                P.replay("pool", e)
    return nc


def run(cfg, x, p, w_in, w_proj_sb, w_proj_dil, w_out, g_mix_pre, g_mix_post, w_ffn_gate, w_ffn_up, w_ffn_down,
        g_ffn_pre, g_ffn_post, w_ple_in, w_ple_gate_down, w_ple_gate_up, g_ple_gate, g_ple_post):
    D, S, T, DEPTH, KC = cfg.D, cfg.S, cfg.T, cfg.DEPTH, cfg.KC
    B = x.shape[0]
    assert B * 2 == 8
    f = lambda a: np.asarray(a, dtype=np.float32)
    x, p = f(x), f(p)
    nc = build(cfg)
    shards = [[None] * DEPTH for _ in range(8)]
    for L in range(DEPTH):
        flats = pack_layer(cfg, f(w_in[L]), f(w_proj_sb[L]), f(w_proj_dil[L]), f(w_out[L]), f(w_ffn_gate[L]), f(w_ffn_up[L]),
                           f(w_ffn_down[L]), f(w_ple_in[L]), f(w_ple_gate_down[L]), f(w_ple_gate_up[L]))
        for c in range(8):
            shards[c][L] = []
        for g, flat in enumerate(flats):
            v = flat.reshape(cfg.gncoll[g], 8, CH)
            for c in range(8):
                shards[c][L].append(np.ascontiguousarray(v[:, c, :]).reshape(cfg.gncoll[g] * CH // FLATW, FLATW))
        del flats, v
    gl = [g_mix_pre, g_mix_post, g_ffn_pre, g_ffn_post, g_ple_gate, g_ple_post]
    gains = np.stack([f(g) for g in gl], 0)
    gains = np.ascontiguousarray(gains.reshape(6, DEPTH, KC, 128).transpose(3, 0, 1, 2)).reshape(128, 6 * DEPTH * KC)
    in_maps = []
    for c in range(8):
        b, s = c // 2, c % 2
        m = {
            "xT": np.ascontiguousarray(x[b, s * T:(s + 1) * T, :].T),
            "pT": np.ascontiguousarray(p[:, b, s * T:(s + 1) * T, :].transpose(0, 2, 1)).reshape(DEPTH * D_PLE, T),
            "gains": gains,
            "consts": make_consts(s),
        }
        for L in range(DEPTH):
            for g in range(cfg.NGRP):
                m[f"wsh{L}_{g}"] = shards[c][L][g]
        in_maps.append(m)
    res = run_bass_kernel_spmd(nc, in_maps, core_ids=list(range(8)))
    if getattr(cfg, "debug", False):
        cfg.dbg = res.results
    out = np.empty((B, S, D), np.float32)
    for c in range(8):
        b, s = c // 2, c % 2
        out[b, s * T:(s + 1) * T, :] = res.results[c]["outT"].T
    return out


def kernel(**inputs):
    cfg = Cfg()
    return run(cfg, **inputs)
```

```python
import contextlib
import numpy as np
import concourse.bass as bass
import concourse.mybir as mybir
from concourse.bass_utils import run_bass_kernel_spmd

F32 = mybir.dt.float32
BF16 = mybir.dt.bfloat16
AF = mybir.ActivationFunctionType
ALU = mybir.AluOpType

HEAD = 128
NH_SB = 8
NG = 3
DILS = (1, 4, 16)
NH_G = 4
W_SB = NH_SB * HEAD
W_DIL = NG * NH_G * HEAD
D_PLE = 256
EPS = 1e-6
NEG = -30000.0
CH = 262144
FLATW = 2048


class Cfg:
    def __init__(self, D=4096, S=4096, DFF=11008, DEPTH=4, TT=512):
        self.D, self.S, self.DFF, self.DEPTH, self.TT = D, S, DFF, DEPTH, TT
        self.T = S // 2
        self.NT = self.T // TT
        self.KC = D // 128
        self.FC = DFF // 128
        self.NQK = (2 * W_SB + 2 * W_DIL) // 128
        segs = [
            ("in_qk", D, 2 * W_SB + 2 * W_DIL, 256),
            ("in_gate", D, 2 * D, 256),
            ("in_v", D, W_SB + W_DIL, 256),
            ("p_sb", W_SB, D, 256),
            ("p_dil", NH_G * HEAD, D, 256),
            ("w_out", D, D, 256),
            ("ffn_gu", D, 2 * DFF, 256),
            ("ffn_d0", DFF // 2, D, 128),
            ("ffn_d1", DFF // 2, D, 128),
            ("ple_gd", D, D_PLE, 256),
            ("ple_gu", D_PLE, D, 512),
            ("ple_in", D_PLE, D, 512),
        ]
        self.seg = {}
        groups = [["in_qk", "in_gate", "in_v", "p_sb", "p_dil", "w_out"], ["ffn_gu"],
                  ["ffn_d0", "ffn_d1", "ple_gd", "ple_gu", "ple_in"]]
        sd = {n: (K, N, NB) for n, K, N, NB in segs}
        per = 8 * CH
        self.gflat, self.gncoll, self.gsegs = [], [], groups
        for gi, names in enumerate(groups):
            off = 0
            for name in names:
                K, N, NB = sd[name]
                be = (K // 128) * NB
                self.seg[name] = dict(off=off, K=K, N=N, NB=NB, be=be, nblk=N // NB, kc=K // 128, grp=gi)
                off += K * N
            fl = ((off + per - 1) // per) * per
            self.gflat.append(fl)
            self.gncoll.append(fl // per)
        self.NGRP = len(groups)


def _pack(W, NB):
    K, N = W.shape
    return np.ascontiguousarray(W.reshape(K // 128, 128, N // NB, NB).transpose(2, 1, 0, 3)).reshape(-1)


def pack_layer(cfg, w_in, w_proj_sb, w_proj_dil, w_out, w_g, w_u, w_d, w_ple_in, w_gd, w_gu):
    D = cfg.D
    o = 0
    q_sb = w_in[:, o:o + W_SB]; o += W_SB
    k_sb = w_in[:, o:o + W_SB]; o += W_SB
    v_sb = w_in[:, o:o + W_SB]; o += W_SB
    q_d = w_in[:, o:o + W_DIL]; o += W_DIL
    k_d = w_in[:, o:o + W_DIL]; o += W_DIL
    v_d = w_in[:, o:o + W_DIL]; o += W_DIL
    g_sb = w_in[:, o:o + D]; o += D
    g_d = w_in[:, o:o + D]; o += D
    FC = cfg.FC
    gu = np.concatenate([w_g.reshape(D, FC, 1, 128), w_u.reshape(D, FC, 1, 128)], axis=2).reshape(D, 2 * cfg.DFF)
    hf = cfg.DFF // 2
    parts = [
        _pack(np.concatenate([q_sb, k_sb, q_d, k_d], axis=1), 256),
        _pack(np.concatenate([g_sb, g_d], axis=1), 256),
        _pack(np.concatenate([v_sb, v_d], axis=1), 256),
        _pack(w_proj_sb, 256),
        _pack(w_proj_dil, 256),
        _pack(w_out, 256),
        _pack(gu, 256),
        _pack(w_d[:hf], 128),
        _pack(w_d[hf:], 128),
        _pack(w_gd, 256),
        _pack(w_gu, 512),
        _pack(w_ple_in, 512),
    ]
    names = ["in_qk", "in_gate", "in_v", "p_sb", "p_dil", "w_out", "ffn_gu", "ffn_d0", "ffn_d1", "ple_gd", "ple_gu", "ple_in"]
    flats = [np.zeros(fl, np.float32) for fl in cfg.gflat]
    for name, p in zip(names, parts):
        sg = cfg.seg[name]
        assert p.size == sg["K"] * sg["N"]
        flats[sg["grp"]][sg["off"]:sg["off"] + p.size] = p
    return flats


CB_ONES, CB_TRI, CB_LI, CB_M01, CB_MNEG = 0, 128, 256, 384, 384 + 2048
CB_SEL = 384 + 4096
NCB = 384 + 4096 + 256
CF_DIL = 0
CF_ONE = NG * 2 * 2 * 128
CF_EPS = CF_ONE + 1
NCF = CF_EPS + 1


def make_consts(rank):
    p = np.arange(128)[:, None].astype(np.float64)
    tri = (p > np.arange(128)[None, :]).astype(np.float64)
    cb = [np.ones((128, 128)), tri, 1.0 - tri]
    c512 = np.arange(512)[None, :]
    for jj in range(4):
        cb.append((c512 > 128 * jj + p).astype(np.float64))
    for jj in range(4):
        cb.append(np.where(c512 > 128 * jj + p, 0.0, NEG))
    eye = np.eye(128)
    cb.append(eye if rank == 0 else np.zeros((128, 128)))
    cb.append(eye if rank == 1 else np.zeros((128, 128)))
    slopes = np.exp2(-8.0 * np.arange(1, NH_G + 1) / NH_G)
    pq = np.arange(128)[None, :]
    cf = []
    for g in range(NG):
        r = DILS[g]
        for lh in range(2):
            sl = slopes[2 * rank + lh]
            cf.append(np.where(pq >= p, -sl * r * (pq - p), NEG))
            cf.append(np.where(p >= pq, -sl * r * (128 + pq - p), NEG))
    cf.append(np.ones((128, 1)))
    cf.append(np.full((128, 1), EPS))
    return np.concatenate(cb + cf, axis=1).astype(np.float32)


class Buf:
    __slots__ = ("name", "w", "rs", "dsem", "dcnt")

    def __init__(self, name):
        self.name = name
        self.w = None
        self.rs = {}
        self.dsem = None
        self.dcnt = 0


class Prog:
    ENGS = ("pe", "act", "dve", "pool", "sp")

    def __init__(self, nc, stack):
        self.nc = nc
        self.stack = stack
        self.ops = {e: [] for e in self.ENGS}
        self.esem = {e: self.sem("prog_" + e) for e in self.ENGS}
        self.ecnt = {e: 0 for e in self.ENGS}
        self.seen = {e: {} for e in self.ENGS}
        self.csem = self.sem("coll")
        self.ccnt = 0
        self.dirty = {}
        self.semcnt = {}

    def sem(self, name):
        return self.stack.enter_context(self.nc.semaphore(name))

    def buf(self, name, dma=False, sem=None):
        b = Buf(name)
        if sem is not None:
            b.dsem = sem
        elif dma:
            b.dsem = self.sem("d_" + name)
        return b

    def _waits(self, eng, reads, writes):
        need = {}

        def add(tok):
            if tok is None:
                return
            s, v = tok
            k = id(s)
            if k not in need or need[k][1] < v:
                need[k] = (s, v)
        for b in reads:
            add(b.w)
        for b in writes:
            add(b.w)
            for r in b.rs.values():
                add(r)
        out = []
        seen = self.seen[eng]
        for k, (s, v) in need.items():
            if eng == "pe" and s is self.esem["pe"]:
                continue
            if k in self.semcnt:
                v = 16 * self.semcnt[k]
            if seen.get(k, 0) >= v:
                continue
            seen[k] = v
            out.append((s, v))
        return out

    def _commit(self, tok, reads, writes):
        k = id(tok[0])
        for b in writes:
            b.w = tok
            b.rs = {}
        for b in reads:
            b.rs[k] = tok

    def op(self, eng, fn, reads=(), writes=()):
        waits = self._waits(eng, reads, writes)
        self.ecnt[eng] += 1
        tok = (self.esem[eng], self.ecnt[eng])
        self.ops[eng].append((fn, waits, (self.esem[eng], 1)))
        self._commit(tok, reads, writes)

    def dma(self, eng, fn, dst, reads=(), extra_writes=()):
        writes = (dst,) + tuple(extra_writes)
        waits = self._waits(eng, reads, writes)
        c = self.semcnt.get(id(dst.dsem), 0) + 1
        self.semcnt[id(dst.dsem)] = c
        tok = (dst.dsem, 16 * c)
        self.ops[eng].append((fn, waits, (dst.dsem, 16)))
        self._commit(tok, reads, writes)
        self.dirty[id(dst)] = (dst, tok)

    def coll(self, fn, reads=(), writes=()):
        waits = self._waits("pool", reads, writes)
        self.ccnt += 1
        tok = (self.csem, self.ccnt)
        self.ops["pool"].append((fn, waits, (self.csem, None)))
        self._commit(tok, reads, writes)

    def barrier(self, engines=("pe", "act", "dve", "sp")):
        toks = [(self.esem[e], self.ecnt[e]) for e in engines if self.ecnt[e] > 0]
        toks += [t for (b, t) in self.dirty.values() if not b.name.startswith("wshb")]
        for e in engines:
            seen = self.seen[e]
            ws = []
            for s, v in toks:
                if seen.get(id(s), 0) >= v:
                    continue
                seen[id(s)] = v
                ws.append((s, v))
            if ws:
                self.ops[e].append((None, ws, None))
        self.dirty = {k: bt for k, bt in self.dirty.items() if bt[0].name.startswith("wshb")}

    def replay(self, eng, e):
        import os
        probe = os.environ.get("KDEBUG3") and eng == "sp"
        pstate = True
        for opi, (fn, waits, inc) in enumerate(self.ops[eng]):
            if probe and opi < 193 and opi % 8 == 0:
                try:
                    self.ops[eng][193][0](e); ok = True
                except Exception as ex:
                    ok = False
                if ok != pstate:
                    print("PROBE change at", opi, ok); pstate = ok
            for s, v in waits:
                e.wait_ge(s, v)
            if fn is None:
                continue
            try:
                ins = fn(e)
            except Exception as ex:
                import os
                if os.environ.get("KDEBUG"):
                    print("REPLAY FAIL", eng, len(self.ops[eng]), self.ops[eng].index((fn, waits, inc)), ex)
                    for t in range(2):
                        try:
                            ins = fn(e); print("retry ok"); break
                        except Exception as ex2:
                            print("retry fail", ex2)
                raise
            if inc is not None:
                if inc[1] is None:
                    ins.then_inc(inc[0])
                else:
                    ins.then_inc(inc[0], inc[1])


def build(cfg):
    nc = bass.Bass("TRN2", target_bir_lowering=False)
    D, S, T, TT, NT, KC, FC, DEPTH = cfg.D, cfg.S, cfg.T, cfg.TT, cfg.NT, cfg.KC, cfg.FC, cfg.DEPTH
    SCALE = float(HEAD) ** -0.5
    NSUB = TT // 128
    assert TT == 512
    stack = contextlib.ExitStack()
    with stack:
        P = Prog(nc, stack)

        xT = nc.dram_tensor("xT", [D, T], F32, kind="ExternalInput")
        pT = nc.dram_tensor("pT", [DEPTH * D_PLE, T], F32, kind="ExternalInput")
        gains = nc.dram_tensor("gains", [128, 6 * DEPTH * KC], F32, kind="ExternalInput")
        consts = nc.dram_tensor("consts", [128, NCB + NCF], F32, kind="ExternalInput")
        RPC = CH // FLATW
        NGRP = cfg.NGRP
        wsh = [[nc.dram_tensor(f"wsh{L}_{g}", [cfg.gncoll[g] * RPC, FLATW], F32, kind="ExternalInput") for g in range(NGRP)] for L in range(DEPTH)]
        outT = nc.dram_tensor("outT", [D, T], F32, kind="ExternalOutput")

        wshb = [[nc.dram_tensor(f"wshb{L}_{g}", [cfg.gncoll[g] * RPC, FLATW], BF16) for g in range(NGRP)] for L in range(DEPTH)]
        ws1 = [[nc.dram_tensor(f"ws1_{L}_{g}", [cfg.gncoll[g] * 4 * RPC, FLATW], BF16) for g in range(NGRP)] for L in range(DEPTH)]
        wfull = [[nc.dram_tensor(f"wfull{L}_{g}", [cfg.gflat[g] // FLATW, FLATW], BF16) for g in range(NGRP)] for L in range(DEPTH)]
        HT = nc.dram_tensor("HT", [D, T], F32)
        Y = nc.dram_tensor("Y", [D, T], F32)
        PTB = nc.dram_tensor("PTB", [DEPTH * D_PLE, T], BF16)
        NQC = cfg.NQK // 4
        XQL = nc.dram_tensor("XQL", [NQC * 512, T], BF16)
        XQG = nc.dram_tensor("XQG", [NQC * 2 * 512, T], BF16)
        HV = T // 2
        XVS = nc.dram_tensor("XVS", [T, W_SB], BF16)
        XVSG = nc.dram_tensor("XVSG", [4 * HV, W_SB], BF16)
        XVD = [nc.dram_tensor(f"XVD{g}", [T, NH_G * HEAD], BF16) for g in range(NG)]
        XVDG = [nc.dram_tensor(f"XVDG{g}", [2 * T, NH_G * HEAD], BF16) for g in range(NG)]
        GATE = nc.dram_tensor("GATE", [2 * D, T], BF16)
        OXL = nc.dram_tensor("OXL", [768, S], BF16)
        OXG = nc.dram_tensor("OXG", [6 * 2 * 128, S], BF16)

        b_X = P.buf("xT")
        b_HT = [P.buf(f"HT{i}", dma=True) for i in range(NT)]
        b_Y = [P.buf(f"Y{i}", dma=True) for i in range(NT)]
        b_XQL = P.buf("XQL", dma=True)
        b_XV = P.buf("XV", dma=True)
        b_GATE = [P.buf(f"GATE{i}", dma=True) for i in range(NT)]
        b_OXL = P.buf("OXL", dma=True)
        b_XQGc = [P.buf(f"XQG{c}") for c in range(cfg.NQK // 4)]
        b_XVGs = P.buf("XVGs")
        b_XVGd = [P.buf(f"XVGd{g}") for g in range(NG)]
        b_OXG = P.buf("OXG")
        b_PTB = P.buf("PTB", dma=True)
        b_OUT = P.buf("OUT", dma=True)
        wcast_sems = [P.sem(f"wcast{k}") for k in range(4)]
        b_wshb = [[[P.buf(f"wshb{L}_{g}_{i}", sem=wcast_sems[i % 4]) for i in range(cfg.gncoll[g])] for g in range(NGRP)] for L in range(DEPTH)]
        b_wfull = [[[P.buf(f"wf{L}_{g}_{i}") for i in range(cfg.gncoll[g])] for g in range(NGRP)] for L in range(DEPTH)]

        def sbt(name, shape, dt):
            return stack.enter_context(nc.sbuf_tensor(name, shape, dt))
        CB = sbt("CB", [128, NCB], BF16)
        CF = sbt("CF", [128, NCF], F32)
        GN = sbt("GN", [128, 6 * DEPTH * KC], F32)
        XNE = max(KC * TT, 16384)
        XN = sbt("XN", [128, XNE], BF16)
        ATE = max(FC * TT, 10 * S, (KC + 24) * TT)
        AT = sbt("AT", [128, ATE], BF16)
        WB = sbt("WB", [128, 2 * 8192 + 2 * 2048], BF16)
        FW = sbt("FW", [128, 8 * 512], F32)
        ST = sbt("ST", [128, 4 * 512], BF16)
        OST = sbt("OST", [128, 2 * 512], BF16)
        SQ = sbt("SQ", [128, 2 * TT], BF16)
        RS = sbt("RS", [128, 2 * TT], F32)
        PS = stack.enter_context(nc.psum_tensor("PS", [128, 8 * 512], F32))
        psb = [P.buf(f"ps{i}") for i in range(8)]

        def ps(i, n=512, o=0):
            return PS[:, i * 512 + o:i * 512 + o + n]

        b_CB, b_CF, b_GN = P.buf("CB", dma=True), P.buf("CF", dma=True), P.buf("GN", dma=True)
        NCHK = ATE // TT
        xn_sems = [P.sem(f"d_xn{k}") for k in range(4)]
        at_sems = [P.sem(f"d_at{k}") for k in range(4)]
        b_XNc = [P.buf(f"XN{k}", sem=xn_sems[k % 4]) for k in range(XNE // TT)]
        b_ATc = [P.buf(f"AT{k}", sem=at_sems[k % 4]) for k in range(NCHK)]
        b_RS = [P.buf("RS0"), P.buf("RS1")]
        b_SQ = [P.buf("sq0"), P.buf("sq1")]
        b_OST = [P.buf("ost0"), P.buf("ost1")]
        fwb = [P.buf(f"fw{k}", dma=True) for k in range(8)]
        b_ST = [P.buf(f"st{k}", dma=True) for k in range(4)]

        def fw(k, n=TT):
            return FW[:, k * 512:k * 512 + n]

        def st(k, n=TT):
            return ST[:, k * 512:k * 512 + n]

        def ost(k, n=TT):
            return OST[:, k * 512:k * 512 + n]

        def rs(i):
            return RS[:, i * TT:(i + 1) * TT]

        def xn(kc):
            return XN[:, kc * TT:(kc + 1) * TT]

        def at(kc):
            return AT[:, kc * TT:(kc + 1) * TT]

        ones_bf = CB[:, CB_ONES:CB_ONES + 128]
        tri_bf = CB[:, CB_TRI:CB_TRI + 128]
        li_bf = CB[:, CB_LI:CB_LI + 128]
        sel_bf = [CB[:, CB_SEL:CB_SEL + 128], CB[:, CB_SEL + 128:CB_SEL + 256]]
        one_col = CF[:, CF_ONE:CF_ONE + 1]
        eps_col = CF[:, CF_EPS:CF_EPS + 1]
        rank_holder = {}

        def gain(kind, L, kc):
            o = (kind * DEPTH + L) * KC + kc
            return GN[:, o:o + 1]

        P.dma("pool", lambda e: e.dma_start(out=CB[:, :], in_=consts[:, 0:NCB]), b_CB)
        P.dma("sp", lambda e: e.dma_start(out=CF[:, :], in_=consts[:, NCB:NCB + NCF]), b_CF)
        P.dma("sp", lambda e: e.dma_start(out=GN[:, :], in_=gains[:, :]), b_GN)
        for L_ in range(DEPTH):
            P.dma("pool", lambda e, L_=L_: e.dma_start(out=PTB[L_ * D_PLE:(L_ + 1) * D_PLE, :], in_=pT[L_ * D_PLE:(L_ + 1) * D_PLE, :]), b_PTB)

        QUADS = [[0, 1, 2, 3], [4, 5, 6, 7]]
        XPAIRS = [[0, 4], [1, 5], [2, 6], [3, 7]]
        PAIRS = [[0, 1], [2, 3], [4, 5], [6, 7]]

        def wcast(L, g, i):
            r0 = i * RPC
            src_ = wsh[L][g][r0:r0 + RPC, :]
            dstb = wshb[L][g][r0:r0 + RPC, :]
            P.dma("pool", lambda e: e.dma_start(out=dstb, in_=src_), b_wshb[L][g][i])

        def wgather(L, g, i):
            r0 = i * RPC
            dstb = wshb[L][g][r0:r0 + RPC, :]
            bsh = b_wshb[L][g][i]
            s1 = ws1[L][g][i * 4 * RPC:(i + 1) * 4 * RPC, :]
            b1 = P.buf("s1")
            P.coll(lambda e: e.collective_compute("AllGather", ALU.bypass, replica_groups=QUADS,
                                                  ins=[dstb], outs=[s1]), reads=[bsh], writes=[b1])
            wf = wfull[L][g][i * 8 * RPC:(i + 1) * 8 * RPC, :]
            P.coll(lambda e: e.collective_compute("AllGather", ALU.bypass, replica_groups=XPAIRS,
                                                  ins=[s1], outs=[wf]), reads=[b1], writes=[b_wfull[L][g][i]])

        wq = [(L, g, i) for L in range(DEPTH) for g in range(NGRP) for i in range(cfg.gncoll[g])]
        ncoll_layer = sum(cfg.gncoll)
        wq_pos = [0]
        wc_pos = [0]
        CAST_AHEAD = 3

        def _step():
            while wc_pos[0] < min(len(wq), wq_pos[0] + 1 + CAST_AHEAD):
                wcast(*wq[wc_pos[0]])
                wc_pos[0] += 1
            wgather(*wq[wq_pos[0]])
            wq_pos[0] += 1

        def pump_weights(n):
            for _ in range(n):
                if wq_pos[0] < len(wq):
                    _step()

        def pump_layer(L):
            while wq_pos[0] < len(wq) and wq[wq_pos[0]][0] <= L:
                _step()

        def wblock(L, name, b):
            sg = cfg.seg[name]
            be = sg["be"]
            off = sg["off"] + b * 128 * be
            g = sg["grp"]
            v = wfull[L][g].ap().rearrange("r (a c) -> (r a) c", c=128)
            r0 = off // 128
            ap = v[r0:r0 + be, :].rearrange("(p a) c -> p (a c)", p=128)
            i0 = off // (8 * CH)
            i1 = (off + 128 * be - 1) // (8 * CH)
            return ap, [b_wfull[L][g][i] for i in range(i0, i1 + 1)], be

        class Slots:
            def __init__(self, n, size, base, name):
                self.n, self.size, self.base = n, size, base
                self.bufs = [P.buf(f"{name}{k}", dma=True) for k in range(n)]
                self.k = 0

            def load(self, L, seg, b):
                ap, deps, be = wblock(L, seg, b)
                assert be <= self.size
                k = self.k
                self.k = (k + 1) % self.n
                o = self.base + k * self.size
                dst = WB[:, o:o + be]
                P.dma("sp", lambda e: e.dma_start(out=dst, in_=ap), self.bufs[k], reads=deps)
                return dst, self.bufs[k]

        slots_b = Slots(2, 8192, 0, "wbig")
        slots_s = Slots(2, 2048, 16384, "wsml")
        bank_toggle = [0]

        def next_bank():
            b = bank_toggle[0] % 2
            bank_toggle[0] += 1
            return b

        def sumsq_accum(src_ap, src_bufs, sqk, first, last):
            sq = SQ[:, sqk * TT:(sqk + 1) * TT]
            P.op("act", lambda e: e.activation(out=sq, in_=src_ap, func=AF.Square), reads=src_bufs, writes=[b_SQ[sqk]])
            P.op("pe", lambda e: e.matmul(ps(7), lhsT=ones_bf, rhs=sq, start=first, stop=last),
                 reads=[b_SQ[sqk], b_CB], writes=[psb[7]])

        def rstd_from_sumsq(ri):
            P.op("act", lambda e: e.activation(out=rs(ri), in_=ps(7), func=AF.Sqrt, bias=eps_col, scale=1.0 / D),
                 reads=[psb[7], b_CF], writes=[b_RS[ri]])
            P.op("dve", lambda e: e.reciprocal(out=rs(ri), in_=rs(ri)), reads=[b_RS[ri]], writes=[b_RS[ri]])

        def hview(t, i):
            return t.ap().rearrange("(kc p) t -> p kc t", p=128)[:, :, i * TT:(i + 1) * TT]

        def norm_pass(L, i, h_src, hbuf_src, resid, g_post, pre, g_pre, h_dst=None, b_dst=None):
            hs = hview(h_src, i)
            yv = hview(Y, i)
            hd = hview(h_dst if h_dst is not None else HT, i)
            bd = b_dst if b_dst is not None else b_HT[i]
            HB = AT[:, 0:KC * 2 * TT].bitcast(F32)

            def hbk(kc):
                return HB[:, kc * TT:(kc + 1) * TT]

            def hbb(kc):
                return [b_ATc[2 * kc], b_ATc[2 * kc + 1]]
            G = min(4, KC)
            for g0 in range(0, KC, G):
                bl = [bb for kc in range(g0, g0 + G) for bb in hbb(kc)]
                P.dma("sp", lambda e, g0=g0: e.dma_start(out=HB[:, g0 * TT:(g0 + G) * TT].rearrange("p (k t) -> p k t", k=G), in_=hs[:, g0:g0 + G, :]),
                      bl[0], reads=[hbuf_src], extra_writes=bl[1:])
            if resid:
                YG = min(4, KC)
                for yg in range(KC // YG):
                    s = yg % 2
                    P.dma("sp", lambda e, yg=yg, s=s: e.dma_start(
                        out=FW[:, 4 * s * 512:(4 * s + YG) * 512].rearrange("p (k t) -> p k t", k=YG), in_=yv[:, yg * YG:(yg + 1) * YG, :]),
                        fwb[4 * s], reads=[b_Y[i]], extra_writes=fwb[4 * s + 1:4 * s + YG])
                    for j in range(YG):
                        kc = yg * YG + j
                        yt, yb = fw(4 * s + j), fwb[4 * s + j]
                        gp = gain(g_post, L, kc)
                        P.op("dve", lambda e, yt=yt, gp=gp: e.scalar_tensor_tensor(
                            out=yt, in0=yt, scalar=gp, in1=rs(0), op0=ALU.mult, op1=ALU.mult),
                            reads=[yb, b_RS[0], b_GN], writes=[yb])
                        P.op("dve", lambda e, yt=yt, kc=kc: e.tensor_tensor(out=hbk(kc), in0=yt, in1=hbk(kc), op=ALU.add),
                             reads=[yb] + hbb(kc), writes=hbb(kc))
                        if pre:
                            sumsq_accum(hbk(kc), hbb(kc), kc % 2, kc == 0, kc == KC - 1)
                        if (kc + 1) % G == 0:
                            g0 = kc + 1 - G
                            bl = [bb for k2 in range(g0, g0 + G) for bb in hbb(k2)]
                            P.dma("act", lambda e, g0=g0: e.dma_start(out=hd[:, g0:g0 + G, :], in_=HB[:, g0 * TT:(g0 + G) * TT].rearrange("p (k t) -> p k t", k=G)),
                                  bd, reads=bl)
            elif pre:
                for kc in range(KC):
                    sumsq_accum(hbk(kc), hbb(kc), kc % 2, kc == 0, kc == KC - 1)
            if pre:
                rstd_from_sumsq(1)
                for kc in range(KC):
                    g2 = gain(g_pre, L, kc)
                    P.op("dve", lambda e, kc=kc, g2=g2: e.scalar_tensor_tensor(
                        out=xn(kc), in0=hbk(kc), scalar=g2, in1=rs(1), op0=ALU.mult, op1=ALU.mult),
                        reads=hbb(kc) + [b_RS[1], b_GN], writes=[b_XNc[kc]])

        def dense_fm(L, segs, rhs_of_kc, rhs_buf_of_kc, epilogue, sub_cols=128):
            sg0 = cfg.seg[segs[0]]
            NB = sg0["NB"]
            for b in range(sg0["nblk"]):
                wts = [slots_b.load(L, s, b) for s in segs[:1]]
                for sub in range(NB // sub_cols):
                    bank = next_bank()
                    kbase = 0
                    for si, s in enumerate(segs):
                        if si > 0 and sub == 0:
                            wts.append(slots_b.load(L, s, b))
                        wt, wbuf = wts[si]
                        kcn = cfg.seg[s]["kc"]
                        for kc in range(kcn):
                            lhsT = wt[:, kc * NB + sub * sub_cols: kc * NB + (sub + 1) * sub_cols]
                            first = (si == 0 and kc == 0)
                            last = (si == len(segs) - 1 and kc == kcn - 1)
                            P.op("pe", lambda e, lhsT=lhsT, kk=kbase + kc, bank=bank, first=first, last=last: e.matmul(
                                ps(bank), lhsT=lhsT, rhs=rhs_of_kc(kk), start=first, stop=last),
                                reads=[wbuf, rhs_buf_of_kc(kbase + kc)], writes=[psb[bank]])
                        kbase += kcn
                    epilogue(b * (NB // sub_cols) + sub, bank)

        st_k = [0]

        def next_st():
            k = st_k[0] % 4
            st_k[0] += 1
            return k


        def select_pieces(npieces, a_piece, b_piece, a_bufs, b_bufs, dst_piece, dst_bufs, banks, width=512):
            for pc in range(npieces):
                bank = banks[pc % len(banks)]
                P.op("pe", lambda e, pc=pc, bank=bank: e.matmul(ps(bank, width), lhsT=sel_bf[0], rhs=a_piece(pc), start=True, stop=False),
                     reads=[b_CB] + a_bufs(pc), writes=[psb[bank]])
                P.op("pe", lambda e, pc=pc, bank=bank: e.matmul(ps(bank, width), lhsT=sel_bf[1], rhs=b_piece(pc), start=False, stop=True),
                     reads=[b_CB] + b_bufs(pc), writes=[psb[bank]])
                if pc % 2 == 0:
                    P.op("act", lambda e, pc=pc, bank=bank: e.copy(out=dst_piece(pc), in_=ps(bank, width)), reads=[psb[bank]], writes=dst_bufs(pc))
                else:
                    P.op("dve", lambda e, pc=pc, bank=bank: e.tensor_copy(out=dst_piece(pc), in_=ps(bank, width)), reads=[psb[bank]], writes=dst_bufs(pc))

        if getattr(cfg, "debug", False):
            dbgT = nc.dram_tensor("dbg_T", [128, 5 * 512], F32, kind="ExternalOutput")
            dbgB = nc.dram_tensor("dbg_B", [128, 5 * 512], BF16, kind="ExternalOutput")
            dbgH1 = nc.dram_tensor("dbg_H1", [D, T], F32, kind="ExternalOutput")
            dbgH2 = nc.dram_tensor("dbg_H2", [D, T], F32, kind="ExternalOutput")
            dbgY1 = nc.dram_tensor("dbg_Y1", [D, T], F32, kind="ExternalOutput")
            dbgY2 = nc.dram_tensor("dbg_Y2", [D, T], F32, kind="ExternalOutput")
            b_dbgT = P.buf("dbgT", dma=True)

        def o_coll(c):
            P.coll(lambda e, c=c: e.collective_compute("AllGather", ALU.bypass, replica_groups=PAIRS,
                                                       ins=[OXL[c * 128:(c + 1) * 128, :]], outs=[OXG[c * 256:(c + 1) * 256, :]]),
                   reads=[b_OXL], writes=[b_OXG])

        def attention():
            nhb = T // 128
            QT, KT, VS = AT[:, 0:S], AT[:, S:2 * S], AT[:, 2 * S:3 * S]
            VSv = VS.rearrange("p (b d) -> p b d", d=128)
            bq = b_ATc[0:S // TT]
            bk = b_ATc[S // TT:2 * S // TT]
            bv = b_ATc[2 * S // TT:3 * S // TT]
            xqg3 = XQG.ap().rearrange("(a b) t -> a b t", b=1024)
            xqg4 = XQG.ap().rearrange("(a b c) t -> a b c t", b=4, c=256)
            xvs_v = XVSG.ap().rearrange("(q b p) (r c) -> q p b r c", p=128, b=HV // 128, c=512)
            def sb_tile(lh, tq, SET):
                QT, KT, VSv, bq, bk, bv = SET
                J = 4 * tq + 4
                qcols = QT[:, tq * 512:(tq + 1) * 512]

                def stage_a(n):
                    j = J - 1 - n
                    jj = j - 4 * tq
                    z = n % 2
                    ktb = KT[:, j * 128:(j + 1) * 128]
                    P.op("pe", lambda e: e.matmul(ps(z), lhsT=ktb, rhs=qcols, start=True, stop=True),
                         reads=[bk[0], bq[0]], writes=[psb[z]])
                    P.op("act", lambda e: e.activation(out=fw(z), in_=ps(z), func=AF.Exp, scale=SCALE), reads=[psb[z]], writes=[fwb[z]])
                    P.op("act", lambda e: e.activation(out=fw(2 + z), in_=fw(z), func=AF.Ln, bias=one_col, scale=1.0),
                         reads=[fwb[z], b_CF], writes=[fwb[2 + z]])

                def stage_a2(n):
                    j = J - 1 - n
                    jj = j - 4 * tq
                    z = n % 2
                    if jj >= 0:
                        m01 = CB[:, CB_M01 + jj * 512:CB_M01 + (jj + 1) * 512]
                        P.op("dve", lambda e: e.scalar_tensor_tensor(out=st(z), in0=fw(2 + z), scalar=-1.0, in1=m01, op0=ALU.mult, op1=ALU.mult),
                             reads=[fwb[2 + z], b_CB], writes=[b_ST[z]])
                    else:
                        P.op("dve", lambda e: e.tensor_scalar(out=st(z), in0=fw(2 + z), scalar1=-1.0, scalar2=None, op0=ALU.mult),
                             reads=[fwb[2 + z]], writes=[b_ST[z]])
                    P.op("dve", lambda e: e.scalar_tensor_tensor(out=fw(4 + z), in0=ps(z), scalar=SCALE, in1=fw(2 + z), op0=ALU.mult, op1=ALU.subtract),
                         reads=[psb[z], fwb[2 + z]], writes=[fwb[4 + z]])

                def o_mm(n):
                    j = J - 1 - n
                    z = n % 2
                    vsb = VSv[:, j, :]
                    P.op("pe", lambda e: e.matmul(ps(3), lhsT=vsb, rhs=st(2 + z), start=(n == 0), stop=(n == J - 1)),
                         reads=[bv[0], b_ST[2 + z]], writes=[psb[3]])

                def stage_b(n):
                    j = J - 1 - n
                    jj = j - 4 * tq
                    z = n % 2
                    P.op("pe", lambda e: e.matmul(ps(2), lhsT=tri_bf, rhs=st(z), start=(n == 0), stop=(n == J - 1)),
                         reads=[b_ST[z], b_CB], writes=[psb[2]])
                    if n > 0:
                        o_mm(n - 1)
                    P.op("dve", lambda e: e.tensor_tensor(out=fw(6 + z), in0=fw(4 + z), in1=ps(2), op=ALU.add),
                         reads=[fwb[4 + z], psb[2]], writes=[fwb[6 + z]])
                    if jj >= 0:
                        mneg = CB[:, CB_MNEG + jj * 512:CB_MNEG + (jj + 1) * 512]
                        P.op("dve", lambda e: e.tensor_tensor(out=fw(6 + z), in0=fw(6 + z), in1=mneg, op=ALU.add),
                             reads=[fwb[6 + z], b_CB], writes=[fwb[6 + z]])

                def stage_b2(n):
                    z = n % 2
                    P.op("act", lambda e: e.activation(out=st(2 + z), in_=fw(6 + z), func=AF.Exp), reads=[fwb[6 + z]], writes=[b_ST[2 + z]])
                    if n < J - 1:
                        P.op("pe", lambda e: e.matmul(ps(2), lhsT=li_bf, rhs=st(z), start=False, stop=False),
                             reads=[b_ST[z], b_CB], writes=[psb[2]])

                stage_a(0)
                stage_a2(0)
                for n in range(J):
                    if n + 1 < J:
                        stage_a(n + 1)
                    stage_b(n)
                    if n + 1 < J:
                        stage_a2(n + 1)
                    stage_b2(n)
                    if getattr(cfg, "debug", False) and lh == 0 and tq == 0 and n == 0:
                        for kk, src_k in enumerate((0, 2, 4, 6)):
                            P.dma("sp", lambda e, kk=kk, src_k=src_k: e.dma_start(out=dbgT[:, kk * 512:(kk + 1) * 512], in_=fw(src_k)), b_dbgT, reads=[fwb[src_k]])
                        for kk, src_k in enumerate((0, 2)):
                            P.dma("sp", lambda e, kk=kk, src_k=src_k: e.dma_start(out=dbgB[:, kk * 512:(kk + 1) * 512], in_=st(src_k)), b_dbgT, reads=[b_ST[src_k]])
                        P.dma("sp", lambda e: e.dma_start(out=dbgB[:, 1024:1536], in_=SET[0][:, 0:512]), b_dbgT, reads=[bq[0]])
                        P.dma("sp", lambda e: e.dma_start(out=dbgB[:, 1536:2048], in_=SET[1][:, 0:512]), b_dbgT, reads=[bk[0]])
                        P.dma("sp", lambda e: e.dma_start(out=dbgB[:, 2048:2560], in_=AT[:, 2 * S:2 * S + 512]), b_dbgT, reads=[bv[0]])
                o_mm(J - 1)
                ok = tq % 2
                P.op("act", lambda e, ok=ok: e.copy(out=ost(ok), in_=ps(3)), reads=[psb[3]], writes=[b_OST[ok]])
                P.dma("act", lambda e, ok=ok, lh=lh, tq=tq: e.dma_start(out=OXL[lh * 128:(lh + 1) * 128, tq * 512:(tq + 1) * 512], in_=ost(ok)),
                      b_OXL, reads=[b_OST[ok]])


            STA, STB = AT[:, 5 * S:6 * S], AT[:, 6 * S:7 * S]
            nbs = S // TT
            bsa, bsb = b_ATc[5 * nbs:6 * nbs], b_ATc[6 * nbs:7 * nbs]
            STAv = STA.rearrange("p (b d) -> p b d", d=128)
            STBv = STB.rearrange("p (b d) -> p b d", d=128)
            xvs3 = XVSG.ap().rearrange("(q b p) c -> q p b c", p=128, b=HV // 128)

            def load_fm(rows_a, rows_b, dst, dbufs, banks):
                for rp in range(2):
                    ra, rb = rows_a(rp), rows_b(rp)
                    P.dma("sp", lambda e, rp=rp, ra=ra: e.dma_start(out=STA[:, rp * T:(rp + 1) * T], in_=XQG[ra:ra + 128, :]),
                          bsa[0], reads=[b_XQGc[ra // 1024]], extra_writes=bsa[1:])
                    P.dma("sp", lambda e, rp=rp, rb=rb: e.dma_start(out=STB[:, rp * T:(rp + 1) * T], in_=XQG[rb:rb + 128, :]),
                          bsb[0], reads=[b_XQGc[rb // 1024]], extra_writes=bsb[1:])
                select_pieces(S // 512, lambda pc: STA[:, pc * 512:(pc + 1) * 512], lambda pc: STB[:, pc * 512:(pc + 1) * 512],
                              lambda pc: [bsa[0]], lambda pc: [bsb[0]], lambda pc: dst[:, pc * 512:(pc + 1) * 512],
                              lambda pc: [dbufs[0]], banks)

            def mk_set(k):
                base = 0 if k == 0 else 7 * S
                cb0 = base // TT
                q_, k_, v_ = AT[:, base:base + S], AT[:, base + S:base + 2 * S], AT[:, base + 2 * S:base + 3 * S]
                return (q_, k_, v_.rearrange("p (b d) -> p b d", d=128), b_ATc[cb0:cb0 + nbs], b_ATc[cb0 + nbs:cb0 + 2 * nbs],
                        b_ATc[cb0 + 2 * nbs:cb0 + 3 * nbs], v_)
            sets = [mk_set(0), mk_set(1)]

            def sel_steps(dst, dbufs):
                out = []
                for pc in range(S // 512):
                    out.append(lambda pc=pc: select_pieces_one(pc, dst, dbufs))
                return out

            def select_pieces_one(pc, dst, dbufs):
                bank = 4 + pc % 2
                P.op("pe", lambda e: e.matmul(ps(bank), lhsT=sel_bf[0], rhs=STA[:, pc * 512:(pc + 1) * 512], start=True, stop=False),
                     reads=[b_CB, bsa[0]], writes=[psb[bank]])
                P.op("pe", lambda e: e.matmul(ps(bank), lhsT=sel_bf[1], rhs=STB[:, pc * 512:(pc + 1) * 512], start=False, stop=True),
                     reads=[b_CB, bsb[0]], writes=[psb[bank]])
                dp = dst[:, pc * 512:(pc + 1) * 512]
                if pc % 2 == 0:
                    P.op("act", lambda e: e.copy(out=dp, in_=ps(bank)), reads=[psb[bank]], writes=[dbufs[0]])
                else:
                    P.op("dve", lambda e: e.tensor_copy(out=dp, in_=ps(bank)), reads=[psb[bank]], writes=[dbufs[0]])

            def fm_steps(ra0, rb0, dst, dbufs):
                out = []
                for rp in range(2):
                    ra, rb = ra0 + rp * 512, rb0 + rp * 512

                    def ld(rp=rp, ra=ra, rb=rb):
                        P.dma("sp", lambda e: e.dma_start(out=STA[:, rp * T:(rp + 1) * T], in_=XQG[ra:ra + 128, :]),
                              bsa[0], reads=[b_XQGc[ra // 1024]], extra_writes=bsa[1:])
                        P.dma("sp", lambda e: e.dma_start(out=STB[:, rp * T:(rp + 1) * T], in_=XQG[rb:rb + 128, :]),
                              bsb[0], reads=[b_XQGc[rb // 1024]], extra_writes=bsb[1:])
                    out.append(ld)
                return out + sel_steps(dst, dbufs)

            def head_steps(lh, SETX):
                q_, k_, vv_, bq_, bk_, bv_, v_ = SETX
                steps = fm_steps(lh * 128, 1024 + lh * 128, q_, bq_)
                steps += fm_steps(2048 + lh * 128, 3072 + lh * 128, k_, bk_)
                for rp in range(2):
                    for c in range(2):
                        blk0 = rp * nhb + c * (HV // 128)

                        def ldv(rp=rp, c=c, blk0=blk0):
                            P.dma("sp", lambda e: e.dma_start(
                                out=STAv[:, blk0:blk0 + HV // 128, :], in_=xvs3[c * 2 + rp, :, :, lh * 128:(lh + 1) * 128]),
                                bsa[0], reads=[b_XVGs], extra_writes=bsa[1:])
                            P.dma("sp", lambda e: e.dma_start(
                                out=STBv[:, blk0:blk0 + HV // 128, :], in_=xvs3[c * 2 + rp, :, :, (4 + lh) * 128:(5 + lh) * 128]),
                                bsb[0], reads=[b_XVGs], extra_writes=bsb[1:])
                        steps.append(ldv)
                steps += sel_steps(v_, bv_)
                return steps

            for stp in head_steps(0, sets[0]):
                stp()
            for lh in range(4):
                nxt = head_steps(lh + 1, sets[(lh + 1) % 2]) if lh + 1 < 4 else []
                per = (len(nxt) + 7) // 8
                for tq in range(S // 512):
                    sb_tile(lh, tq, sets[lh % 2][:6])
                    for stp in nxt[tq * per:(tq + 1) * per]:
                        stp()
            for c in range(4):
                o_coll(c)
            P.barrier()
            QN, KN, QP, KP, VD = (AT[:, k * S:(k + 1) * S] for k in range(5))
            nb_ = S // TT
            bqn, bkn, bqp, bkp, bvd = (b_ATc[k * nb_:(k + 1) * nb_] for k in range(5))
            ACC = XN[:, 0:16384].bitcast(F32)
            ACCN, ACCD = ACC[:, 0:S], ACC[:, S:2 * S]
            bacc = b_XNc[0]
            for lh2 in range(2):
                P.op("dve", lambda e: e.memset(ACC[:, :], 0.0), writes=[bacc])
                for g in range(NG):
                    r = DILS[g]
                    UB = S // (128 * r)
                    UBh = UB // 2
                    load_fm(lambda rp: (4 + g) * 1024 + rp * 512 + lh2 * 128, lambda rp: (4 + g) * 1024 + rp * 512 + (2 + lh2) * 128, QN, bqn, (0, 1))
                    load_fm(lambda rp: (7 + g) * 1024 + rp * 512 + lh2 * 128, lambda rp: (7 + g) * 1024 + rp * 512 + (2 + lh2) * 128, KN, bkn, (0, 1))
                    vsrc = XVDG[g].ap().rearrange("(q rho b p) c -> q p rho b c", p=128, rho=r, b=UBh)
                    sta_v = STA.rearrange("p (rho ub d) -> p rho ub d", rho=r, ub=UB)
                    stb_v = STB.rearrange("p (rho ub d) -> p rho ub d", rho=r, ub=UB)
                    for rp in range(2):
                        for rho in range(r):
                            P.dma("sp", lambda e, rp=rp, lh2=lh2, vsrc=vsrc, sta_v=sta_v, rho=rho, UBh=UBh: e.dma_start(
                                out=sta_v[:, rho, rp * UBh:(rp + 1) * UBh, :], in_=vsrc[rp, :, rho, :, lh2 * 128:(lh2 + 1) * 128]),
                                bsa[0], reads=[b_XVGd[g]], extra_writes=bsa[1:])
                            P.dma("sp", lambda e, rp=rp, lh2=lh2, vsrc=vsrc, stb_v=stb_v, rho=rho, UBh=UBh: e.dma_start(
                                out=stb_v[:, rho, rp * UBh:(rp + 1) * UBh, :], in_=vsrc[rp, :, rho, :, (2 + lh2) * 128:(3 + lh2) * 128]),
                                bsb[0], reads=[b_XVGd[g]], extra_writes=bsb[1:])
                    select_pieces(S // 512, lambda pc: STA[:, pc * 512:(pc + 1) * 512], lambda pc: STB[:, pc * 512:(pc + 1) * 512],
                                  lambda pc: [bsa[0]], lambda pc: [bsb[0]], lambda pc: VD[:, pc * 512:(pc + 1) * 512],
                                  lambda pc: [bvd[0]], (0, 1))
                    if r == 1:
                        Qs, Ks, bqs, bks = QN, KN, bqn, bkn
                    else:
                        P.op("act", lambda e, r=r: e.copy(out=QP.rearrange("p (r u) -> p r u", r=r), in_=QN.rearrange("p (u r) -> p r u", r=r)),
                             reads=[bqn[0]], writes=bqp)
                        P.op("dve", lambda e, r=r: e.tensor_copy(out=KP.rearrange("p (r u) -> p r u", r=r), in_=KN.rearrange("p (u r) -> p r u", r=r)),
                             reads=[bkn[0]], writes=bkp)
                        Qs, Ks, bqs, bks = QP, KP, bqp, bkp
                    VDv = VD.rearrange("p (b d) -> p b d", d=128)
                    bias_o = CF_DIL + ((g * 2 + lh2) * 2) * 128
                    accn_v = ACCN.rearrange("p (u r) -> p r u", r=r)
                    accd_v = ACCD.rearrange("p (u r) -> p r u", r=r)
                    qbs = [(rho, ub) for rho in range(r) for ub in range(UB)]

                    def d_stage1(qi, Ks=Ks, Qs=Qs, bks=bks, bqs=bqs, UB=UB, bias_o=bias_o):
                        rho, ub = qbs[qi]
                        blk = rho * UB + ub
                        kbs = [(blk, 0)] + ([(blk - 1, 1)] if ub > 0 else [])
                        for ki, (kb, kind) in enumerate(kbs):
                            fk = (2 * qi + ki) % 4
                            sps = ps(fk, 128)
                            P.op("pe", lambda e, kb=kb, blk=blk, sps=sps: e.matmul(
                                sps, lhsT=Ks[:, kb * 128:(kb + 1) * 128], rhs=Qs[:, blk * 128:(blk + 1) * 128], start=True, stop=True),
                                reads=[bks[0], bqs[0]], writes=[psb[fk]])
                            bias = CF[:, bias_o + kind * 128: bias_o + (kind + 1) * 128]
                            P.op("dve", lambda e, sps=sps, bias=bias, fk=fk: e.scalar_tensor_tensor(
                                out=fw(fk, 128), in0=sps, scalar=SCALE, in1=bias, op0=ALU.mult, op1=ALU.add),
                                reads=[psb[fk], b_CF], writes=[fwb[fk]])
                            P.op("act", lambda e, fk=fk: e.activation(out=st(fk, 128), in_=fw(fk, 128), func=AF.Exp),
                                 reads=[fwb[fk]], writes=[b_ST[fk]])

                    def d_stage2(qi, UB=UB, accn_v=accn_v, accd_v=accd_v):
                        rho, ub = qbs[qi]
                        blk = rho * UB + ub
                        kbs = [(blk, 0)] + ([(blk - 1, 1)] if ub > 0 else [])
                        nk = len(kbs)
                        bn, bd_ = 4 + qi % 2, 6 + qi % 2
                        for ki, (kb, kind) in enumerate(kbs):
                            fk = (2 * qi + ki) % 4
                            P.op("pe", lambda e, kb=kb, fk=fk, ki=ki: e.matmul(
                                ps(bn, 128), lhsT=VDv[:, kb, :], rhs=st(fk, 128), start=(ki == 0), stop=(ki == nk - 1)),
                                reads=[bvd[0], b_ST[fk]], writes=[psb[bn]])
                            P.op("pe", lambda e, fk=fk, ki=ki: e.matmul(
                                ps(bd_, 128), lhsT=ones_bf, rhs=st(fk, 128), start=(ki == 0), stop=(ki == nk - 1)),
                                reads=[b_CB, b_ST[fk]], writes=[psb[bd_]])
                        an = accn_v[:, rho, ub * 128:(ub + 1) * 128]
                        ad = accd_v[:, rho, ub * 128:(ub + 1) * 128]
                        P.op("dve", lambda e: e.tensor_tensor(out=an, in0=an, in1=ps(bn, 128), op=ALU.add),
                             reads=[psb[bn], bacc], writes=[bacc])
                        P.op("dve", lambda e: e.tensor_tensor(out=ad, in0=ad, in1=ps(bd_, 128), op=ALU.add),
                             reads=[psb[bd_], bacc], writes=[bacc])

                    d_stage1(0)
                    for qi in range(len(qbs)):
                        if qi + 1 < len(qbs):
                            d_stage1(qi + 1)
                        d_stage2(qi)
                for pc in range(S // 512):
                    k = 4 + pc % 2
                    ok = pc % 2
                    P.op("dve", lambda e, pc=pc, k=k: e.reciprocal(out=fw(k), in_=ACCD[:, pc * 512:(pc + 1) * 512]), reads=[bacc], writes=[fwb[k]])
                    P.op("dve", lambda e, pc=pc, k=k, ok=ok: e.tensor_tensor(out=ost(ok), in0=ACCN[:, pc * 512:(pc + 1) * 512], in1=fw(k), op=ALU.mult),
                         reads=[bacc, fwb[k]], writes=[b_OST[ok]])
                    P.dma("act", lambda e, pc=pc, ok=ok, lh2=lh2: e.dma_start(
                        out=OXL[512 + lh2 * 128:512 + (lh2 + 1) * 128, pc * 512:(pc + 1) * 512], in_=ost(ok)),
                        b_OXL, reads=[b_OST[ok]])

        pump_layer(0)
        for L in range(DEPTH):
            for i in range(NT):
                if L == 0:
                    norm_pass(L, i, xT, b_X, False, 0, True, 0)
                else:
                    norm_pass(L, i, HT, b_HT[i], False, 0, True, 0)

                def ep_qk(c, bank, i=i):
                    k = next_st()
                    if c % 2 == 0:
                        P.op("act", lambda e: e.copy(out=st(k), in_=ps(bank)), reads=[psb[bank]], writes=[b_ST[k]])
                    else:
                        P.op("dve", lambda e: e.tensor_copy(out=st(k), in_=ps(bank)), reads=[psb[bank]], writes=[b_ST[k]])
                    P.dma("act", lambda e: e.dma_start(out=XQL[c * 128:(c + 1) * 128, i * TT:(i + 1) * TT], in_=st(k)),
                          b_XQL, reads=[b_ST[k]])
                dense_fm(L, ["in_qk"], xn, lambda kc: b_XNc[kc], ep_qk)

                def ep_gate(c, bank, i=i):
                    k = next_st()
                    P.op("act", lambda e: e.activation(out=st(k), in_=ps(bank), func=AF.Sigmoid), reads=[psb[bank]], writes=[b_ST[k]])
                    P.dma("act", lambda e: e.dma_start(out=GATE[c * 128:(c + 1) * 128, i * TT:(i + 1) * TT], in_=st(k)),
                          b_GATE[i], reads=[b_ST[k]])
                dense_fm(L, ["in_gate"], xn, lambda kc: b_XNc[kc], ep_gate)

                sg = cfg.seg["in_v"]
                VW = W_SB + W_DIL
                VST = AT[:, 0:NSUB * VW].rearrange("p (s c) -> p s c", s=NSUB)
                b_VST = b_ATc[0:(NSUB * VW + TT - 1) // TT]
                for b in range(sg["nblk"]):
                    wt, wbuf = slots_b.load(L, "in_v", b)
                    r = 1 if b < 4 else DILS[(b - 4) // 2]
                    for sbk in range(NSUB):
                        bank = next_bank()
                        if r == 1:
                            cols = lambda kc, sbk=sbk: XN[:, kc * TT + sbk * 128: kc * TT + (sbk + 1) * 128]
                        elif r == 4:
                            cols = lambda kc, sbk=sbk: xn(kc).rearrange("p (u r) -> p r u", r=4)[:, sbk, :]
                        else:
                            cols = lambda kc, sbk=sbk: xn(kc).rearrange("p (u r) -> p r u", r=4)[:, sbk, :]
                        for kc in range(KC):
                            P.op("pe", lambda e, kc=kc, cols=cols, bank=bank, wt=wt: e.matmul(
                                ps(bank, 256), lhsT=cols(kc), rhs=wt[:, kc * 256:(kc + 1) * 256], start=(kc == 0), stop=(kc == KC - 1)),
                                reads=[wbuf, b_XNc[kc]], writes=[psb[bank]])
                        dstv = VST[:, sbk, b * 256:(b + 1) * 256]
                        if (b + sbk) % 2 == 0:
                            P.op("act", lambda e, dstv=dstv, bank=bank: e.copy(out=dstv, in_=ps(bank, 256)), reads=[psb[bank]], writes=[b_VST[0]])
                        else:
                            P.op("dve", lambda e, dstv=dstv, bank=bank: e.tensor_copy(out=dstv, in_=ps(bank, 256)), reads=[psb[bank]], writes=[b_VST[0]])
                for sbk in range(NSUB):
                    r0 = i * TT + sbk * 128
                    P.dma("act", lambda e, sbk=sbk, r0=r0: e.dma_start(out=XVS[r0:r0 + 128, :], in_=VST[:, sbk, 0:W_SB]), b_XV, reads=[b_VST[0]])
                    P.dma("act", lambda e, sbk=sbk, r0=r0: e.dma_start(out=XVD[0][r0:r0 + 128, :], in_=VST[:, sbk, W_SB:W_SB + 512]), b_XV, reads=[b_VST[0]])
                    rr = sbk * (T // 4) + i * (TT // 4)
                    P.dma("act", lambda e, sbk=sbk, rr=rr: e.dma_start(out=XVD[1][rr:rr + 128, :], in_=VST[:, sbk, W_SB + 512:W_SB + 1024]), b_XV, reads=[b_VST[0]])
                    for m in range(4):
                        rr2 = (4 * m + sbk) * (T // 16) + i * (TT // 16)
                        P.dma("act", lambda e, sbk=sbk, m=m, rr2=rr2: e.dma_start(
                            out=XVD[2][rr2:rr2 + TT // 16, :], in_=AT[m:128:4, sbk * VW + W_SB + 1024: sbk * VW + W_SB + 1536]),
                            b_XV, reads=[b_VST[0]])
                for bb in b_VST[1:]:
                    bb.w = b_VST[0].w
                    bb.rs = dict(b_VST[0].rs)

            def xq_coll(c):
                P.coll(lambda e, c=c: e.collective_compute("AllGather", ALU.bypass, replica_groups=PAIRS,
                                                           ins=[XQL[c * 512:(c + 1) * 512, :]], outs=[XQG[c * 1024:(c + 1) * 1024, :]]),
                       reads=[b_XQL], writes=[b_XQGc[c]])
            for c in range(4):
                xq_coll(c)
            for c in range(2):
                P.coll(lambda e, c=c: e.collective_compute("AllGather", ALU.bypass, replica_groups=PAIRS,
                                                           ins=[XVS[c * HV:(c + 1) * HV, :]], outs=[XVSG[c * 2 * HV:(c + 1) * 2 * HV, :]]),
                       reads=[b_XV], writes=[b_XVGs])
            for c in range(4, NQC):
                xq_coll(c)
            for g in range(NG):
                P.coll(lambda e, g=g: e.collective_compute("AllGather", ALU.bypass, replica_groups=PAIRS,
                                                           ins=[XVD[g][:, :]], outs=[XVDG[g][:, :]]),
                       reads=[b_XV], writes=[b_XVGd[g]])
            if L + 1 < DEPTH:
                pump_weights(ncoll_layer // 3)
            P.barrier()

            attention()
            P.barrier()
            for c in range(4, 6):
                o_coll(c)
            if L + 1 < DEPTH:
                pump_layer(L + 1)

            for i in range(NT):
                for kc in range(12):
                    rblk = (kc % 4) if kc < 8 else (4 + (kc - 8) % 2)
                    rp = (kc // 4) if kc < 8 else ((kc - 8) // 2)
                    row0 = (rblk * 2 + rp) * 128
                    P.dma("sp", lambda e, kc=kc, row0=row0, i=i: e.dma_start(out=at(KC + kc), in_=OXG[row0:row0 + 128, i * TT:(i + 1) * TT]),
                          b_ATc[KC + kc], reads=[b_OXG])
                    P.dma("sp", lambda e, kc=kc, row0=row0, i=i: e.dma_start(out=at(KC + 12 + kc), in_=OXG[row0:row0 + 128, T + i * TT:T + (i + 1) * TT]),
                          b_ATc[KC + 12 + kc], reads=[b_OXG])
                select_pieces(12, lambda pc: at(KC + pc), lambda pc: at(KC + 12 + pc), lambda pc: [b_ATc[KC + pc]], lambda pc: [b_ATc[KC + 12 + pc]],
                              lambda pc: xn(pc), lambda pc: [b_XNc[pc]], (0, 1))
                gv = GATE.ap().rearrange("(c p) t -> p c t", p=128)
                for b in range(cfg.seg["p_sb"]["nblk"]):
                    wsb_t, wsb_b = slots_s.load(L, "p_sb", b)
                    wd_t, wd_b = slots_s.load(L, "p_dil", b)
                    for sub in range(2):
                        oc = b * 2 + sub
                        kg = oc % 2
                        P.dma("sp", lambda e, oc=oc, kg=kg, i=i: e.dma_start(out=st(kg), in_=gv[:, oc, i * TT:(i + 1) * TT]),
                              b_ST[kg], reads=[b_GATE[i]])
                        P.dma("sp", lambda e, oc=oc, kg=kg, i=i: e.dma_start(out=st(2 + kg), in_=gv[:, KC + oc, i * TT:(i + 1) * TT]),
                              b_ST[2 + kg], reads=[b_GATE[i]])
                        for kc in range(8):
                            lhsT = wsb_t[:, kc * 256 + sub * 128: kc * 256 + (sub + 1) * 128]
                            P.op("pe", lambda e, lhsT=lhsT, kc=kc: e.matmul(ps(2), lhsT=lhsT, rhs=xn(kc), start=(kc == 0), stop=(kc == 7)),
                                 reads=[wsb_b, b_XNc[kc]], writes=[psb[2]])
                        for kc in range(4):
                            lhsT = wd_t[:, kc * 256 + sub * 128: kc * 256 + (sub + 1) * 128]
                            P.op("pe", lambda e, lhsT=lhsT, kc=kc: e.matmul(ps(3), lhsT=lhsT, rhs=xn(8 + kc), start=(kc == 0), stop=(kc == 3)),
                                 reads=[wd_b, b_XNc[8 + kc]], writes=[psb[3]])
                        P.op("dve", lambda e, kg=kg: e.tensor_tensor(out=fw(kg), in0=ps(2), in1=st(kg), op=ALU.mult),
                             reads=[psb[2], b_ST[kg]], writes=[fwb[kg]])
                        P.op("dve", lambda e, kg=kg: e.tensor_tensor(out=fw(2 + kg), in0=ps(3), in1=st(2 + kg), op=ALU.mult),
                             reads=[psb[3], b_ST[2 + kg]], writes=[fwb[2 + kg]])
                        P.op("dve", lambda e, kg=kg, oc=oc: e.tensor_tensor(out=at(oc), in0=fw(kg), in1=fw(2 + kg), op=ALU.add),
                             reads=[fwb[kg], fwb[2 + kg]], writes=[b_ATc[oc]])

                def ep_y(c, bank, i=i, total=KC):
                    k = 4 + c % 2
                    P.op("act", lambda e: e.copy(out=fw(k), in_=ps(bank)), reads=[psb[bank]], writes=[fwb[k]])
                    sumsq_accum(ps(bank), [psb[bank]], c % 2, c == 0, c == total - 1)
                    P.dma("act", lambda e: e.dma_start(out=Y[c * 128:(c + 1) * 128, i * TT:(i + 1) * TT], in_=fw(k)),
                          b_Y[i], reads=[fwb[k]])
                dense_fm(L, ["w_out"], at, lambda kc: b_ATc[kc], ep_y)
                rstd_from_sumsq(0)
                if L == 0:
                    norm_pass(L, i, xT, b_X, True, 1, True, 2)
                else:
                    norm_pass(L, i, HT, b_HT[i], True, 1, True, 2)

                if getattr(cfg, "debug", False) and i == 0:
                    P.dma("sp", lambda e, i=i: e.dma_start(out=dbgH1[:, i * TT:(i + 1) * TT], in_=HT[:, i * TT:(i + 1) * TT]), b_dbgT, reads=[b_HT[i]])
                    P.dma("sp", lambda e, i=i: e.dma_start(out=dbgY1[:, i * TT:(i + 1) * TT], in_=Y[:, i * TT:(i + 1) * TT]), b_dbgT, reads=[b_Y[i]])
                gu_state = {}

                def ep_gu(c, bank, i=i):
                    blk, which = c // 2, c % 2
                    if which == 0:
                        k = 6 + blk % 2
                        P.op("act", lambda e: e.activation(out=fw(k), in_=ps(bank), func=AF.Silu), reads=[psb[bank]], writes=[fwb[k]])
                        gu_state["k"] = k
                    else:
                        k = gu_state["k"]
                        P.op("dve", lambda e: e.tensor_tensor(out=at(blk), in0=fw(k), in1=ps(bank), op=ALU.mult),
                             reads=[fwb[k], psb[bank]], writes=[b_ATc[blk]])
                dense_fm(L, ["ffn_gu"], xn, lambda kc: b_XNc[kc], ep_gu)
                dense_fm(L, ["ffn_d0", "ffn_d1"], at, lambda kc: b_ATc[kc], ep_y)
                rstd_from_sumsq(0)
                norm_pass(L, i, HT, b_HT[i], True, 3, True, 4)

                if getattr(cfg, "debug", False) and i == 0:
                    P.dma("sp", lambda e, i=i: e.dma_start(out=dbgH2[:, i * TT:(i + 1) * TT], in_=HT[:, i * TT:(i + 1) * TT]), b_dbgT, reads=[b_HT[i]])
                    P.dma("sp", lambda e, i=i: e.dma_start(out=dbgY2[:, i * TT:(i + 1) * TT], in_=Y[:, i * TT:(i + 1) * TT]), b_dbgT, reads=[b_Y[i]])
                def ep_d1(c, bank):
                    P.op("act", lambda e: e.copy(out=at(c), in_=ps(bank)), reads=[psb[bank]], writes=[b_ATc[c]])
                dense_fm(L, ["ple_gd"], xn, lambda kc: b_XNc[kc], ep_d1)
                ptv = PTB.ap().rearrange("(l c p) t -> l p c t", c=2, p=128)
                for c2 in range(2):
                    P.dma("sp", lambda e, i=i, L=L, c2=c2: e.dma_start(out=at(2 + c2), in_=ptv[L, :, c2, i * TT:(i + 1) * TT]),
                          b_ATc[2 + c2], reads=[b_PTB])
                for b in range(cfg.seg["ple_gu"]["nblk"]):
                    wgu_t, wgu_b = slots_s.load(L, "ple_gu", b)
                    win_t, win_b = slots_s.load(L, "ple_in", b)
                    for sub in range(4):
                        oc = b * 4 + sub
                        for kc in range(2):
                            P.op("pe", lambda e, kc=kc, sub=sub, wgu_t=wgu_t: e.matmul(
                                ps(2), lhsT=wgu_t[:, kc * 512 + sub * 128: kc * 512 + (sub + 1) * 128], rhs=at(kc),
                                start=(kc == 0), stop=(kc == 1)), reads=[wgu_b, b_ATc[kc]], writes=[psb[2]])
                        for kc in range(2):
                            P.op("pe", lambda e, kc=kc, sub=sub, win_t=win_t: e.matmul(
                                ps(3), lhsT=win_t[:, kc * 512 + sub * 128: kc * 512 + (sub + 1) * 128], rhs=at(2 + kc),
                                start=(kc == 0), stop=(kc == 1)), reads=[win_b, b_ATc[2 + kc]], writes=[psb[3]])
                        k = oc % 2
                        k2 = 4 + oc % 2
                        P.op("act", lambda e, k=k: e.activation(out=fw(k), in_=ps(2), func=AF.Sigmoid), reads=[psb[2]], writes=[fwb[k]])
                        P.op("dve", lambda e, k=k, k2=k2: e.tensor_tensor(out=fw(k2), in0=fw(k), in1=ps(3), op=ALU.mult),
                             reads=[fwb[k], psb[3]], writes=[fwb[k2]])
                        sumsq_accum(fw(k2), [fwb[k2]], oc % 2, oc == 0, oc == KC - 1)
                        P.dma("act", lambda e, oc=oc, k2=k2, i=i: e.dma_start(out=Y[oc * 128:(oc + 1) * 128, i * TT:(i + 1) * TT], in_=fw(k2)),
                              b_Y[i], reads=[fwb[k2]])
                rstd_from_sumsq(0)
                last = (L == DEPTH - 1)
                norm_pass(L, i, HT, b_HT[i], True, 5, False, 0, h_dst=(outT if last else None), b_dst=(b_OUT if last else None))
            P.barrier()

        if getattr(cfg, "debug", False):
            b_dbg = P.buf("dbg", dma=True)
            for name, t in (("XQG", XQG), ("XVSG", XVSG), ("OXG", OXG), ("XQL", XQL), ("XVS", XVS), ("XVD0", XVD[0]), ("XVD1", XVD[1]), ("XVD2", XVD[2]), ("GATE", GATE), ("OXL", OXL), ("Ydbg", Y)):
                shp = list(t.shape)
                dt_ = BF16 if name != "Ydbg" else F32
                o = nc.dram_tensor("dbg_" + name, shp, dt_, kind="ExternalOutput")
                step = max(1, 1024 * 1024 // (shp[1] * 2))
                for r0 in range(0, shp[0], step):
                    r1 = min(shp[0], r0 + step)
                    P.dma("sp", lambda e, o=o, t=t, r0=r0, r1=r1: e.dma_start(out=o[r0:r1, :], in_=t[r0:r1, :]), b_dbg)
        P.barrier(engines=("pe", "act", "dve", "sp", "pool"))

        with nc.Block() as block:
            @block.sync
            def _(e):
                P.replay("sp", e)

            @block.tensor
            def _(e):
                P.replay("pe", e)

            @block.vector
            def _(e):
                P.replay("dve", e)

            @block.scalar
            def _(e):
                P.replay("act", e)

            @block.gpsimd
            def _(e):
                P.replay("pool", e)
    return nc


def run(cfg, x, p, w_in, w_proj_sb, w_proj_dil, w_out, g_mix_pre, g_mix_post, w_ffn_gate, w_ffn_up, w_ffn_down,
        g_ffn_pre, g_ffn_post, w_ple_in, w_ple_gate_down, w_ple_gate_up, g_ple_gate, g_ple_post):
    D, S, T, DEPTH, KC = cfg.D, cfg.S, cfg.T, cfg.DEPTH, cfg.KC
    B = x.shape[0]
    assert B * 2 == 8
    f = lambda a: np.asarray(a, dtype=np.float32)
    x, p = f(x), f(p)
    nc = build(cfg)
    shards = [[None] * DEPTH for _ in range(8)]
    for L in range(DEPTH):
        flats = pack_layer(cfg, f(w_in[L]), f(w_proj_sb[L]), f(w_proj_dil[L]), f(w_out[L]), f(w_ffn_gate[L]), f(w_ffn_up[L]),
                           f(w_ffn_down[L]), f(w_ple_in[L]), f(w_ple_gate_down[L]), f(w_ple_gate_up[L]))
        for c in range(8):
            shards[c][L] = []
        for g, flat in enumerate(flats):
            v = flat.reshape(cfg.gncoll[g], 8, CH)
            for c in range(8):
                shards[c][L].append(np.ascontiguousarray(v[:, c, :]).reshape(cfg.gncoll[g] * CH // FLATW, FLATW))
        del flats, v
    gl = [g_mix_pre, g_mix_post, g_ffn_pre, g_ffn_post, g_ple_gate, g_ple_post]
    gains = np.stack([f(g) for g in gl], 0)
    gains = np.ascontiguousarray(gains.reshape(6, DEPTH, KC, 128).transpose(3, 0, 1, 2)).reshape(128, 6 * DEPTH * KC)
    in_maps = []
    for c in range(8):
        b, s = c // 2, c % 2
        m = {
            "xT": np.ascontiguousarray(x[b, s * T:(s + 1) * T, :].T),
            "pT": np.ascontiguousarray(p[:, b, s * T:(s + 1) * T, :].transpose(0, 2, 1)).reshape(DEPTH * D_PLE, T),
            "gains": gains,
            "consts": make_consts(s),
        }
        for L in range(DEPTH):
            for g in range(cfg.NGRP):
                m[f"wsh{L}_{g}"] = shards[c][L][g]
        in_maps.append(m)
    res = run_bass_kernel_spmd(nc, in_maps, core_ids=list(range(8)))
    if getattr(cfg, "debug", False):
        cfg.dbg = res.results
    out = np.empty((B, S, D), np.float32)
    for c in range(8):
        b, s = c // 2, c % 2
        out[b, s * T:(s + 1) * T, :] = res.results[c]["outT"].T
    return out


def kernel(**inputs):
    cfg = Cfg()
    return run(cfg, **inputs)
```
